# Optimizing a Trainium2 kernel written in Bass

```python
import math
import jax, jax.numpy as jnp
from jax import lax
import numpy as np

D_MODEL = 1024
BATCH = 16
SEQ = 4096
DEPTH = 2
DEC_BATCH = 4
DEC_SEQ = 8192
PAST_LEN = 128

N_META = 16
GRID_W = 64
WIN_ROWS = 8
WIN_COLS = 16
NA_HEADS = 16
NA_HEAD_DIM = D_MODEL // NA_HEADS
DIFF_HEADS = 8
DIFF_HEAD_DIM = D_MODEL // (2 * DIFF_HEADS)
ROPE_THETA = 500000.0
ROPE_DIM = DIFF_HEAD_DIM // 4
D_FF = ((8 * D_MODEL // 3 + 127) // 128) * 128
Q_BLOCK = 128
RMS_EPS = 1e-6
N_MIXERS = 2
N_LAYERS_A = (DEPTH + 1) // 2
N_LAYERS_B = DEPTH // 2

kernel_name = "hybrid_natten_diffattn_macaron_encoder"


def rms_norm(x, g):
    xf = x.astype(jnp.float32)
    y = xf * lax.rsqrt(jnp.mean(xf * xf, axis=-1, keepdims=True) + RMS_EPS)
    return (y * g.astype(jnp.float32)).astype(x.dtype)


def swiglu(x, w_in, w_out):
    gate, up = jnp.split(x @ w_in, 2, axis=-1)
    return (jax.nn.silu(gate) * up) @ w_out


def neighbourhood_attention(x, w_qkv, w_o, rpb, meta_bias):
    B, L, _ = x.shape
    T = L - N_META
    rows = T // GRID_W
    wr = min(WIN_ROWS, rows)
    qkv = (x @ w_qkv).reshape(B, L, 3, NA_HEADS, NA_HEAD_DIM)
    q = qkv[:, :, 0] * (NA_HEAD_DIM ** -0.5)
    k = qkv[:, :, 1]
    v = qkv[:, :, 2]
    qm, km, vm = q[:, :N_META], k[:, :N_META], v[:, :N_META]
    kg = k[:, N_META:].reshape(B, rows, GRID_W, NA_HEADS, NA_HEAD_DIM)
    vg = v[:, N_META:].reshape(B, rows, GRID_W, NA_HEADS, NA_HEAD_DIM)
    qg = q[:, N_META:].reshape(B, rows, GRID_W, NA_HEADS, NA_HEAD_DIM).transpose(1, 0, 2, 3, 4)
    cols = jnp.arange(GRID_W)
    col_start = jnp.clip(cols - WIN_COLS // 2, 0, GRID_W - WIN_COLS)
    col_idx = col_start[:, None] + jnp.arange(WIN_COLS)[None, :]
    dc = col_idx - cols[:, None]
    mb = meta_bias.astype(jnp.float32)
    rpb_f = rpb.astype(jnp.float32)

    def row_block(args):
        r, qr = args
        r0 = jnp.clip(r - wr // 2, 0, rows - wr)
        kb = lax.dynamic_slice_in_dim(kg, r0, wr, axis=1)[:, :, col_idx]
        vb = lax.dynamic_slice_in_dim(vg, r0, wr, axis=1)[:, :, col_idx]
        dr = r0 + jnp.arange(wr) - r
        bias = rpb_f[:, dr[:, None, None] + WIN_ROWS - 1, dc[None] + WIN_COLS - 1]
        bias = bias.transpose(0, 2, 1, 3)
        s_loc = jnp.einsum('bqhd,bwqjhd->bhqwj', qr, kb).astype(jnp.float32) + bias[None]
        s_meta = jnp.einsum('bqhd,bmhd->bhqm', qr, km).astype(jnp.float32) + mb[None, :, None, :]
        s = jnp.concatenate([s_loc.reshape(B, NA_HEADS, GRID_W, wr * WIN_COLS), s_meta], axis=-1)
        p = jax.nn.softmax(s, axis=-1).astype(v.dtype)
        p_loc = p[..., :wr * WIN_COLS].reshape(B, NA_HEADS, GRID_W, wr, WIN_COLS)
        p_meta = p[..., wr * WIN_COLS:]
        return (jnp.einsum('bhqwj,bwqjhd->bqhd', p_loc, vb)
                + jnp.einsum('bhqm,bmhd->bqhd', p_meta, vm))

    og = lax.map(row_block, (jnp.arange(rows), qg))
    o_real = og.transpose(1, 0, 2, 3, 4).reshape(B, T, D_MODEL)
    s_mm = jnp.einsum('bqhd,bmhd->bhqm', qm, km).astype(jnp.float32) + mb[None, :, None, :]
    p_mm = jax.nn.softmax(s_mm, axis=-1).astype(v.dtype)
    o_meta = jnp.einsum('bhqm,bmhd->bqhd', p_mm, vm).reshape(B, N_META, D_MODEL)
    o = jnp.concatenate([o_meta, o_real], axis=1)
    return o @ w_o


def rope_cos_sin(L):
    inv = ROPE_THETA ** (-jnp.arange(0, ROPE_DIM, 2, dtype=jnp.float32) / ROPE_DIM)
    ang = jnp.arange(L, dtype=jnp.float32)[:, None] * inv[None, :]
    return jnp.cos(ang), jnp.sin(ang)


def apply_partial_rope(x, cos, sin):
    half = ROPE_DIM // 2
    x1 = x[..., :half].astype(jnp.float32)
    x2 = x[..., half:ROPE_DIM].astype(jnp.float32)
    c = cos[None, :, None, None, :]
    s = sin[None, :, None, None, :]
    rot = jnp.concatenate([x1 * c - x2 * s, x2 * c + x1 * s], axis=-1).astype(x.dtype)
    return jnp.concatenate([rot, x[..., ROPE_DIM:]], axis=-1)


def diff_attention(x, w_qkv, w_o, lam, subln, lambda_init):
    B, L, _ = x.shape
    qkv = x @ w_qkv
    q = qkv[..., :D_MODEL].reshape(B, L, DIFF_HEADS, 2, DIFF_HEAD_DIM)
    k = qkv[..., D_MODEL:2 * D_MODEL].reshape(B, L, DIFF_HEADS, 2, DIFF_HEAD_DIM)
    v = qkv[..., 2 * D_MODEL:].reshape(B, L, DIFF_HEADS, 2 * DIFF_HEAD_DIM)
    cos, sin = rope_cos_sin(L)
    q = apply_partial_rope(q, cos, sin) * (DIFF_HEAD_DIM ** -0.5)
    k = apply_partial_rope(k, cos, sin)
    lf = lam.astype(jnp.float32)
    lam_full = jnp.exp(jnp.sum(lf[0] * lf[1])) - jnp.exp(jnp.sum(lf[2] * lf[3])) + lambda_init

    def attend(qb):
        s = jnp.einsum('bqhmd,bkhmd->bhmqk', qb, k).astype(jnp.float32)
        p = jax.nn.softmax(s, axis=-1)
        a = (p[:, :, 0] - lam_full * p[:, :, 1]).astype(v.dtype)
        return jnp.einsum('bhqk,bkhe->bqhe', a, v)

    T = L - N_META
    o_meta = attend(q[:, :N_META])
    qr = q[:, N_META:].reshape(B, T // Q_BLOCK, Q_BLOCK, DIFF_HEADS, 2, DIFF_HEAD_DIM)
    o_real = lax.map(attend, qr.transpose(1, 0, 2, 3, 4, 5))
    o_real = o_real.transpose(1, 0, 2, 3, 4).reshape(B, T, DIFF_HEADS, 2 * DIFF_HEAD_DIM)
    o = jnp.concatenate([o_meta, o_real], axis=1)
    o = rms_norm(o, subln) * (1.0 - lambda_init)
    return o.reshape(B, L, D_MODEL) @ w_o


def run_trunk(x, meta_tokens, norm_g, w_ffn_in, w_ffn_out, w_qkv_a, w_o_a, rpb_a,
              meta_bias_a, w_qkv_b, w_o_b, lambda_b, subln_b):
    B = x.shape[0]
    meta = jnp.broadcast_to(meta_tokens.astype(x.dtype)[None], (B, N_META, D_MODEL))
    h = jnp.concatenate([meta, x], axis=1)
    for i in range(DEPTH):
        g = norm_g[i]
        h = h + 0.5 * rms_norm(swiglu(rms_norm(h, g[0]), w_ffn_in[i, 0], w_ffn_out[i, 0]), g[1])
        u = rms_norm(h, g[2])
        j = i // N_MIXERS
        if i % N_MIXERS == 0:
            m = neighbourhood_attention(u, w_qkv_a[j], w_o_a[j], rpb_a[j], meta_bias_a[j])
        else:
            lambda_init = 0.8 - 0.6 * math.exp(-0.3 * i)
            m = diff_attention(u, w_qkv_b[j], w_o_b[j], lambda_b[j], subln_b[j], lambda_init)
        h = h + rms_norm(m, g[3])
        h = h + 0.5 * rms_norm(swiglu(rms_norm(h, g[4]), w_ffn_in[i, 1], w_ffn_out[i, 1]), g[5])
    return h[:, N_META:]


def setup_inputs(seed: int = 0) -> dict:
    key = jax.random.key(seed)
    ks = jax.random.split(key, 16)
    f32 = jnp.float32
    nrm = lambda k, shape, s: jax.random.normal(k, shape, f32) * s
    return {
        "x_prompt": nrm(ks[0], (BATCH, SEQ, D_MODEL), 1.0),
        "x_sample": nrm(ks[1], (DEC_BATCH, DEC_SEQ, D_MODEL), 1.0),
        "meta_tokens": nrm(ks[2], (N_META, D_MODEL), 1.0),
        "norm_g": 1.0 + nrm(ks[3], (DEPTH, 6, D_MODEL), 0.1),
        "w_ffn_in": nrm(ks[4], (DEPTH, 2, D_MODEL, 2 * D_FF), D_MODEL ** -0.5),
        "w_ffn_out": nrm(ks[5], (DEPTH, 2, D_FF, D_MODEL), D_FF ** -0.5),
        "w_qkv_a": nrm(ks[6], (N_LAYERS_A, D_MODEL, 3 * D_MODEL), D_MODEL ** -0.5),
        "w_o_a": nrm(ks[7], (N_LAYERS_A, D_MODEL, D_MODEL), D_MODEL ** -0.5),
        "rpb_a": nrm(ks[8], (N_LAYERS_A, NA_HEADS, 2 * WIN_ROWS - 1, 2 * WIN_COLS - 1), 0.1),
        "meta_bias_a": nrm(ks[9], (N_LAYERS_A, NA_HEADS, N_META), 0.1),
        "w_qkv_b": nrm(ks[10], (N_LAYERS_B, D_MODEL, 3 * D_MODEL), D_MODEL ** -0.5),
        "w_o_b": nrm(ks[11], (N_LAYERS_B, D_MODEL, D_MODEL), D_MODEL ** -0.5),
        "lambda_b": nrm(ks[12], (N_LAYERS_B, 4, DIFF_HEAD_DIM), 0.1),
        "subln_b": 1.0 + nrm(ks[13], (N_LAYERS_B, 2 * DIFF_HEAD_DIM), 0.1),
    }


def reference(x_prompt, x_sample, meta_tokens, norm_g, w_ffn_in, w_ffn_out, w_qkv_a, w_o_a,
              rpb_a, meta_bias_a, w_qkv_b, w_o_b, lambda_b, subln_b):
    y_prompt = run_trunk(x_prompt, meta_tokens, norm_g, w_ffn_in, w_ffn_out, w_qkv_a, w_o_a,
                         rpb_a, meta_bias_a, w_qkv_b, w_o_b, lambda_b, subln_b)
    y_sample = run_trunk(x_sample, meta_tokens, norm_g, w_ffn_in, w_ffn_out, w_qkv_a, w_o_a,
                         rpb_a, meta_bias_a, w_qkv_b, w_o_b, lambda_b, subln_b)
    return (y_prompt, y_sample)
```

```python
import math
from contextlib import ExitStack
import numpy as np
import concourse.bass as bass
import concourse.mybir as mybir
from concourse.bass_utils import run_bass_kernel_spmd

F32 = mybir.dt.float32
BF16 = mybir.dt.bfloat16
AF = mybir.ActivationFunctionType
ALU = mybir.AluOpType
ENGS = ['pe', 'act', 'dve', 'pool', 'sp']
DMA_SLOTS = 8
D = 1024
DFF = 2816
NJ = 22
NEG = -30000.0
EPS = 1e-6


class Op:
    __slots__ = ('eng', 'fn', 'deps', 'needed', 'dma', 'val', 'slot', 'prev')

    def __init__(s, eng, fn, dma):
        s.eng = eng; s.fn = fn; s.dma = dma; s.deps = set(); s.needed = False
        s.val = 0; s.slot = None; s.prev = 0


class Prog:
    def __init__(s, nc, es):
        s.nc = nc
        s.streams = {e: [] for e in ENGS}
        s.cells = {}
        s.ndma = {e: 0 for e in ENGS}
        s.count = {e: 0 for e in ENGS}
        s.seen = {e: {} for e in ENGS}
        s.csem = {e: es.enter_context(nc.semaphore('c_' + e)) for e in ENGS}
        s.dsem = {e: [es.enter_context(nc.semaphore('d_%s%d' % (e, i))) for i in range(DMA_SLOTS)]
                  for e in ('sp', 'pool', 'act')}
        s.nblk = 0

    def op(s, eng, fn, reads=(), writes=(), dma=False):
        o = Op(eng, fn, dma)
        deps = set()
        for c in reads:
            st = s.cells.get(c)
            if st is not None and st[0] is not None:
                deps.add(st[0])
        for c in writes:
            st = s.cells.get(c)
            if st is not None:
                if st[0] is not None:
                    deps.add(st[0])
                deps.update(st[1])
        for d in deps:
            if d.eng == 'pe' and eng == 'pe' and not d.dma and not dma:
                continue
            o.deps.add(d); d.needed = True
        for c in reads:
            st = s.cells.setdefault(c, [None, []])
            st[1].append(o)
        for c in writes:
            s.cells[c] = [o, []]
        if dma:
            n = s.ndma[eng]; s.ndma[eng] = n + 1
            o.slot = n % DMA_SLOTS; o.val = 16 * (n // DMA_SLOTS + 1); o.prev = 16 * (n // DMA_SLOTS)
        s.streams[eng].append(o)
        return o

    def flush(s):
        nc = s.nc
        handles = {'pe': 'tensor', 'act': 'scalar', 'dve': 'vector', 'pool': 'gpsimd', 'sp': 'sync'}
        for e in ENGS:
            c = s.count[e]
            for o in s.streams[e]:
                if not o.dma and o.needed:
                    c += 1; o.val = c
            s.count[e] = c
        s.nblk += 1
        with nc.Block() as block:
            def mk(e):
                def body(h):
                    seen = s.seen[e]

                    def wait(key, sem, val):
                        if seen.get(key, 0) < val:
                            h.wait_ge(sem, val); seen[key] = val
                    for o in s.streams[e]:
                        for d in o.deps:
                            if d.dma:
                                wait(('d', d.eng, d.slot), s.dsem[d.eng][d.slot], d.val)
                            else:
                                wait(('c', d.eng), s.csem[d.eng], d.val)
                        if o.dma and o.prev > 0:
                            wait(('d', e, o.slot), s.dsem[e][o.slot], o.prev)
                        ins = o.fn(h)
                        if o.dma:
                            ins.then_inc(s.dsem[e][o.slot], 16)
                        elif o.needed:
                            ins.then_inc(s.csem[e], 1)
                    n = s.ndma[e]
                    if n > 0:
                        for sl in range(min(n, DMA_SLOTS)):
                            last = ((n - 1 - sl) // DMA_SLOTS) + 1
                            wait(('d', e, sl), s.dsem[e][sl], 16 * last)
                return body
            for e in ENGS:
                getattr(block, handles[e])(mk(e))
        s.streams = {e: [] for e in ENGS}
        s.cells = {}


def na_classes(T):
    cls = [('INT', [-2, -1, 0, 1, 2]), ('TOP0', [0, 1, 2, 3]), ('TOP1', [-1, 0, 1, 2]),
           ('BOT0', [-2, -1, 0, 1]), ('BOT1', [-3, -2, -1, 0]),
           ('SPA0', [-2, -1, 0, 1, 2]), ('SPA1', [-3, -2, -1, 0, 1, 2]),
           ('SPB0', [-2, -1, 0, 1, 2, 3]), ('SPB1', [-2, -1, 0, 1, 2])]
    start = {}
    n = 0
    for name, dl in cls:
        start[name] = (n, dl); n += len(dl)

    def lookup(seg, t):
        if seg == 0 and t == T - 2: return start['SPA0']
        if seg == 0 and t == T - 1: return start['SPA1']
        if seg == 1 and t == 0: return start['SPB0']
        if seg == 1 and t == 1: return start['SPB1']
        if t == 0: return start['TOP0']
        if t == 1: return start['TOP1']
        if t == T - 2: return start['BOT0']
        if t == T - 1: return start['BOT1']
        return start['INT']
    rep = {'INT': (2, 2), 'TOP0': (2, 0), 'TOP1': (2, 1), 'BOT0': (2, T - 2), 'BOT1': (2, T - 1),
           'SPA0': (0, T - 2), 'SPA1': (0, T - 1), 'SPB0': (1, 0), 'SPB1': (1, 1)}
    return cls, start, lookup, rep, n


def host_vmask(SEG, typ):
    R = SEG // 64; T = R // 2
    cls, start, lookup, rep, n = na_classes(T)
    out = np.zeros((n, 128, 128), np.float32)
    qc = np.arange(64)
    cs = np.clip(qc - 8, 0, 48)
    kc = np.arange(64)
    colok = (kc[:, None] >= cs[None, :]) & (kc[:, None] < cs[None, :] + 16)
    for name, dl in cls:
        seg, t = rep[name]
        s0, _ = start[name]
        for i, dl_ in enumerate(dl):
            blk = np.full((2, 64, 2, 64), NEG, np.float32)
            for b in range(2):
                qg = seg * R + 2 * t + b
                if typ == 2 and qg < 2 * R:
                    sq0, nr = 0, 2 * R
                else:
                    sq0, nr = (qg // R) * R, R
                r = qg - sq0
                r0 = min(max(r - 4, 0), nr - 8)
                for a in range(2):
                    kg = seg * R + 2 * (t + dl_) + a
                    if sq0 + r0 <= kg <= sq0 + r0 + 7:
                        blk[a, :, b, :] = np.where(colok, 0.0, NEG)
            out[s0 + i] = blk.reshape(128, 128)
    return out


def host_rt(rpb):
    rt = np.zeros((16, 7, 2, 64, 2, 64), np.float32)
    kc = np.arange(64)[:, None]; qc = np.arange(64)[None, :]
    dcol = kc - qc + 15
    ok = (dcol >= 0) & (dcol <= 30)
    dcc = np.clip(dcol, 0, 30)
    for di, Dl in enumerate(range(-3, 4)):
        for a in range(2):
            for b in range(2):
                dr = 2 * Dl + a - b + 7
                if 0 <= dr <= 14:
                    rt[:, di, a, :, b, :] = np.where(ok[None], rpb[:, dr][:, dcc], 0.0)
    return rt.reshape(16, 7, 128, 128)


def host_cs(SEG, typ):
    NTOK = 3 * SEG + 16
    pos = np.zeros(NTOK, np.float32)
    ar = np.arange(SEG, dtype=np.float32)
    pos[0:SEG] = 16 + ar
    pos[SEG:2 * SEG] = (16 + SEG + ar) if typ == 2 else (16 + ar)
    pos[2 * SEG:3 * SEG] = 16 + ar
    pos[3 * SEG:] = np.arange(16, dtype=np.float32)
    inv = (np.float32(500000.0) ** (-np.arange(0, 16, 2, dtype=np.float32) / np.float32(16))).astype(np.float32)
    ang = (pos[:, None] * inv[None, :]).astype(np.float32)
    return np.concatenate([np.cos(ang), np.sin(ang)], axis=1).astype(np.float32)


def build(SEG, lambda_init, stop=9, dbg=False):
    NT = 3 * SEG
    NTOK = NT + 16
    MOFF = NT
    NCH = NT // 128
    R = SEG // 64; T = R // 2
    KB = min(2048, SEG)
    cls, cstart, na_lookup, _, NSLOT = na_classes(T)

    nc = bass.Bass("TRN2", target_bir_lowering=False)

    def din(name, shape, dt=F32):
        return nc.dram_tensor(name, list(shape), dt, kind="ExternalInput").ap()

    def dscr(name, shape, dt=BF16):
        return nc.dram_tensor(name, list(shape), dt, kind=("ExternalOutput" if dbg else "Internal")).ap()
    xs = din("xs", [NT, D]); meta = din("meta", [16, D]); gcol_d = din("gcol", [128, 96])
    w_in = din("w_in", [4, D, 2 * DFF]); w_out = din("w_out", [4, DFF, D])
    wqkv_a = din("wqkv_a", [D, 3 * D]); wo_a = din("wo_a", [D, D])
    wqkv_b = din("wqkv_b", [D, 3 * D]); wo_b = din("wo_b", [D, D])
    rt_d = din("rt", [16, 7, 128, 128]); vmask_d = din("vmask", [NSLOT, 128, 128])
    mbT_d = din("mbT", [16, 16]); lam_d = din("lam", [1, 256]); subg_d = din("subg", [1, 128])
    cs_d = din("cs", [NTOK, 16]); segm_d = din("segm", [128, 1]); ident_d = din("ident", [128, 128])
    y = nc.dram_tensor("y", [NT, D], F32, kind="ExternalOutput").ap()
    dbgO = nc.dram_tensor("dbgO", [NT, D], F32, kind="ExternalOutput").ap() if dbg else None

    WinS = dscr("WinS", [4, NJ, 128, 2, 8, 128])
    WoutS = dscr("WoutS", [4, 8, 128, NJ, 128])
    WqkA = dscr("WqkA", [16, 128, 8, 128])
    WvA = dscr("WvA", [128, 8, 1024])
    WoA = dscr("WoA", [8, 128, 8, 128])
    WqkB = dscr("WqkB", [128, 8, 2048])
    WvB = dscr("WvB", [128, 8, 1024])
    WoB = dscr("WoB", [8, 128, 8, 128])
    hS = dscr("hS", [128, 8, NTOK], F32)
    QTa = dscr("QTa", [16, 64, NTOK]); KTa = dscr("KTa", [16, 64, NTOK])
    Va = dscr("Va", [16, 128, NCH, 64]); Vma = dscr("Vma", [16, 16, 64])
    OTa = dscr("OTa", [8, 128, NTOK])
    QTb = dscr("QTb", [8, 2, 64, NTOK]); KTb = dscr("KTb", [8, 2, 64, NTOK])
    Vb = dscr("Vb", [8, 128, NCH, 128]); Vmb = dscr("Vmb", [8, 16, 128])

    with ExitStack() as es:
        P = Prog(nc, es)
        sb = lambda name, shape, dt=F32: es.enter_context(nc.sbuf_tensor("s_" + name, list(shape), dt))
        psA = es.enter_context(nc.psum_tensor("psA", [128, 1024], F32))
        psB = es.enter_context(nc.psum_tensor("psB", [128, 1024], F32))
        psO = es.enter_context(nc.psum_tensor("psO", [128, 2048], F32))
        banks = [(psA[:, 0:512], 'b0'), (psA[:, 512:1024], 'b1'), (psB[:, 0:512], 'b2'), (psB[:, 512:1024], 'b3'),
                 (psO[:, 0:512], 'b4'), (psO[:, 512:1024], 'b5'), (psO[:, 1024:1536], 'b6'), (psO[:, 1536:2048], 'b7')]
        identf = sb("identf", [128, 128]); onesb = sb("onesb", [128, 128], BF16)
        gcol = sb("gcol", [128, 96]); g05 = sb("g05", [128, 96])
        epsc = sb("epsc", [128, 1]); zeroc = sb("zeroc", [128, 1]); segm = sb("segm", [128, 1])
        mbT = sb("mbT", [16, 16])
        lamb = sb("lamb", [128, 256]); lamt = sb("lamt", [128, 4]); neglam = sb("neglam", [128, 1])
        subg = sb("subg", [128, 128])
        NWB = 2

        def lin_alloc(tag, stack):
            a = lambda name, shape, dt=F32: stack.enter_context(nc.sbuf_tensor("s_%s_%s" % (tag, name), list(shape), dt))
            return (a("hT", [128, 8, 512]), a("yT", [128, 8, 512]), a("xn", [128, 8, 512], BF16), a("aT", [128, NJ, 512], BF16),
                    a("rstd", [128, 512]), a("tmpf", [128, 512]), [a("sg0", [128, 512]), a("sg1", [128, 512])],
                    [a("wbuf%d" % i, [128, 8192], BF16) for i in range(NWB)])
        hT = yT = xn = aT = rstd = tmpf = sg = wbuf = None
        wctr = [0]

        def wload(src_ap, n_per_part):
            i = wctr[0] % NWB; wctr[0] += 1
            dst = wbuf[i][:, 0:n_per_part]
            if len(src_ap.shape) == 3:
                dstv = dst.rearrange("p (a b) -> p a b", a=src_ap.shape[1])
            else:
                dstv = dst
            P.op('sp', lambda h: h.dma_start(out=dstv, in_=src_ap), writes=['wbuf%d' % i], dma=True)
            return dst, 'wbuf%d' % i
        bctr = [0]

        def nbank(lo=0, hi=4):
            i = lo + bctr[0] % (hi - lo); bctr[0] += 1
            return banks[i]

        P.op('sp', lambda h: h.dma_start(out=identf[:], in_=ident_d), writes=['identf'], dma=True)
        P.op('sp', lambda h: h.dma_start(out=gcol[:], in_=gcol_d), writes=['gcol'], dma=True)
        P.op('sp', lambda h: h.dma_start(out=segm[:], in_=segm_d), writes=['segm'], dma=True)
        P.op('sp', lambda h: h.dma_start(out=mbT[:], in_=mbT_d), writes=['mbT'], dma=True)
        P.op('sp', lambda h: h.dma_start(out=lamb[:], in_=lam_d.partition_broadcast(128)), writes=['lamb'], dma=True)
        P.op('sp', lambda h: h.dma_start(out=subg[:], in_=subg_d.partition_broadcast(128)), writes=['subg'], dma=True)
        P.op('dve', lambda h: h.memset(onesb[:], 1.0), writes=['onesb'])
        P.op('dve', lambda h: h.memset(epsc[:], EPS), writes=['epsc'])
        P.op('dve', lambda h: h.memset(zeroc[:], 0.0), writes=['zeroc'])
        P.op('dve', lambda h: h.tensor_scalar(out=g05[:], in0=gcol[:], scalar1=0.5, scalar2=None, op0=ALU.mult),
             reads=['gcol'], writes=['g05'])
        P.op('dve', lambda h: h.tensor_tensor(out=lamb[:, 0:64], in0=lamb[:, 0:64], in1=lamb[:, 64:128], op=ALU.mult),
             reads=['lamb'], writes=['lamb'])
        P.op('dve', lambda h: h.tensor_tensor(out=lamb[:, 128:192], in0=lamb[:, 128:192], in1=lamb[:, 192:256], op=ALU.mult),
             reads=['lamb'], writes=['lamb'])
        P.op('dve', lambda h: h.tensor_reduce(out=lamt[:, 0:1], in_=lamb[:, 0:64], axis=mybir.AxisListType.X, op=ALU.add),
             reads=['lamb'], writes=['lamt'])
        P.op('dve', lambda h: h.tensor_reduce(out=lamt[:, 1:2], in_=lamb[:, 128:192], axis=mybir.AxisListType.X, op=ALU.add),
             reads=['lamb'], writes=['lamt'])
        P.op('act', lambda h: h.activation(out=lamt[:, 2:4], in_=lamt[:, 0:2], func=AF.Exp), reads=['lamt'], writes=['lamt'])
        P.op('dve', lambda h: h.tensor_tensor(out=neglam[:], in0=lamt[:, 3:4], in1=lamt[:, 2:3], op=ALU.subtract),
             reads=['lamt'], writes=['neglam'])
        P.op('dve', lambda h: h.tensor_scalar(out=neglam[:], in0=neglam[:], scalar1=-float(lambda_init), scalar2=None, op0=ALU.add),
             reads=['neglam'], writes=['neglam'])
        P.op('dve', lambda h: h.tensor_scalar(out=subg[:], in0=subg[:], scalar1=float(1.0 - lambda_init), scalar2=None, op0=ALU.mult),
             reads=['subg'], writes=['subg'])

        def cast(dst, src, name):
            P.op('pool', lambda h: h.dma_start(out=dst, in_=src), writes=[name], dma=True)
        for f4 in range(4):
            for gu in range(2):
                for kc in range(8):
                    cast(WinS[f4, :, :, gu, kc, :].rearrange("j p f -> p j f"),
                         w_in[f4, kc * 128:(kc + 1) * 128, gu * DFF:(gu + 1) * DFF].rearrange("p (j f) -> p j f", f=128), 'WinS')
            for dc in range(8):
                cast(WoutS[f4, dc].rearrange("p fc d -> p fc d"),
                     w_out[f4, :, dc * 128:(dc + 1) * 128].rearrange("(fc p) d -> p fc d", p=128), 'WoutS')
        for kc in range(8):
            rows = slice(kc * 128, (kc + 1) * 128)
            cast(WqkA[:, :, kc, :].rearrange("h p e -> p h e"), wqkv_a[rows, 0:2048].rearrange("p (h e) -> p h e", e=128), 'WqkA')
            cast(WvA[:, kc, :], wqkv_a[rows, 2048:3072], 'WvA')
            cast(WqkB[:, kc, :], wqkv_b[rows, 0:2048], 'WqkB')
            cast(WvB[:, kc, :], wqkv_b[rows, 2048:3072], 'WvB')
            cast(WoA[:, :, kc, :].rearrange("dc p d -> p dc d"), wo_a[rows, :].rearrange("p (dc d) -> p dc d", d=128), 'WoA')
            cast(WoB[:, :, kc, :].rearrange("dc p d -> p dc d"), wo_b[rows, :].rearrange("p (dc d) -> p dc d", d=128), 'WoB')
        P.flush()

        XN = [('xn', c) for c in range(8)]; HT = [('hT', c) for c in range(8)]; YT = [('yT', c) for c in range(8)]

        def rstd_from_sumsq(W):
            psn = banks[7][0][:, 0:W]
            P.op('act', lambda h: h.activation(out=rstd[:, 0:W], in_=psn, func=AF.Ln, bias=epsc[:], scale=1.0 / 1024.0),
                 reads=['b7', 'epsc'], writes=['rstd'])
            P.op('act', lambda h: h.activation(out=rstd[:, 0:W], in_=rstd[:, 0:W], func=AF.Exp, scale=-0.5),
                 reads=['rstd'], writes=['rstd'])

        def sq_chunk(src_ap, src_cells, dst_ap, dst_cells, W):
            P.op('act', lambda h: h.activation(out=dst_ap, in_=src_ap, func=AF.Square), reads=src_cells, writes=dst_cells)

        def ones_mm(sq_ap, sq_cells, c, W):
            psn = banks[7][0][:, 0:W]
            P.op('pe', lambda h: h.matmul(psn, lhsT=onesb[:], rhs=sq_ap, start=(c == 0), stop=(c == 7)),
                 reads=sq_cells + ['onesb'], writes=['b7'])

        def norm_to_xn(gi, W, presq=False):
            if not presq:
                for c in range(8):
                    sq_chunk(hT[:, c, 0:W], [('hT', c)], xn[:, c, 0:W], [('xn', c)], W)
                    ones_mm(xn[:, c, 0:W], [('xn', c)], c, W)
            rstd_from_sumsq(W)
            for c in range(8):
                P.op('dve', lambda h, c=c: h.scalar_tensor_tensor(out=xn[:, c, 0:W], in0=hT[:, c, 0:W],
                                                                  scalar=gcol[:, gi * 8 + c:gi * 8 + c + 1], in1=rstd[:, 0:W],
                                                                  op0=ALU.mult, op1=ALU.mult),
                     reads=[('hT', c), 'rstd', 'gcol'], writes=[('xn', c)])

        def postnorm_add(gi, W, half, next_norm):
            gt = g05 if half else gcol
            rstd_from_sumsq(W)
            for c in range(8):
                P.op('dve', lambda h, c=c: h.scalar_tensor_tensor(out=tmpf[:, 0:W], in0=yT[:, c, 0:W],
                                                                  scalar=gt[:, gi * 8 + c:gi * 8 + c + 1], in1=rstd[:, 0:W],
                                                                  op0=ALU.mult, op1=ALU.mult),
                     reads=[('yT', c), 'rstd', 'g05', 'gcol'], writes=['tmpf'])
                P.op('dve', lambda h, c=c: h.tensor_tensor(out=hT[:, c, 0:W], in0=hT[:, c, 0:W], in1=tmpf[:, 0:W], op=ALU.add),
                     reads=[('hT', c), 'tmpf'], writes=[('hT', c)])
                if next_norm:
                    sq_chunk(hT[:, c, 0:W], [('hT', c)], xn[:, c, 0:W], [('xn', c)], W)
                    ones_mm(xn[:, c, 0:W], [('xn', c)], c, W)

        def ffn(f4, W, presq=False, next_norm=True):
            l, i = f4 // 2, f4 % 2
            norm_to_xn(l * 6 + (0 if i == 0 else 4), W, presq)
            JG = 4
            for j0 in range(0, NJ, JG):
                nj = min(JG, NJ - j0)
                wt, wn = wload(WinS[f4, j0:j0 + nj].rearrange("j p g k f -> p j (g k f)"), nj * 2048)
                wv = wt.rearrange("p (j g k f) -> p j g k f", j=nj, g=2, k=8)
                for jj in range(nj):
                    j = j0 + jj
                    (pg, pgn), (pu, pun) = banks[(2 * j) % 4], banks[(2 * j + 1) % 4]
                    for g_, (pp, ppn) in enumerate(((pg, pgn), (pu, pun))):
                        for kc in range(8):
                            P.op('pe', lambda h, pp=pp, g_=g_, kc=kc, jj=jj, wv=wv: h.matmul(
                                pp[:, 0:W], lhsT=wv[:, jj, g_, kc, :], rhs=xn[:, kc, 0:W], start=(kc == 0), stop=(kc == 7)),
                                 reads=[('xn', kc), wn], writes=[ppn])
                    sgt = sg[j % 2]; sgn = 'sg%d' % (j % 2)
                    P.op('act', lambda h, pg=pg, sgt=sgt: h.activation(out=sgt[:, 0:W], in_=pg[:, 0:W], func=AF.Silu),
                         reads=[pgn], writes=[sgn])
                    P.op('dve', lambda h, pu=pu, sgt=sgt, j=j: h.tensor_tensor(out=aT[:, j, 0:W], in0=sgt[:, 0:W], in1=pu[:, 0:W], op=ALU.mult),
                         reads=[pun, sgn], writes=['aT'])
            pend = None
            for d0 in range(0, 8, 2):
                wt, wn = wload(WoutS[f4, d0:d0 + 2].rearrange("dc p fc d -> p dc (fc d)"), 2 * NJ * 128)
                wv = wt.rearrange("p (dc fc d) -> p dc fc d", dc=2, fc=NJ)
                for dd in range(2):
                    dc = d0 + dd
                    pp, ppn = banks[4 + dc % 2]
                    for fc in range(NJ):
                        P.op('pe', lambda h, pp=pp, dd=dd, fc=fc, wv=wv: h.matmul(
                            pp[:, 0:W], lhsT=wv[:, dd, fc, :], rhs=aT[:, fc, 0:W], start=(fc == 0), stop=(fc == NJ - 1)),
                             reads=['aT', wn], writes=[ppn])
                    if pend is not None:
                        ones_mm(xn[:, pend, 0:W], [('xn', pend)], pend, W)
                    P.op('act', lambda h, pp=pp, dc=dc: h.activation(out=yT[:, dc, 0:W], in_=pp[:, 0:W], func=AF.Copy),
                         reads=[ppn], writes=[('yT', dc)])
                    sq_chunk(pp[:, 0:W], [ppn], xn[:, dc, 0:W], [('xn', dc)], W)
                    pend = dc
            ones_mm(xn[:, pend, 0:W], [('xn', pend)], pend, W)
            postnorm_add(l * 6 + (1 if i == 0 else 5), W, True, next_norm)

        def proj_fm(Wscr, W):
            pend = None
            for d0 in range(0, 8, 4):
                wt, wn = wload(Wscr[d0:d0 + 4].rearrange("dc p k d -> p dc (k d)"), 4 * 1024)
                wv = wt.rearrange("p (dc k d) -> p dc k d", dc=4, k=8)
                for dd in range(4):
                    dc = d0 + dd
                    pp, ppn = banks[4 + dc % 2]
                    for kc in range(8):
                        P.op('pe', lambda h, pp=pp, dd=dd, kc=kc, wv=wv: h.matmul(
                            pp[:, 0:W], lhsT=wv[:, dd, kc, :], rhs=xn[:, kc, 0:W], start=(kc == 0), stop=(kc == 7)),
                             reads=[('xn', kc), wn], writes=[ppn])
                    if pend is not None:
                        ones_mm(aT[:, pend, 0:W], ['aT'], pend, W)
                    P.op('act', lambda h, pp=pp, dc=dc: h.activation(out=yT[:, dc, 0:W], in_=pp[:, 0:W], func=AF.Copy),
                         reads=[ppn], writes=[('yT', dc)])
                    sq_chunk(pp[:, 0:W], [ppn], aT[:, dc, 0:W], ['aT'], W)
                    pend = dc
            ones_mm(aT[:, pend, 0:W], ['aT'], pend, W)

        tiles = [(t0, 512) for t0 in range(0, NT, 512)] + [(MOFF, 16)]

        with ExitStack() as es1:
            hT, yT, xn, aT, rstd, tmpf, sg, wbuf = lin_alloc('p1', es1)
            sb1 = lambda name, shape, dt=F32: es1.enter_context(nc.sbuf_tensor("s_" + name, list(shape), dt))
            xt = [sb1("xt0", [128, 1024]), sb1("xt1", [128, 1024])]
            qk_sb = sb1("qk_sb", [128, 16, 512], BF16)
            v_sb = [sb1("v_sb0", [128, 1024], BF16), sb1("v_sb1", [128, 1024], BF16)]
            xcl = [0]

            def p1_tile(t0, W):
                ismeta = (W == 16)
                nsub = 1 if ismeta else 4
                for s_ in range(nsub):
                    npt = 16 if ismeta else 128
                    xc = xcl[0]; xcl[0] += 1
                    xtt = xt[xc % 2]; xtn = 'xt%d' % (xc % 2)
                    src = meta if ismeta else xs[t0 + s_ * 128: t0 + (s_ + 1) * 128, :]
                    P.op('sp', lambda h, xtt=xtt, src=src, npt=npt: h.dma_start(out=xtt[0:npt, :], in_=src), writes=[xtn], dma=True)
                    for c0 in (0, 4):
                        pp, ppn = nbank(0, 4)
                        for cc in range(4):
                            c = c0 + cc
                            P.op('pe', lambda h, pp=pp, cc=cc, c=c, xtt=xtt, npt=npt: h.transpose(
                                pp[:, cc * 128: cc * 128 + npt], xtt[0:npt, c * 128:(c + 1) * 128], identf[0:npt, 0:npt]),
                                 reads=[xtn, 'identf'], writes=[ppn])
                        P.op('dve', lambda h, pp=pp, c0=c0, s_=s_, npt=npt: h.tensor_copy(
                            out=hT[:, c0:c0 + 4, s_ * 128: s_ * 128 + npt],
                            in_=pp.rearrange("p (c t) -> p c t", c=4)[:, :, 0:npt]), reads=[ppn], writes=[*HT])
                ffn(0, W, False, True)
                P.op('pool', lambda h, t0=t0, W=W: h.dma_start(out=hS[:, :, t0:t0 + W], in_=hT[:, :, 0:W]),
                     reads=[*HT], writes=[('hS', t0)], dma=True)
                norm_to_xn(2, W, True)
                for h0 in range(0, 16, 8):
                    wt, wn = wload(WqkA[h0:h0 + 8].rearrange("h p k e -> p h (k e)"), 8 * 1024)
                    wv = wt.rearrange("p (h k e) -> p h k e", h=8, k=8)
                    for hh in range(8):
                        hd = h0 + hh
                        pp, ppn = nbank(0, 4)
                        for kc in range(8):
                            P.op('pe', lambda h, pp=pp, hh=hh, kc=kc, wv=wv: h.matmul(
                                pp[:, 0:W], lhsT=wv[:, hh, kc, :], rhs=xn[:, kc, 0:W], start=(kc == 0), stop=(kc == 7)),
                                 reads=[*XN, wn], writes=[ppn])
                        P.op('act', lambda h, pp=pp, hd=hd: h.activation(out=qk_sb[:, hd, 0:W], in_=pp[:, 0:W], func=AF.Copy,
                                                                          scale=(0.125 if hd < 8 else 1.0)),
                             reads=[ppn], writes=['qk_sb'])
                P.op('pool', lambda h, t0=t0, W=W: h.dma_start(out=QTa[:, :, t0:t0 + W].rearrange("(i two) e t -> (two e) i t", two=2), in_=qk_sb[:, 0:8, 0:W]),
                     reads=['qk_sb'], writes=[('QTa', t0)], dma=True)
                P.op('pool', lambda h, t0=t0, W=W: h.dma_start(out=KTa[:, :, t0:t0 + W].rearrange("(i two) e t -> (two e) i t", two=2), in_=qk_sb[:, 8:16, 0:W]),
                     reads=['qk_sb'], writes=[('KTa', t0)], dma=True)
                wt, wn = wload(WvA, 8192)
                wv = wt.rearrange("p (k n) -> p k n", k=8)
                for s_ in range(nsub):
                    npt = 16 if ismeta else 128
                    vt = v_sb[s_ % 2]; vn = 'v_sb%d' % (s_ % 2)
                    for half in range(2):
                        pp, ppn = nbank(0, 4)
                        for kc in range(8):
                            P.op('pe', lambda h, pp=pp, kc=kc, s_=s_, half=half, npt=npt, wv=wv: h.matmul(
                                pp[0:npt, :], lhsT=xn[:, kc, s_ * 128: s_ * 128 + npt], rhs=wv[:, kc, half * 512:(half + 1) * 512],
                                start=(kc == 0), stop=(kc == 7)), reads=[*XN, wn], writes=[ppn])
                        P.op('dve', lambda h, pp=pp, vt=vt, half=half, npt=npt: h.tensor_copy(out=vt[0:npt, half * 512:(half + 1) * 512], in_=pp[0:npt, :]),
                             reads=[ppn], writes=[vn])
                    if ismeta:
                        P.op('pool', lambda h, vt=vt: h.dma_start(out=Vma.rearrange("h m e -> m h e"),
                                                                  in_=vt[0:16, :].rearrange("m (h e) -> m h e", e=64)),
                             reads=[vn], writes=['Vma'], dma=True)
                    else:
                        ch = t0 // 128 + s_
                        P.op('pool', lambda h, vt=vt, ch=ch: h.dma_start(out=Va[:, :, ch, :].rearrange("h p e -> p h e"),
                                                                         in_=vt[:, :].rearrange("p (h e) -> p h e", e=64)),
                             reads=[vn], writes=[('Va', ch)], dma=True)
            for (t0_, W_) in tiles:
                p1_tile(t0_, W_)
            P.flush()
        if stop <= 1:
            return nc

        with ExitStack() as es2:
            sb2 = lambda name, shape, dt=F32: es2.enter_context(nc.sbuf_tensor("s_" + name, list(shape), dt))
            KT = sb2("KT", [64, NTOK], BF16); QT = sb2("QT", [64, NTOK], BF16); OT = sb2("OT", [64, NTOK], BF16)
            Vh = sb2("Vh", [128, NCH, 64], BF16); Vmh = sb2("Vmh", [16, 64], BF16)
            RTh = sb2("RTh", [128, 7, 128]); RTV = sb2("RTV", [128, NSLOT, 128]); vmask = sb2("vmask", [128, NSLOT, 128])
            ssb = [sb2("ssb%d" % i, [128, 6, 128]) for i in range(3)]
            pt = [sb2("pt%d" % i, [128, 6, 128], BF16) for i in range(3)]
            pm = [sb2("pm%d" % i, [16, 128], BF16) for i in range(3)]
            rl = [sb2("rl0", [64, 128]), sb2("rl1", [64, 128])]
            P.op('sp', lambda h: h.dma_start(out=vmask[:], in_=vmask_d.rearrange("s k q -> k s q")), writes=['vmask'], dma=True)
            psS = [(psA, 'psA'), (psB, 'psB')]
            it = 0
            for hd in range(16):
                P.op('sp', lambda h, hd=hd: h.dma_start(out=KT[:], in_=KTa[hd]), writes=['KT'], dma=True)
                P.op('sp', lambda h, hd=hd: h.dma_start(out=QT[:], in_=QTa[hd]), writes=['QT'], dma=True)
                P.op('sp', lambda h, hd=hd: h.dma_start(out=Vh[:], in_=Va[hd]), writes=['Vh'], dma=True)
                P.op('sp', lambda h, hd=hd: h.dma_start(out=Vmh[:], in_=Vma[hd]), writes=['Vmh'], dma=True)
                P.op('sp', lambda h, hd=hd: h.dma_start(out=RTh[:], in_=rt_d[hd].rearrange("d k q -> k d q")), writes=['RTh'], dma=True)
                for name, dl in cls:
                    s0, _ = cstart[name]
                    for i_, dl_ in enumerate(dl):
                        P.op('pool', lambda h, s0=s0, i_=i_, dl_=dl_: h.tensor_tensor(out=RTV[:, s0 + i_, :], in0=vmask[:, s0 + i_, :],
                                                                                      in1=RTh[:, dl_ + 3, :], op=ALU.add),
                             reads=['vmask', 'RTh'], writes=['RTV'])
                def na_s1(gt, b_, b3, hd=hd):
                    ismeta = (gt == 3 * T)
                    pS, pSn = psS[b_]
                    pX, pXn = banks[4 + b_]
                    nq = 16 if ismeta else 128
                    q0 = MOFF if ismeta else gt * 128
                    if not ismeta:
                        seg, t = gt // T, gt % T
                        s0, dl = na_lookup(seg, t)
                        nb = len(dl)
                        for i_, dl_ in enumerate(dl):
                            gc = gt + dl_
                            P.op('pe', lambda h, pS=pS, i_=i_, gc=gc, q0=q0: h.matmul(
                                pS[:, i_ * 128:(i_ + 1) * 128], lhsT=KT[:, gc * 128:(gc + 1) * 128], rhs=QT[:, q0:q0 + 128], start=True, stop=True),
                                 reads=['KT', 'QT'], writes=[pSn])
                    P.op('pe', lambda h, pS=pS, q0=q0, nq=nq: h.matmul(pS[0:16, 768:768 + nq], lhsT=KT[:, MOFF:MOFF + 16], rhs=QT[:, q0:q0 + nq], start=True, stop=True),
                         reads=['KT', 'QT'], writes=[pSn])
                    P.op('act', lambda h, pS=pS, b3=b3, nq=nq, hd=hd: h.activation(out=pm[b3][:, 0:nq], in_=pS[0:16, 768:768 + nq], func=AF.Exp, bias=mbT[:, hd:hd + 1]),
                         reads=[pSn, 'mbT'], writes=['pm%d' % b3])
                    if not ismeta:
                        P.op('dve', lambda h, pS=pS, b3=b3, nb=nb, s0=s0: h.tensor_tensor(
                            out=ssb[b3][:, 0:nb, :], in0=pS[:, 0:nb * 128].rearrange("p (n q) -> p n q", q=128),
                            in1=RTV[:, s0:s0 + nb, :], op=ALU.add), reads=[pSn, 'RTV', 'pm%d' % b3], writes=['ssb%d' % b3])
                        P.op('act', lambda h, b3=b3, nb=nb: h.activation(out=pt[b3][:, 0:nb, :], in_=ssb[b3][:, 0:nb, :], func=AF.Exp),
                             reads=['ssb%d' % b3], writes=['pt%d' % b3])

                def na_s2(gt, b_, b3, hd=hd):
                    ismeta = (gt == 3 * T)
                    pX, pXn = banks[4 + b_]
                    nq = 16 if ismeta else 128
                    q0 = MOFF if ismeta else gt * 128
                    if not ismeta:
                        seg, t = gt // T, gt % T
                        s0, dl = na_lookup(seg, t)
                        for i_, dl_ in enumerate(dl):
                            gc = gt + dl_
                            P.op('pe', lambda h, pX=pX, i_=i_, gc=gc, b3=b3: h.matmul(
                                pX[0:64, 128:256], lhsT=Vh[:, gc, :], rhs=pt[b3][:, i_, :], start=(i_ == 0), stop=False),
                                 reads=['Vh', 'pt%d' % b3], writes=[pXn])
                    P.op('pe', lambda h, pX=pX, b3=b3, nq=nq, ismeta=ismeta: h.matmul(pX[0:64, 128:128 + nq], lhsT=Vmh[:, :], rhs=pm[b3][:, 0:nq], start=ismeta, stop=True),
                         reads=['Vmh', 'pm%d' % b3], writes=[pXn])
                    if not ismeta:
                        for i_, dl_ in enumerate(dl):
                            P.op('pe', lambda h, pX=pX, i_=i_, b3=b3: h.matmul(
                                pX[0:64, 256:384], lhsT=onesb[:, 0:64], rhs=pt[b3][:, i_, :], start=(i_ == 0), stop=False),
                                 reads=['onesb', 'pt%d' % b3], writes=[pXn])
                    P.op('pe', lambda h, pX=pX, b3=b3, nq=nq, ismeta=ismeta: h.matmul(pX[0:64, 256:256 + nq], lhsT=onesb[0:16, 0:64], rhs=pm[b3][:, 0:nq], start=ismeta, stop=True),
                         reads=['onesb', 'pm%d' % b3], writes=[pXn])
                    P.op('dve', lambda h, pX=pX, b_=b_, nq=nq: h.reciprocal(out=rl[b_][:, 0:nq], in_=pX[0:64, 256:256 + nq]),
                         reads=[pXn], writes=['rl%d' % b_])
                    P.op('dve', lambda h, pX=pX, b_=b_, nq=nq, q0=q0: h.tensor_tensor(out=OT[:, q0:q0 + nq], in0=pX[0:64, 128:128 + nq], in1=rl[b_][:, 0:nq], op=ALU.mult),
                         reads=[pXn, 'rl%d' % b_], writes=['OT'])

                pend = []
                for gt in range(3 * T + 1):
                    b_ = it % 2; b3 = it % 3; it += 1
                    na_s1(gt, b_, b3)
                    pend.append((gt, b_, b3))
                    if len(pend) > 2:
                        na_s2(*pend.pop(0))
                while pend:
                    na_s2(*pend.pop(0))
                P.op('pool', lambda h, hd=hd: h.dma_start(out=OTa[hd // 2, (hd % 2) * 64:(hd % 2) * 64 + 64, :], in_=OT[:]),
                     reads=['OT'], writes=[('OTa', hd)], dma=True)
            P.flush()
        if stop <= 2:
            return nc

        with ExitStack() as es3:
            hT, yT, xn, aT, rstd, tmpf, sg, wbuf = lin_alloc('p3', es3)
            sb3 = lambda name, shape, dt=F32: es3.enter_context(nc.sbuf_tensor("s_" + name, list(shape), dt))
            qk_tm = sb3("qk_tm", [128, 2048])
            rtmp = [sb3("rtmp%d" % i, [128, 32, 8]) for i in range(4)]
            cst = sb3("cst", [128, 16])
            qkT = sb3("qkT", [64, 32, 512], BF16)
            v_sb = [sb3("v3_sb0", [128, 1024], BF16), sb3("v3_sb1", [128, 1024], BF16)]

            def p3_tile(t0, W):
                ismeta = (W == 16)
                nsub = 1 if ismeta else 4
                npt = 16 if ismeta else 128
                P.op('sp', lambda h, t0=t0, W=W: h.dma_start(out=xn[:, :, 0:W], in_=OTa[:, :, t0:t0 + W].rearrange("c p t -> p c t")),
                     writes=[*XN], dma=True)
                P.op('sp', lambda h, t0=t0, W=W: h.dma_start(out=hT[:, :, 0:W], in_=hS[:, :, t0:t0 + W]), writes=[*HT], dma=True)
                proj_fm(WoA, W)
                postnorm_add(3, W, False, True)
                ffn(1, W, True, True)
                ffn(2, W, True, True)
                P.op('pool', lambda h, t0=t0, W=W: h.dma_start(out=hS[:, :, t0:t0 + W], in_=hT[:, :, 0:W]),
                     reads=[*HT], writes=[('hS', t0)], dma=True)
                norm_to_xn(6 + 2, W, True)
                for s_ in range(nsub):
                    for cb in range(4):
                        if cb % 2 == 0:
                            wt, wn = wload(WqkB[:, :, cb * 512:(cb + 2) * 512], 8192)
                            wv = wt.rearrange("p (k n) -> p k n", k=8)
                        pp, ppn = nbank(0, 4)
                        for kc in range(8):
                            P.op('pe', lambda h, pp=pp, kc=kc, s_=s_, cb=cb, wv=wv: h.matmul(
                                pp[0:npt, :], lhsT=xn[:, kc, s_ * 128: s_ * 128 + npt], rhs=wv[:, kc, (cb % 2) * 512:(cb % 2 + 1) * 512],
                                start=(kc == 0), stop=(kc == 7)), reads=[*XN, wn], writes=[ppn])
                        P.op('act', lambda h, pp=pp, cb=cb: h.activation(out=qk_tm[0:npt, cb * 512:(cb + 1) * 512], in_=pp[0:npt, :], func=AF.Copy),
                             reads=[ppn], writes=['qk_tm'])
                    tk = t0 + s_ * 128
                    P.op('sp', lambda h, tk=tk: h.dma_start(out=cst[0:npt, :], in_=cs_d[tk:tk + npt, :]), writes=['cst'], dma=True)
                    qv = qk_tm[0:npt, :].rearrange("p (m e) -> p m e", e=64)
                    x1 = qv[:, :, 0:8]; x2 = qv[:, :, 8:16]
                    cosb = cst[0:npt, 0:8].unsqueeze(1).to_broadcast([npt, 32, 8])
                    sinb = cst[0:npt, 8:16].unsqueeze(1).to_broadcast([npt, 32, 8])
                    for k_, (a_, b2) in enumerate(((x1, cosb), (x2, sinb), (x2, cosb), (x1, sinb))):
                        P.op('dve', lambda h, k_=k_, a_=a_, b2=b2: h.tensor_tensor(out=rtmp[k_][0:npt], in0=a_, in1=b2, op=ALU.mult),
                             reads=['qk_tm', 'cst'], writes=['rtmp%d' % k_])
                    P.op('dve', lambda h, x1=x1: h.tensor_tensor(out=x1, in0=rtmp[0][0:npt], in1=rtmp[1][0:npt], op=ALU.subtract),
                         reads=['rtmp0', 'rtmp1'], writes=['qk_tm'])
                    P.op('dve', lambda h, x2=x2: h.tensor_tensor(out=x2, in0=rtmp[2][0:npt], in1=rtmp[3][0:npt], op=ALU.add),
                         reads=['rtmp2', 'rtmp3'], writes=['qk_tm'])
                    for i0 in range(0, 32, 4):
                        pp, ppn = nbank(0, 4)
                        for ii in range(4):
                            i_ = i0 + ii
                            P.op('pe', lambda h, pp=pp, ii=ii, i_=i_: h.transpose(
                                pp[0:64, ii * 128: ii * 128 + npt], qk_tm[0:npt, i_ * 64:(i_ + 1) * 64], identf[0:npt, 0:npt]),
                                 reads=['qk_tm', 'identf'], writes=[ppn])
                        P.op('act', lambda h, pp=pp, i0=i0, s_=s_: h.activation(
                            out=qkT[:, i0:i0 + 4, s_ * 128: s_ * 128 + npt], in_=pp[0:64, :].rearrange("p (i t) -> p i t", i=4)[:, :, 0:npt],
                            func=AF.Copy, scale=(0.125 if i0 < 16 else 1.0)), reads=[ppn], writes=['qkT'])
                    vt = v_sb[s_ % 2]; vn = 'v3_sb%d' % (s_ % 2)
                    wt, wn = wload(WvB, 8192)
                    wv = wt.rearrange("p (k n) -> p k n", k=8)
                    for half in range(2):
                        pp, ppn = nbank(0, 4)
                        for kc in range(8):
                            P.op('pe', lambda h, pp=pp, kc=kc, s_=s_, half=half, wv=wv: h.matmul(
                                pp[0:npt, :], lhsT=xn[:, kc, s_ * 128: s_ * 128 + npt], rhs=wv[:, kc, half * 512:(half + 1) * 512],
                                start=(kc == 0), stop=(kc == 7)), reads=[*XN, wn], writes=[ppn])
                        P.op('dve', lambda h, pp=pp, vt=vt, half=half: h.tensor_copy(out=vt[0:npt, half * 512:(half + 1) * 512], in_=pp[0:npt, :]),
                             reads=[ppn], writes=[vn])
                    if ismeta:
                        P.op('pool', lambda h, vt=vt: h.dma_start(out=Vmb.rearrange("h m e -> m h e"),
                                                                  in_=vt[0:16, :].rearrange("m (h e) -> m h e", e=128)),
                             reads=[vn], writes=['Vmb'], dma=True)
                    else:
                        ch = t0 // 128 + s_
                        P.op('pool', lambda h, vt=vt, ch=ch: h.dma_start(out=Vb[:, :, ch, :].rearrange("h p e -> p h e"),
                                                                         in_=vt[:, :].rearrange("p (h e) -> p h e", e=128)),
                             reads=[vn], writes=[('Vb', ch)], dma=True)
                P.op('pool', lambda h, t0=t0, W=W: h.dma_start(out=QTb[:, :, :, t0:t0 + W].rearrange("h m e t -> e (h m) t"), in_=qkT[:, 0:16, 0:W]),
                     reads=['qkT'], writes=[('QTb', t0)], dma=True)
                P.op('pool', lambda h, t0=t0, W=W: h.dma_start(out=KTb[:, :, :, t0:t0 + W].rearrange("h m e t -> e (h m) t"), in_=qkT[:, 16:32, 0:W]),
                     reads=['qkT'], writes=[('KTb', t0)], dma=True)
            for (t0_, W_) in tiles:
                p3_tile(t0_, W_)
            P.flush()
        if stop <= 3:
            return nc

        with ExitStack() as es4:
            hT, yT, xn, aT, rstd, tmpf, sg, wbuf = lin_alloc('p4', es4)
            sb4 = lambda name, shape, dt=F32: es4.enter_context(nc.sbuf_tensor("s_" + name, list(shape), dt))
            NKC = KB // 128
            Kb = [sb4("Kb%d" % i, [128, KB], BF16) for i in range(2)]
            Vp = [sb4("Vp%d" % i, [128, NKC, 130], BF16) for i in range(2)]
            Km = sb4("Km", [128, 8, 16], BF16); Vpm = sb4("Vpm", [16, 8, 130], BF16)
            Qg = [sb4("Qg%d" % i, [128, 512], BF16) for i in range(2)]
            PT = [sb4("PT%d" % i, [128, 2, 512], BF16) for i in range(3)]
            Otm = sb4("Otm", [128, 4, 1024])
            t0s = sb4("t0s", [128, 128]); dsq = sb4("dsq", [128, 128])
            dsb4 = [sb4("dsb%d" % i, [128, 128]) for i in range(4)]; rec4 = sb4("rec4", [128, 16])
            rec = sb4("rec", [128, 4]); ss = sb4("ss", [128, 2])
            ot = [sb4("ot0", [128, 1024]), sb4("ot1", [128, 1024])]
            for i in range(2):
                P.op('pool', lambda h, i=i: h.memset(Vp[i][:], 1.0), writes=['Vp%d' % i])
            P.op('pool', lambda h: h.memset(Vpm[:], 1.0), writes=['Vpm'])
            P.op('sp', lambda h: h.dma_start(out=Km[:], in_=KTb[:, :, :, MOFF:MOFF + 16].rearrange("h m e t -> (m e) h t")), writes=['Km'], dma=True)
            P.op('sp', lambda h: h.dma_start(out=Vpm[:, :, 0:128], in_=Vmb.rearrange("h m e -> m h e")), reads=['Vpm'], writes=['Vpm'], dma=True)
            psS = [(psA, ['b0', 'b1']), (psB, ['b2', 'b3'])]
            PSO = ['b4', 'b5', 'b6', 'b7']
            kvc = 0; sc = 0
            NG = NT // 512
            GPS = SEG // 512
            for g in range(NG):
                seg = g // GPS
                q0 = g * 512
                ksegs = [0, 1] if seg < 2 else [2]
                st = {'kvc': kvc, 'sc': sc}

                def gen_units():
                    for hd in range(8):
                        qb = Qg[hd % 2]; qbn = 'Qg%d' % (hd % 2)
                        blocks = [(ks, kb0) for ks in ksegs for kb0 in range(0, SEG, KB)] + [None]
                        first = True
                        for u in blocks:
                            nch = NKC if u is not None else 1
                            for j in range(nch):
                                yield dict(hd=hd, qb=qb, qbn=qbn, u=u, j=j, first=first, last=(u is None), newhead=(first), newblk=(j == 0))
                                first = False

                def prep(d):
                    hd = d['hd']
                    if d['newhead']:
                        P.op('sp', lambda h, qb=d['qb'], hd=hd, q0=q0: h.dma_start(out=qb[:], in_=QTb[hd, :, :, q0:q0 + 512].rearrange("m e t -> (m e) t")),
                             writes=[d['qbn']], dma=True)
                    if d['u'] is not None:
                        ks, kb0 = d['u']
                        if d['newblk']:
                            i = st['kvc'] % 2; st['kvc'] += 1
                            st['i'] = i
                            k0 = ks * SEG + kb0
                            P.op('sp', lambda h, i=i, hd=hd, k0=k0: h.dma_start(out=Kb[i][:], in_=KTb[hd, :, :, k0:k0 + KB].rearrange("m e t -> (m e) t")),
                                 writes=['Kb%d' % i], dma=True)
                            P.op('sp', lambda h, i=i, hd=hd, k0=k0: h.dma_start(out=Vp[i][:, :, 0:128], in_=Vb[hd, :, k0 // 128:k0 // 128 + NKC, :]),
                                 reads=['Vp%d' % i], writes=['Vp%d' % i], dma=True)
                        i = st['i']; j = d['j']
                        d['bias'] = zeroc if ks == seg else segm
                        d['kap'] = Kb[i][:, j * 128:(j + 1) * 128]; d['vap'] = Vp[i][:, j, 0:129]; d['nk'] = 128
                        d['rd'] = ['Kb%d' % i, 'Vp%d' % i]
                    else:
                        d['bias'] = zeroc
                        d['kap'] = Km[:, hd, :]; d['vap'] = Vpm[:, hd, 0:129]; d['nk'] = 16; d['rd'] = ['Km', 'Vpm']
                    d['b'] = st['sc'] % 2; d['pb'] = st['sc'] % 3; st['sc'] += 1

                def emit_S(d):
                    b_ = d['b']; pS, pSn = psS[b_]; nk = d['nk']; kap = d['kap']; qb = d['qb']
                    for m in range(2):
                        P.op('pe', lambda h, pS=pS, m=m, kap=kap, qb=qb, nk=nk: h.matmul(
                            pS[0:nk, m * 512:(m + 1) * 512], lhsT=kap[m * 64:(m + 1) * 64, :], rhs=qb[m * 64:(m + 1) * 64, :], start=True, stop=True,
                            tile_position=(m * 64, 0)),
                             reads=[d['rd'][0], d['qbn']], writes=pSn)
                    pb = d['pb']
                    P.op('act', lambda h, pS=pS, pb=pb, nk=nk, bias=d['bias']: h.activation(
                        out=PT[pb][0:nk].rearrange("p m q -> p (m q)"), in_=pS[0:nk, :], func=AF.Exp, bias=bias[0:nk, :]),
                         reads=pSn + ['segm', 'zeroc'], writes=['PT%d' % pb])

                def emit_AV(d):
                    b_ = d['pb']; nk = d['nk']; vap = d['vap']
                    for qs in range(4):
                        for m in range(2):
                            a_ = qs * 2 + m
                            P.op('pe', lambda h, a_=a_, qs=qs, m=m, b_=b_, vap=vap, nk=nk, first=d['first'], last=d['last']: h.matmul(
                                psO[:, a_ * 256: a_ * 256 + 129], lhsT=PT[b_][0:nk, m, qs * 128:(qs + 1) * 128], rhs=vap[0:nk, :],
                                start=(first and a_ % 2 == 0), stop=last), reads=['PT%d' % b_, d['rd'][1]], writes=PSO)
                    if d['last']:
                        combine(d['hd'])

                def combine(hd):
                    for qs in range(4):
                        a0 = psO[:, (2 * qs) * 256:(2 * qs) * 256 + 129]; a1 = psO[:, (2 * qs + 1) * 256:(2 * qs + 1) * 256 + 129]
                        r_ = rec4[:, qs * 4:(qs + 1) * 4]; rn = 'rec%d' % qs; dn = 'dsb%d' % qs
                        P.op('dve', lambda h, a0=a0, r_=r_: h.reciprocal(out=r_[:, 0:1], in_=a0[:, 128:129]), reads=PSO, writes=[rn])
                        P.op('dve', lambda h, a1=a1, r_=r_: h.reciprocal(out=r_[:, 1:2], in_=a1[:, 128:129]), reads=PSO, writes=[rn])
                        P.op('dve', lambda h, r_=r_: h.tensor_tensor(out=r_[:, 2:3], in0=r_[:, 1:2], in1=neglam[:], op=ALU.mult),
                             reads=[rn, 'neglam'], writes=[rn])
                        P.op('dve', lambda h, a0=a0, r_=r_: h.tensor_scalar(out=t0s[:], in0=a0[:, 0:128], scalar1=r_[:, 0:1], scalar2=None, op0=ALU.mult),
                             reads=PSO + [rn], writes=['t0s'])
                        P.op('dve', lambda h, a1=a1, r_=r_, qs=qs: h.scalar_tensor_tensor(out=dsb4[qs][:], in0=a1[:, 0:128], scalar=r_[:, 2:3], in1=t0s[:],
                                                                                      op0=ALU.mult, op1=ALU.add),
                             reads=PSO + [rn, 't0s'], writes=[dn])
                    for qs in range(4):
                        dn = 'dsb%d' % qs; d_ = dsb4[qs]
                        P.op('dve', lambda h: h.memset(ss[:, 0:1], 0.0), writes=['ss'])
                        P.op('dve', lambda h, d_=d_: h.scalar_tensor_tensor(out=dsq[:], in0=d_[:], scalar=1.0, in1=d_[:], op0=ALU.mult, op1=ALU.mult,
                                                                            accum_out=ss[:, 0:1]), reads=[dn], writes=['dsq', 'ss'])
                        P.op('act', lambda h: h.activation(out=ss[:, 1:2], in_=ss[:, 0:1], func=AF.Ln, bias=epsc[:], scale=1.0 / 128),
                             reads=['ss', 'epsc'], writes=['ss'])
                        P.op('act', lambda h: h.activation(out=ss[:, 1:2], in_=ss[:, 1:2], func=AF.Exp, scale=-0.5), reads=['ss'], writes=['ss'])
                        P.op('dve', lambda h, qs=qs, hd=hd, d_=d_: h.scalar_tensor_tensor(out=Otm[:, qs, hd * 128:(hd + 1) * 128], in0=d_[:], scalar=ss[:, 1:2],
                                                                                        in1=subg[:], op0=ALU.mult, op1=ALU.mult),
                             reads=[dn, 'ss', 'subg'], writes=['Otm'])

                pending = []
                for d in gen_units():
                    prep(d)
                    emit_S(d)
                    pending.append(d)
                    if len(pending) > 2:
                        emit_AV(pending.pop(0))
                while pending:
                    emit_AV(pending.pop(0))
                kvc = st['kvc']; sc = st['sc']
                if dbg:
                    for qs in range(4):
                        P.op('pool', lambda h, qs=qs, q0=q0: h.dma_start(out=dbgO[q0 + qs * 128:q0 + (qs + 1) * 128, :], in_=Otm[:, qs, :]),
                             reads=['Otm'], writes=[('dbgO', q0, qs)], dma=True)
                for qs in range(4):
                    for c0 in (0, 4):
                        pp, ppn = nbank(0, 4)
                        for cc in range(4):
                            c = c0 + cc
                            P.op('pe', lambda h, pp=pp, cc=cc, c=c, qs=qs: h.transpose(
                                pp[:, cc * 128:(cc + 1) * 128], Otm[:, qs, c * 128:(c + 1) * 128], identf[:]),
                                 reads=['Otm', 'identf'], writes=[ppn])
                        P.op('dve', lambda h, pp=pp, c0=c0, qs=qs: h.tensor_copy(
                            out=xn[:, c0:c0 + 4, qs * 128:(qs + 1) * 128], in_=pp.rearrange("p (c t) -> p c t", c=4)), reads=[ppn], writes=[*XN])
                P.op('sp', lambda h, q0=q0: h.dma_start(out=hT[:], in_=hS[:, :, q0:q0 + 512]), writes=[*HT], dma=True)
                proj_fm(WoB, 512)
                postnorm_add(6 + 3, 512, False, True)
                ffn(3, 512, True, False)
                for qs in range(4):
                    o_ = ot[qs % 2]; on_ = 'ot%d' % (qs % 2)
                    for c0 in (0, 4):
                        pp, ppn = nbank(0, 4)
                        for cc in range(4):
                            c = c0 + cc
                            P.op('pe', lambda h, pp=pp, cc=cc, c=c, qs=qs: h.transpose(
                                pp[:, cc * 128:(cc + 1) * 128], hT[:, c, qs * 128:(qs + 1) * 128], identf[:]),
                                 reads=[*HT, 'identf'], writes=[ppn])
                        P.op('act', lambda h, pp=pp, c0=c0, o_=o_: h.activation(out=o_[:, c0 * 128:(c0 + 4) * 128], in_=pp[:, :], func=AF.Copy),
                             reads=[ppn], writes=[on_])
                    P.op('pool', lambda h, o_=o_, q0=q0, qs=qs: h.dma_start(out=y[q0 + qs * 128: q0 + (qs + 1) * 128, :], in_=o_[:]),
                         reads=[on_], writes=[('y', q0, qs)], dma=True)
            P.flush()
    return nc


def host_inputs(SEG, segsA, typ, common):
    m = dict(common)
    m["xs"] = np.ascontiguousarray(np.concatenate(segsA, axis=0), dtype=np.float32)
    m["vmask"] = host_vmask(SEG, typ)
    m["cs"] = host_cs(SEG, typ)
    m["segm"] = np.full((128, 1), 0.0 if typ == 2 else NEG, np.float32)
    return m


def host_common(meta_tokens, norm_g, w_ffn_in, w_ffn_out, w_qkv_a, w_o_a, rpb_a, meta_bias_a, w_qkv_b, w_o_b, lambda_b, subln_b):
    f = lambda a: np.ascontiguousarray(np.asarray(a, dtype=np.float32))
    g = f(norm_g).reshape(12, 8, 128)
    return {
        "meta": f(meta_tokens),
        "gcol": np.ascontiguousarray(g.transpose(2, 0, 1).reshape(128, 96)),
        "w_in": f(w_ffn_in).reshape(4, D, 2 * DFF), "w_out": f(w_ffn_out).reshape(4, DFF, D),
        "wqkv_a": f(w_qkv_a)[0], "wo_a": f(w_o_a)[0], "wqkv_b": f(w_qkv_b)[0], "wo_b": f(w_o_b)[0],
        "rt": host_rt(f(rpb_a)[0]),
        "mbT": np.ascontiguousarray(f(meta_bias_a)[0].T),
        "lam": f(lambda_b)[0].reshape(1, 256), "subg": f(subln_b)[0].reshape(1, 128),
        "ident": np.eye(128, dtype=np.float32),
    }


def run(SEG, x_prompt, x_sample, params):
    lambda_init = 0.8 - 0.6 * math.exp(-0.3 * 1)
    common = host_common(**params)
    nc = build(SEG, lambda_init)
    xp = np.asarray(x_prompt, dtype=np.float32); xsamp = np.asarray(x_sample, dtype=np.float32)
    in_maps = []
    for c in range(4):
        in_maps.append(host_inputs(SEG, [xsamp[c, :SEG], xsamp[c, SEG:], xp[c]], 2, common))
    for c in range(4):
        in_maps.append(host_inputs(SEG, [xp[4 + 3 * c], xp[5 + 3 * c], xp[6 + 3 * c]], 1, common))
    res = run_bass_kernel_spmd(nc, in_maps, core_ids=list(range(8)))
    yp = np.zeros_like(xp); ysamp = np.zeros_like(xsamp)
    for c in range(4):
        yc = res.results[c]["y"]
        ysamp[c, :SEG] = yc[0:SEG]; ysamp[c, SEG:] = yc[SEG:2 * SEG]; yp[c] = yc[2 * SEG:]
    for c in range(4):
        yc = res.results[4 + c]["y"]
        for k in range(3):
            yp[4 + 3 * c + k] = yc[k * SEG:(k + 1) * SEG]
    return yp, ysamp


def kernel(x_prompt, x_sample, meta_tokens, norm_g, w_ffn_in, w_ffn_out, w_qkv_a, w_o_a,
           rpb_a, meta_bias_a, w_qkv_b, w_o_b, lambda_b, subln_b):
    params = dict(meta_tokens=meta_tokens, norm_g=norm_g, w_ffn_in=w_ffn_in, w_ffn_out=w_ffn_out,
                  w_qkv_a=w_qkv_a, w_o_a=w_o_a, rpb_a=rpb_a, meta_bias_a=meta_bias_a,
                  w_qkv_b=w_qkv_b, w_o_b=w_o_b, lambda_b=lambda_b, subln_b=subln_b)
    yp, ysamp = run(4096, x_prompt, x_sample, params)
    return (yp, ysamp)
```

```python
import math
from contextlib import ExitStack
import numpy as np
import concourse.bass as bass
import concourse.mybir as mybir
from concourse.bass_utils import run_bass_kernel_spmd

F32 = mybir.dt.float32
BF16 = mybir.dt.bfloat16
AF = mybir.ActivationFunctionType
ALU = mybir.AluOpType
ENGS = ['pe', 'act', 'dve', 'pool', 'sp']
DMA_SLOTS = 8
D = 1024
DFF = 2816
NJ = 22
NEG = -30000.0
EPS = 1e-6


class Op:
    __slots__ = ('eng', 'fn', 'deps', 'needed', 'dma', 'val', 'slot', 'prev')

    def __init__(s, eng, fn, dma):
        s.eng = eng; s.fn = fn; s.dma = dma; s.deps = set(); s.needed = False
        s.val = 0; s.slot = None; s.prev = 0


class Prog:
    def __init__(s, nc, es):
        s.nc = nc
        s.streams = {e: [] for e in ENGS}
        s.cells = {}
        s.ndma = {e: 0 for e in ENGS}
        s.count = {e: 0 for e in ENGS}
        s.seen = {e: {} for e in ENGS}
        s.csem = {e: es.enter_context(nc.semaphore('c_' + e)) for e in ENGS}
        s.dsem = {e: [es.enter_context(nc.semaphore('d_%s%d' % (e, i))) for i in range(DMA_SLOTS)]
                  for e in ('sp', 'pool', 'act')}
        s.nblk = 0

    def op(s, eng, fn, reads=(), writes=(), dma=False):
        o = Op(eng, fn, dma)
        deps = set()
        for c in reads:
            st = s.cells.get(c)
            if st is not None and st[0] is not None:
                deps.add(st[0])
        for c in writes:
            st = s.cells.get(c)
            if st is not None:
                if st[0] is not None:
                    deps.add(st[0])
                deps.update(st[1])
        for d in deps:
            if d.eng == 'pe' and eng == 'pe' and not d.dma and not dma:
                continue
            o.deps.add(d); d.needed = True
        for c in reads:
            st = s.cells.setdefault(c, [None, []])
            st[1].append(o)
        for c in writes:
            s.cells[c] = [o, []]
        if dma:
            n = s.ndma[eng]; s.ndma[eng] = n + 1
            o.slot = n % DMA_SLOTS; o.val = 16 * (n // DMA_SLOTS + 1); o.prev = 16 * (n // DMA_SLOTS)
        s.streams[eng].append(o)
        return o

    def flush(s):
        nc = s.nc
        handles = {'pe': 'tensor', 'act': 'scalar', 'dve': 'vector', 'pool': 'gpsimd', 'sp': 'sync'}
        for e in ENGS:
            c = s.count[e]
            for o in s.streams[e]:
                if not o.dma and o.needed:
                    c += 1; o.val = c
            s.count[e] = c
        s.nblk += 1
        with nc.Block() as block:
            def mk(e):
                def body(h):
                    seen = s.seen[e]

                    def wait(key, sem, val):
                        if seen.get(key, 0) < val:
                            h.wait_ge(sem, val); seen[key] = val
                    for o in s.streams[e]:
                        for d in o.deps:
                            if d.dma:
                                wait(('d', d.eng, d.slot), s.dsem[d.eng][d.slot], d.val)
                            else:
                                wait(('c', d.eng), s.csem[d.eng], d.val)
                        if o.dma and o.prev > 0:
                            wait(('d', e, o.slot), s.dsem[e][o.slot], o.prev)
                        ins = o.fn(h)
                        if o.dma:
                            ins.then_inc(s.dsem[e][o.slot], 16)
                        elif o.needed:
                            ins.then_inc(s.csem[e], 1)
                    n = s.ndma[e]
                    if n > 0:
                        for sl in range(min(n, DMA_SLOTS)):
                            last = ((n - 1 - sl) // DMA_SLOTS) + 1
                            wait(('d', e, sl), s.dsem[e][sl], 16 * last)
                return body
            for e in ENGS:
                getattr(block, handles[e])(mk(e))
        s.streams = {e: [] for e in ENGS}
        s.cells = {}


def na_classes(T):
    cls = [('INT', [-2, -1, 0, 1, 2]), ('TOP0', [0, 1, 2, 3]), ('TOP1', [-1, 0, 1, 2]),
           ('BOT0', [-2, -1, 0, 1]), ('BOT1', [-3, -2, -1, 0]),
           ('SPA0', [-2, -1, 0, 1, 2]), ('SPA1', [-3, -2, -1, 0, 1, 2]),
           ('SPB0', [-2, -1, 0, 1, 2, 3]), ('SPB1', [-2, -1, 0, 1, 2])]
    start = {}
    n = 0
    for name, dl in cls:
        start[name] = (n, dl); n += len(dl)

    def lookup(seg, t):
        if seg == 0 and t == T - 2: return start['SPA0']
        if seg == 0 and t == T - 1: return start['SPA1']
        if seg == 1 and t == 0: return start['SPB0']
        if seg == 1 and t == 1: return start['SPB1']
        if t == 0: return start['TOP0']
        if t == 1: return start['TOP1']
        if t == T - 2: return start['BOT0']
        if t == T - 1: return start['BOT1']
        return start['INT']
    rep = {'INT': (2, 2), 'TOP0': (2, 0), 'TOP1': (2, 1), 'BOT0': (2, T - 2), 'BOT1': (2, T - 1),
           'SPA0': (0, T - 2), 'SPA1': (0, T - 1), 'SPB0': (1, 0), 'SPB1': (1, 1)}
    return cls, start, lookup, rep, n


def host_vmask(SEG, typ):
    R = SEG // 64; T = R // 2
    cls, start, lookup, rep, n = na_classes(T)
    out = np.zeros((n, 128, 128), np.float32)
    qc = np.arange(64)
    cs = np.clip(qc - 8, 0, 48)
    kc = np.arange(64)
    colok = (kc[:, None] >= cs[None, :]) & (kc[:, None] < cs[None, :] + 16)
    for name, dl in cls:
        seg, t = rep[name]
        s0, _ = start[name]
        for i, dl_ in enumerate(dl):
            blk = np.full((2, 64, 2, 64), NEG, np.float32)
            for b in range(2):
                qg = seg * R + 2 * t + b
                if typ == 2 and qg < 2 * R:
                    sq0, nr = 0, 2 * R
                else:
                    sq0, nr = (qg // R) * R, R
                r = qg - sq0
                r0 = min(max(r - 4, 0), nr - 8)
                for a in range(2):
                    kg = seg * R + 2 * (t + dl_) + a
                    if sq0 + r0 <= kg <= sq0 + r0 + 7:
                        blk[a, :, b, :] = np.where(colok, 0.0, NEG)
            out[s0 + i] = blk.reshape(128, 128)
    return out


def host_rt(rpb):
    rt = np.zeros((16, 7, 2, 64, 2, 64), np.float32)
    kc = np.arange(64)[:, None]; qc = np.arange(64)[None, :]
    dcol = kc - qc + 15
    ok = (dcol >= 0) & (dcol <= 30)
    dcc = np.clip(dcol, 0, 30)
    for di, Dl in enumerate(range(-3, 4)):
        for a in range(2):
            for b in range(2):
                dr = 2 * Dl + a - b + 7
                if 0 <= dr <= 14:
                    rt[:, di, a, :, b, :] = np.where(ok[None], rpb[:, dr][:, dcc], 0.0)
    return rt.reshape(16, 7, 128, 128)


def host_cs(SEG, typ):
    NTOK = 3 * SEG + 16
    pos = np.zeros(NTOK, np.float32)
    ar = np.arange(SEG, dtype=np.float32)
    pos[0:SEG] = 16 + ar
    pos[SEG:2 * SEG] = (16 + SEG + ar) if typ == 2 else (16 + ar)
    pos[2 * SEG:3 * SEG] = 16 + ar
    pos[3 * SEG:] = np.arange(16, dtype=np.float32)
    inv = (np.float32(500000.0) ** (-np.arange(0, 16, 2, dtype=np.float32) / np.float32(16))).astype(np.float32)
    ang = (pos[:, None] * inv[None, :]).astype(np.float32)
    return np.concatenate([np.cos(ang), np.sin(ang)], axis=1).astype(np.float32)


def build(SEG, lambda_init, stop=9, dbg=False):
    NT = 3 * SEG
    NTOK = NT + 16
    MOFF = NT
    NCH = NT // 128
    R = SEG // 64; T = R // 2
    KB = min(2048, SEG)
    cls, cstart, na_lookup, _, NSLOT = na_classes(T)

    nc = bass.Bass("TRN2", target_bir_lowering=False)

    def din(name, shape, dt=F32):
        return nc.dram_tensor(name, list(shape), dt, kind="ExternalInput").ap()

    def dscr(name, shape, dt=BF16):
        return nc.dram_tensor(name, list(shape), dt, kind=("ExternalOutput" if dbg else "Internal")).ap()
    xs = din("xs", [NT, D]); meta = din("meta", [16, D]); gcol_d = din("gcol", [128, 96])
    w_in = din("w_in", [4, D, 2 * DFF]); w_out = din("w_out", [4, DFF, D])
    wqkv_a = din("wqkv_a", [D, 3 * D]); wo_a = din("wo_a", [D, D])
    wqkv_b = din("wqkv_b", [D, 3 * D]); wo_b = din("wo_b", [D, D])
    rt_d = din("rt", [16, 7, 128, 128]); vmask_d = din("vmask", [NSLOT, 128, 128])
    mbT_d = din("mbT", [16, 16]); lam_d = din("lam", [1, 256]); subg_d = din("subg", [1, 128])
    cs_d = din("cs", [NTOK, 16]); segm_d = din("segm", [128, 1]); ident_d = din("ident", [128, 128])
    y = nc.dram_tensor("y", [NT, D], F32, kind="ExternalOutput").ap()
    dbgO = nc.dram_tensor("dbgO", [NT, D], F32, kind="ExternalOutput").ap() if dbg else None

    WinS = dscr("WinS", [4, NJ, 128, 2, 8, 128])
    WoutS = dscr("WoutS", [4, 8, 128, NJ, 128])
    WqkA = dscr("WqkA", [16, 128, 8, 128])
    WvA = dscr("WvA", [128, 8, 1024])
    WoA = dscr("WoA", [8, 128, 8, 128])
    WqkB = dscr("WqkB", [128, 8, 2048])
    WvB = dscr("WvB", [128, 8, 1024])
    WoB = dscr("WoB", [8, 128, 8, 128])
    hS = dscr("hS", [128, 8, NTOK], F32)
    QTa = dscr("QTa", [16, 64, NTOK]); KTa = dscr("KTa", [16, 64, NTOK])
    Va = dscr("Va", [16, 128, NCH, 64]); Vma = dscr("Vma", [16, 16, 64])
    OTa = dscr("OTa", [8, 128, NTOK])
    QTb = dscr("QTb", [8, 2, 64, NTOK]); KTb = dscr("KTb", [8, 2, 64, NTOK])
    Vb = dscr("Vb", [8, 128, NCH, 128]); Vmb = dscr("Vmb", [8, 16, 128])

    with ExitStack() as es:
        P = Prog(nc, es)
        sb = lambda name, shape, dt=F32: es.enter_context(nc.sbuf_tensor("s_" + name, list(shape), dt))
        psA = es.enter_context(nc.psum_tensor("psA", [128, 1024], F32))
        psB = es.enter_context(nc.psum_tensor("psB", [128, 1024], F32))
        psO = es.enter_context(nc.psum_tensor("psO", [128, 2048], F32))
        banks = [(psA[:, 0:512], 'b0'), (psA[:, 512:1024], 'b1'), (psB[:, 0:512], 'b2'), (psB[:, 512:1024], 'b3'),
                 (psO[:, 0:512], 'b4'), (psO[:, 512:1024], 'b5'), (psO[:, 1024:1536], 'b6'), (psO[:, 1536:2048], 'b7')]
        identf = sb("identf", [128, 128]); onesb = sb("onesb", [128, 128], BF16)
        gcol = sb("gcol", [128, 96]); g05 = sb("g05", [128, 96])
        epsc = sb("epsc", [128, 1]); zeroc = sb("zeroc", [128, 1]); segm = sb("segm", [128, 1])
        mbT = sb("mbT", [16, 16])
        lamb = sb("lamb", [128, 256]); lamt = sb("lamt", [128, 4]); neglam = sb("neglam", [128, 1])
        subg = sb("subg", [128, 128])
        NWB = 2

        def lin_alloc(tag, stack):
            a = lambda name, shape, dt=F32: stack.enter_context(nc.sbuf_tensor("s_%s_%s" % (tag, name), list(shape), dt))
            return (a("hT", [128, 8, 512]), a("yT", [128, 8, 512]), a("xn", [128, 8, 512], BF16), a("aT", [128, NJ, 512], BF16),
                    a("rstd", [128, 512]), a("tmpf", [128, 512]), [a("sg0", [128, 512]), a("sg1", [128, 512])],
                    [a("wbuf%d" % i, [128, 8192], BF16) for i in range(NWB)])
        hT = yT = xn = aT = rstd = tmpf = sg = wbuf = None
        wctr = [0]

        def wload(src_ap, n_per_part):
            i = wctr[0] % NWB; wctr[0] += 1
            dst = wbuf[i][:, 0:n_per_part]
            if len(src_ap.shape) == 3:
                dstv = dst.rearrange("p (a b) -> p a b", a=src_ap.shape[1])
            else:
                dstv = dst
            P.op('sp', lambda h: h.dma_start(out=dstv, in_=src_ap), writes=['wbuf%d' % i], dma=True)
            return dst, 'wbuf%d' % i
        bctr = [0]

        def nbank(lo=0, hi=4):
            i = lo + bctr[0] % (hi - lo); bctr[0] += 1
            return banks[i]

        P.op('sp', lambda h: h.dma_start(out=identf[:], in_=ident_d), writes=['identf'], dma=True)
        P.op('sp', lambda h: h.dma_start(out=gcol[:], in_=gcol_d), writes=['gcol'], dma=True)
        P.op('sp', lambda h: h.dma_start(out=segm[:], in_=segm_d), writes=['segm'], dma=True)
        P.op('sp', lambda h: h.dma_start(out=mbT[:], in_=mbT_d), writes=['mbT'], dma=True)
        P.op('sp', lambda h: h.dma_start(out=lamb[:], in_=lam_d.partition_broadcast(128)), writes=['lamb'], dma=True)
        P.op('sp', lambda h: h.dma_start(out=subg[:], in_=subg_d.partition_broadcast(128)), writes=['subg'], dma=True)
        P.op('dve', lambda h: h.memset(onesb[:], 1.0), writes=['onesb'])
        P.op('dve', lambda h: h.memset(epsc[:], EPS), writes=['epsc'])
        P.op('dve', lambda h: h.memset(zeroc[:], 0.0), writes=['zeroc'])
        P.op('dve', lambda h: h.tensor_scalar(out=g05[:], in0=gcol[:], scalar1=0.5, scalar2=None, op0=ALU.mult),
             reads=['gcol'], writes=['g05'])
        P.op('dve', lambda h: h.tensor_tensor(out=lamb[:, 0:64], in0=lamb[:, 0:64], in1=lamb[:, 64:128], op=ALU.mult),
             reads=['lamb'], writes=['lamb'])
        P.op('dve', lambda h: h.tensor_tensor(out=lamb[:, 128:192], in0=lamb[:, 128:192], in1=lamb[:, 192:256], op=ALU.mult),
             reads=['lamb'], writes=['lamb'])
        P.op('dve', lambda h: h.tensor_reduce(out=lamt[:, 0:1], in_=lamb[:, 0:64], axis=mybir.AxisListType.X, op=ALU.add),
             reads=['lamb'], writes=['lamt'])
        P.op('dve', lambda h: h.tensor_reduce(out=lamt[:, 1:2], in_=lamb[:, 128:192], axis=mybir.AxisListType.X, op=ALU.add),
             reads=['lamb'], writes=['lamt'])
        P.op('act', lambda h: h.activation(out=lamt[:, 2:4], in_=lamt[:, 0:2], func=AF.Exp), reads=['lamt'], writes=['lamt'])
        P.op('dve', lambda h: h.tensor_tensor(out=neglam[:], in0=lamt[:, 3:4], in1=lamt[:, 2:3], op=ALU.subtract),
             reads=['lamt'], writes=['neglam'])
        P.op('dve', lambda h: h.tensor_scalar(out=neglam[:], in0=neglam[:], scalar1=-float(lambda_init), scalar2=None, op0=ALU.add),
             reads=['neglam'], writes=['neglam'])
        P.op('dve', lambda h: h.tensor_scalar(out=subg[:], in0=subg[:], scalar1=float(1.0 - lambda_init), scalar2=None, op0=ALU.mult),
             reads=['subg'], writes=['subg'])

        def cast(dst, src, name):
            P.op('pool', lambda h: h.dma_start(out=dst, in_=src), writes=[name], dma=True)
        for f4 in range(4):
            for gu in range(2):
                for kc in range(8):
                    cast(WinS[f4, :, :, gu, kc, :].rearrange("j p f -> p j f"),
                         w_in[f4, kc * 128:(kc + 1) * 128, gu * DFF:(gu + 1) * DFF].rearrange("p (j f) -> p j f", f=128), 'WinS')
            for dc in range(8):
                cast(WoutS[f4, dc].rearrange("p fc d -> p fc d"),
                     w_out[f4, :, dc * 128:(dc + 1) * 128].rearrange("(fc p) d -> p fc d", p=128), 'WoutS')
        for kc in range(8):
            rows = slice(kc * 128, (kc + 1) * 128)
            cast(WqkA[:, :, kc, :].rearrange("h p e -> p h e"), wqkv_a[rows, 0:2048].rearrange("p (h e) -> p h e", e=128), 'WqkA')
            cast(WvA[:, kc, :], wqkv_a[rows, 2048:3072], 'WvA')
            cast(WqkB[:, kc, :], wqkv_b[rows, 0:2048], 'WqkB')
            cast(WvB[:, kc, :], wqkv_b[rows, 2048:3072], 'WvB')
            cast(WoA[:, :, kc, :].rearrange("dc p d -> p dc d"), wo_a[rows, :].rearrange("p (dc d) -> p dc d", d=128), 'WoA')
            cast(WoB[:, :, kc, :].rearrange("dc p d -> p dc d"), wo_b[rows, :].rearrange("p (dc d) -> p dc d", d=128), 'WoB')
        P.flush()

        XN = [('xn', c) for c in range(8)]; HT = [('hT', c) for c in range(8)]; YT = [('yT', c) for c in range(8)]

        def rstd_from_sumsq(W):
            psn = banks[7][0][:, 0:W]
            P.op('act', lambda h: h.activation(out=rstd[:, 0:W], in_=psn, func=AF.Ln, bias=epsc[:], scale=1.0 / 1024.0),
                 reads=['b7', 'epsc'], writes=['rstd'])
            P.op('act', lambda h: h.activation(out=rstd[:, 0:W], in_=rstd[:, 0:W], func=AF.Exp, scale=-0.5),
                 reads=['rstd'], writes=['rstd'])

        def sq_chunk(src_ap, src_cells, dst_ap, dst_cells, W):
            P.op('act', lambda h: h.activation(out=dst_ap, in_=src_ap, func=AF.Square), reads=src_cells, writes=dst_cells)

        def ones_mm(sq_ap, sq_cells, c, W):
            psn = banks[7][0][:, 0:W]
            P.op('pe', lambda h: h.matmul(psn, lhsT=onesb[:], rhs=sq_ap, start=(c == 0), stop=(c == 7)),
                 reads=sq_cells + ['onesb'], writes=['b7'])

        def norm_to_xn(gi, W, presq=False):
            if not presq:
                for c in range(8):
                    sq_chunk(hT[:, c, 0:W], [('hT', c)], xn[:, c, 0:W], [('xn', c)], W)
                    ones_mm(xn[:, c, 0:W], [('xn', c)], c, W)
            rstd_from_sumsq(W)
            for c in range(8):
                P.op('dve', lambda h, c=c: h.scalar_tensor_tensor(out=xn[:, c, 0:W], in0=hT[:, c, 0:W],
                                                                  scalar=gcol[:, gi * 8 + c:gi * 8 + c + 1], in1=rstd[:, 0:W],
                                                                  op0=ALU.mult, op1=ALU.mult),
                     reads=[('hT', c), 'rstd', 'gcol'], writes=[('xn', c)])

        def postnorm_add(gi, W, half, next_norm):
            gt = g05 if half else gcol
            rstd_from_sumsq(W)
            for c in range(8):
                P.op('dve', lambda h, c=c: h.scalar_tensor_tensor(out=tmpf[:, 0:W], in0=yT[:, c, 0:W],
                                                                  scalar=gt[:, gi * 8 + c:gi * 8 + c + 1], in1=rstd[:, 0:W],
                                                                  op0=ALU.mult, op1=ALU.mult),
                     reads=[('yT', c), 'rstd', 'g05', 'gcol'], writes=['tmpf'])
                P.op('dve', lambda h, c=c: h.tensor_tensor(out=hT[:, c, 0:W], in0=hT[:, c, 0:W], in1=tmpf[:, 0:W], op=ALU.add),
                     reads=[('hT', c), 'tmpf'], writes=[('hT', c)])
                if next_norm:
                    sq_chunk(hT[:, c, 0:W], [('hT', c)], xn[:, c, 0:W], [('xn', c)], W)
                    ones_mm(xn[:, c, 0:W], [('xn', c)], c, W)

        def ffn(f4, W, presq=False, next_norm=True):
            l, i = f4 // 2, f4 % 2
            norm_to_xn(l * 6 + (0 if i == 0 else 4), W, presq)
            JG = 4
            for j0 in range(0, NJ, JG):
                nj = min(JG, NJ - j0)
                wt, wn = wload(WinS[f4, j0:j0 + nj].rearrange("j p g k f -> p j (g k f)"), nj * 2048)
                wv = wt.rearrange("p (j g k f) -> p j g k f", j=nj, g=2, k=8)
                for jj in range(nj):
                    j = j0 + jj
                    (pg, pgn), (pu, pun) = banks[(2 * j) % 4], banks[(2 * j + 1) % 4]
                    for g_, (pp, ppn) in enumerate(((pg, pgn), (pu, pun))):
                        for kc in range(8):
                            P.op('pe', lambda h, pp=pp, g_=g_, kc=kc, jj=jj, wv=wv: h.matmul(
                                pp[:, 0:W], lhsT=wv[:, jj, g_, kc, :], rhs=xn[:, kc, 0:W], start=(kc == 0), stop=(kc == 7)),
                                 reads=[('xn', kc), wn], writes=[ppn])
                    sgt = sg[j % 2]; sgn = 'sg%d' % (j % 2)
                    P.op('act', lambda h, pg=pg, sgt=sgt: h.activation(out=sgt[:, 0:W], in_=pg[:, 0:W], func=AF.Silu),
                         reads=[pgn], writes=[sgn])
                    P.op('dve', lambda h, pu=pu, sgt=sgt, j=j: h.tensor_tensor(out=aT[:, j, 0:W], in0=sgt[:, 0:W], in1=pu[:, 0:W], op=ALU.mult),
                         reads=[pun, sgn], writes=['aT'])
            pend = None
            for d0 in range(0, 8, 2):
                wt, wn = wload(WoutS[f4, d0:d0 + 2].rearrange("dc p fc d -> p dc (fc d)"), 2 * NJ * 128)
                wv = wt.rearrange("p (dc fc d) -> p dc fc d", dc=2, fc=NJ)
                for dd in range(2):
                    dc = d0 + dd
                    pp, ppn = banks[4 + dc % 2]
                    for fc in range(NJ):
                        P.op('pe', lambda h, pp=pp, dd=dd, fc=fc, wv=wv: h.matmul(
                            pp[:, 0:W], lhsT=wv[:, dd, fc, :], rhs=aT[:, fc, 0:W], start=(fc == 0), stop=(fc == NJ - 1)),
                             reads=['aT', wn], writes=[ppn])
                    if pend is not None:
                        ones_mm(xn[:, pend, 0:W], [('xn', pend)], pend, W)
                    P.op('act', lambda h, pp=pp, dc=dc: h.activation(out=yT[:, dc, 0:W], in_=pp[:, 0:W], func=AF.Copy),
                         reads=[ppn], writes=[('yT', dc)])
                    sq_chunk(pp[:, 0:W], [ppn], xn[:, dc, 0:W], [('xn', dc)], W)
                    pend = dc
            ones_mm(xn[:, pend, 0:W], [('xn', pend)], pend, W)
            postnorm_add(l * 6 + (1 if i == 0 else 5), W, True, next_norm)

        def proj_fm(Wscr, W):
            pend = None
            for d0 in range(0, 8, 4):
                wt, wn = wload(Wscr[d0:d0 + 4].rearrange("dc p k d -> p dc (k d)"), 4 * 1024)
                wv = wt.rearrange("p (dc k d) -> p dc k d", dc=4, k=8)
                for dd in range(4):
                    dc = d0 + dd
                    pp, ppn = banks[4 + dc % 2]
                    for kc in range(8):
                        P.op('pe', lambda h, pp=pp, dd=dd, kc=kc, wv=wv: h.matmul(
                            pp[:, 0:W], lhsT=wv[:, dd, kc, :], rhs=xn[:, kc, 0:W], start=(kc == 0), stop=(kc == 7)),
                             reads=[('xn', kc), wn], writes=[ppn])
                    if pend is not None:
                        ones_mm(aT[:, pend, 0:W], ['aT'], pend, W)
                    P.op('act', lambda h, pp=pp, dc=dc: h.activation(out=yT[:, dc, 0:W], in_=pp[:, 0:W], func=AF.Copy),
                         reads=[ppn], writes=[('yT', dc)])
                    sq_chunk(pp[:, 0:W], [ppn], aT[:, dc, 0:W], ['aT'], W)
                    pend = dc
            ones_mm(aT[:, pend, 0:W], ['aT'], pend, W)

        tiles = [(t0, 512) for t0 in range(0, NT, 512)] + [(MOFF, 16)]

        with ExitStack() as es1:
            hT, yT, xn, aT, rstd, tmpf, sg, wbuf = lin_alloc('p1', es1)
            sb1 = lambda name, shape, dt=F32: es1.enter_context(nc.sbuf_tensor("s_" + name, list(shape), dt))
            xt = [sb1("xt0", [128, 1024]), sb1("xt1", [128, 1024])]
            qk_sb = sb1("qk_sb", [128, 16, 512], BF16)
            v_sb = [sb1("v_sb0", [128, 1024], BF16), sb1("v_sb1", [128, 1024], BF16)]
            xcl = [0]

            def p1_tile(t0, W):
                ismeta = (W == 16)
                nsub = 1 if ismeta else 4
                for s_ in range(nsub):
                    npt = 16 if ismeta else 128
                    xc = xcl[0]; xcl[0] += 1
                    xtt = xt[xc % 2]; xtn = 'xt%d' % (xc % 2)
                    src = meta if ismeta else xs[t0 + s_ * 128: t0 + (s_ + 1) * 128, :]
                    P.op('sp', lambda h, xtt=xtt, src=src, npt=npt: h.dma_start(out=xtt[0:npt, :], in_=src), writes=[xtn], dma=True)
                    for c0 in (0, 4):
                        pp, ppn = nbank(0, 4)
                        for cc in range(4):
                            c = c0 + cc
                            P.op('pe', lambda h, pp=pp, cc=cc, c=c, xtt=xtt, npt=npt: h.transpose(
                                pp[:, cc * 128: cc * 128 + npt], xtt[0:npt, c * 128:(c + 1) * 128], identf[0:npt, 0:npt]),
                                 reads=[xtn, 'identf'], writes=[ppn])
                        P.op('dve', lambda h, pp=pp, c0=c0, s_=s_, npt=npt: h.tensor_copy(
                            out=hT[:, c0:c0 + 4, s_ * 128: s_ * 128 + npt],
                            in_=pp.rearrange("p (c t) -> p c t", c=4)[:, :, 0:npt]), reads=[ppn], writes=[*HT])
                ffn(0, W, False, True)
                P.op('pool', lambda h, t0=t0, W=W: h.dma_start(out=hS[:, :, t0:t0 + W], in_=hT[:, :, 0:W]),
                     reads=[*HT], writes=[('hS', t0)], dma=True)
                norm_to_xn(2, W, True)
                for h0 in range(0, 16, 8):
                    wt, wn = wload(WqkA[h0:h0 + 8].rearrange("h p k e -> p h (k e)"), 8 * 1024)
                    wv = wt.rearrange("p (h k e) -> p h k e", h=8, k=8)
                    for hh in range(8):
                        hd = h0 + hh
                        pp, ppn = nbank(0, 4)
                        for kc in range(8):
                            P.op('pe', lambda h, pp=pp, hh=hh, kc=kc, wv=wv: h.matmul(
                                pp[:, 0:W], lhsT=wv[:, hh, kc, :], rhs=xn[:, kc, 0:W], start=(kc == 0), stop=(kc == 7)),
                                 reads=[*XN, wn], writes=[ppn])
                        P.op('act', lambda h, pp=pp, hd=hd: h.activation(out=qk_sb[:, hd, 0:W], in_=pp[:, 0:W], func=AF.Copy,
                                                                          scale=(0.125 if hd < 8 else 1.0)),
                             reads=[ppn], writes=['qk_sb'])
                P.op('pool', lambda h, t0=t0, W=W: h.dma_start(out=QTa[:, :, t0:t0 + W].rearrange("(i two) e t -> (two e) i t", two=2), in_=qk_sb[:, 0:8, 0:W]),
                     reads=['qk_sb'], writes=[('QTa', t0)], dma=True)
                P.op('pool', lambda h, t0=t0, W=W: h.dma_start(out=KTa[:, :, t0:t0 + W].rearrange("(i two) e t -> (two e) i t", two=2), in_=qk_sb[:, 8:16, 0:W]),
                     reads=['qk_sb'], writes=[('KTa', t0)], dma=True)
                wt, wn = wload(WvA, 8192)
                wv = wt.rearrange("p (k n) -> p k n", k=8)
                for s_ in range(nsub):
                    npt = 16 if ismeta else 128
                    vt = v_sb[s_ % 2]; vn = 'v_sb%d' % (s_ % 2)
                    for half in range(2):
                        pp, ppn = nbank(0, 4)
                        for kc in range(8):
                            P.op('pe', lambda h, pp=pp, kc=kc, s_=s_, half=half, npt=npt, wv=wv: h.matmul(
                                pp[0:npt, :], lhsT=xn[:, kc, s_ * 128: s_ * 128 + npt], rhs=wv[:, kc, half * 512:(half + 1) * 512],
                                start=(kc == 0), stop=(kc == 7)), reads=[*XN, wn], writes=[ppn])
                        P.op('dve', lambda h, pp=pp, vt=vt, half=half, npt=npt: h.tensor_copy(out=vt[0:npt, half * 512:(half + 1) * 512], in_=pp[0:npt, :]),
                             reads=[ppn], writes=[vn])
                    if ismeta:
                        P.op('pool', lambda h, vt=vt: h.dma_start(out=Vma.rearrange("h m e -> m h e"),
                                                                  in_=vt[0:16, :].rearrange("m (h e) -> m h e", e=64)),
                             reads=[vn], writes=['Vma'], dma=True)
                    else:
                        ch = t0 // 128 + s_
                        P.op('pool', lambda h, vt=vt, ch=ch: h.dma_start(out=Va[:, :, ch, :].rearrange("h p e -> p h e"),
                                                                         in_=vt[:, :].rearrange("p (h e) -> p h e", e=64)),
                             reads=[vn], writes=[('Va', ch)], dma=True)
            for (t0_, W_) in tiles:
                p1_tile(t0_, W_)
            P.flush()
        if stop <= 1:
            return nc

        with ExitStack() as es2:
            sb2 = lambda name, shape, dt=F32: es2.enter_context(nc.sbuf_tensor("s_" + name, list(shape), dt))
            KT = sb2("KT", [64, NTOK], BF16); QT = sb2("QT", [64, NTOK], BF16); OT = sb2("OT", [64, NTOK], BF16)
            Vh = sb2("Vh", [128, NCH, 64], BF16); Vmh = sb2("Vmh", [16, 64], BF16)
            RTh = sb2("RTh", [128, 7, 128]); RTV = sb2("RTV", [128, NSLOT, 128]); vmask = sb2("vmask", [128, NSLOT, 128])
            ssb = [sb2("ssb%d" % i, [128, 6, 128]) for i in range(3)]
            pt = [sb2("pt%d" % i, [128, 6, 128], BF16) for i in range(3)]
            pm = [sb2("pm%d" % i, [16, 128], BF16) for i in range(3)]
            rl = [sb2("rl0", [64, 128]), sb2("rl1", [64, 128])]
            P.op('sp', lambda h: h.dma_start(out=vmask[:], in_=vmask_d.rearrange("s k q -> k s q")), writes=['vmask'], dma=True)
            psS = [(psA, 'psA'), (psB, 'psB')]
            it = 0
            for hd in range(16):
                P.op('sp', lambda h, hd=hd: h.dma_start(out=KT[:], in_=KTa[hd]), writes=['KT'], dma=True)
                P.op('sp', lambda h, hd=hd: h.dma_start(out=QT[:], in_=QTa[hd]), writes=['QT'], dma=True)
                P.op('sp', lambda h, hd=hd: h.dma_start(out=Vh[:], in_=Va[hd]), writes=['Vh'], dma=True)
                P.op('sp', lambda h, hd=hd: h.dma_start(out=Vmh[:], in_=Vma[hd]), writes=['Vmh'], dma=True)
                P.op('sp', lambda h, hd=hd: h.dma_start(out=RTh[:], in_=rt_d[hd].rearrange("d k q -> k d q")), writes=['RTh'], dma=True)
                for name, dl in cls:
                    s0, _ = cstart[name]
                    for i_, dl_ in enumerate(dl):
                        P.op('pool', lambda h, s0=s0, i_=i_, dl_=dl_: h.tensor_tensor(out=RTV[:, s0 + i_, :], in0=vmask[:, s0 + i_, :],
                                                                                      in1=RTh[:, dl_ + 3, :], op=ALU.add),
                             reads=['vmask', 'RTh'], writes=['RTV'])
                def na_s1(gt, b_, b3, hd=hd):
                    ismeta = (gt == 3 * T)
                    pS, pSn = psS[b_]
                    pX, pXn = banks[4 + b_]
                    nq = 16 if ismeta else 128
                    q0 = MOFF if ismeta else gt * 128
                    if not ismeta:
                        seg, t = gt // T, gt % T
                        s0, dl = na_lookup(seg, t)
                        nb = len(dl)
                        for i_, dl_ in enumerate(dl):
                            gc = gt + dl_
                            P.op('pe', lambda h, pS=pS, i_=i_, gc=gc, q0=q0: h.matmul(
                                pS[:, i_ * 128:(i_ + 1) * 128], lhsT=KT[:, gc * 128:(gc + 1) * 128], rhs=QT[:, q0:q0 + 128], start=True, stop=True),
                                 reads=['KT', 'QT'], writes=[pSn])
                    P.op('pe', lambda h, pS=pS, q0=q0, nq=nq: h.matmul(pS[0:16, 768:768 + nq], lhsT=KT[:, MOFF:MOFF + 16], rhs=QT[:, q0:q0 + nq], start=True, stop=True),
                         reads=['KT', 'QT'], writes=[pSn])
                    P.op('act', lambda h, pS=pS, b3=b3, nq=nq, hd=hd: h.activation(out=pm[b3][:, 0:nq], in_=pS[0:16, 768:768 + nq], func=AF.Exp, bias=mbT[:, hd:hd + 1]),
                         reads=[pSn, 'mbT'], writes=['pm%d' % b3])
                    if not ismeta:
                        P.op('dve', lambda h, pS=pS, b3=b3, nb=nb, s0=s0: h.tensor_tensor(
                            out=ssb[b3][:, 0:nb, :], in0=pS[:, 0:nb * 128].rearrange("p (n q) -> p n q", q=128),
                            in1=RTV[:, s0:s0 + nb, :], op=ALU.add), reads=[pSn, 'RTV', 'pm%d' % b3], writes=['ssb%d' % b3])
                        P.op('act', lambda h, b3=b3, nb=nb: h.activation(out=pt[b3][:, 0:nb, :], in_=ssb[b3][:, 0:nb, :], func=AF.Exp),
                             reads=['ssb%d' % b3], writes=['pt%d' % b3])

                def na_s2(gt, b_, b3, hd=hd):
                    ismeta = (gt == 3 * T)
                    pX, pXn = banks[4 + b_]
                    nq = 16 if ismeta else 128
                    q0 = MOFF if ismeta else gt * 128
                    if not ismeta:
                        seg, t = gt // T, gt % T
                        s0, dl = na_lookup(seg, t)
                        for i_, dl_ in enumerate(dl):
                            gc = gt + dl_
                            P.op('pe', lambda h, pX=pX, i_=i_, gc=gc, b3=b3: h.matmul(
                                pX[0:64, 128:256], lhsT=Vh[:, gc, :], rhs=pt[b3][:, i_, :], start=(i_ == 0), stop=False),
                                 reads=['Vh', 'pt%d' % b3], writes=[pXn])
                    P.op('pe', lambda h, pX=pX, b3=b3, nq=nq, ismeta=ismeta: h.matmul(pX[0:64, 128:128 + nq], lhsT=Vmh[:, :], rhs=pm[b3][:, 0:nq], start=ismeta, stop=True),
                         reads=['Vmh', 'pm%d' % b3], writes=[pXn])
                    if not ismeta:
                        for i_, dl_ in enumerate(dl):
                            P.op('pe', lambda h, pX=pX, i_=i_, b3=b3: h.matmul(
                                pX[0:64, 256:384], lhsT=onesb[:, 0:64], rhs=pt[b3][:, i_, :], start=(i_ == 0), stop=False),
                                 reads=['onesb', 'pt%d' % b3], writes=[pXn])
                    P.op('pe', lambda h, pX=pX, b3=b3, nq=nq, ismeta=ismeta: h.matmul(pX[0:64, 256:256 + nq], lhsT=onesb[0:16, 0:64], rhs=pm[b3][:, 0:nq], start=ismeta, stop=True),
                         reads=['onesb', 'pm%d' % b3], writes=[pXn])
                    P.op('dve', lambda h, pX=pX, b_=b_, nq=nq: h.reciprocal(out=rl[b_][:, 0:nq], in_=pX[0:64, 256:256 + nq]),
                         reads=[pXn], writes=['rl%d' % b_])
                    P.op('dve', lambda h, pX=pX, b_=b_, nq=nq, q0=q0: h.tensor_tensor(out=OT[:, q0:q0 + nq], in0=pX[0:64, 128:128 + nq], in1=rl[b_][:, 0:nq], op=ALU.mult),
                         reads=[pXn, 'rl%d' % b_], writes=['OT'])

                pend = []
                for gt in range(3 * T + 1):
                    b_ = it % 2; b3 = it % 3; it += 1
                    na_s1(gt, b_, b3)
                    pend.append((gt, b_, b3))
                    if len(pend) > 2:
                        na_s2(*pend.pop(0))
                while pend:
                    na_s2(*pend.pop(0))
                P.op('pool', lambda h, hd=hd: h.dma_start(out=OTa[hd // 2, (hd % 2) * 64:(hd % 2) * 64 + 64, :], in_=OT[:]),
                     reads=['OT'], writes=[('OTa', hd)], dma=True)
            P.flush()
        if stop <= 2:
            return nc

        with ExitStack() as es3:
            hT, yT, xn, aT, rstd, tmpf, sg, wbuf = lin_alloc('p3', es3)
            sb3 = lambda name, shape, dt=F32: es3.enter_context(nc.sbuf_tensor("s_" + name, list(shape), dt))
            qk_tm = [sb3("qk_tm0", [128, 1024]), sb3("qk_tm1", [128, 1024])]
            rtmp = [sb3("rtmp%d" % i, [128, 32, 8]) for i in range(4)]
            cst = sb3("cst", [128, 4, 16])
            qkT = sb3("qkT", [64, 32, 512], BF16)
            v_sb = [sb3("v3_sb0", [128, 1024], BF16), sb3("v3_sb1", [128, 1024], BF16)]

            def p3_tile(t0, W):
                ismeta = (W == 16)
                nsub = 1 if ismeta else 4
                npt = 16 if ismeta else 128
                P.op('sp', lambda h, t0=t0, W=W: h.dma_start(out=xn[:, :, 0:W], in_=OTa[:, :, t0:t0 + W].rearrange("c p t -> p c t")),
                     writes=[*XN], dma=True)
                P.op('sp', lambda h, t0=t0, W=W: h.dma_start(out=hT[:, :, 0:W], in_=hS[:, :, t0:t0 + W]), writes=[*HT], dma=True)
                proj_fm(WoA, W)
                postnorm_add(3, W, False, True)
                ffn(1, W, True, True)
                ffn(2, W, True, True)
                P.op('pool', lambda h, t0=t0, W=W: h.dma_start(out=hS[:, :, t0:t0 + W], in_=hT[:, :, 0:W]),
                     reads=[*HT], writes=[('hS', t0)], dma=True)
                norm_to_xn(6 + 2, W, True)
                if ismeta:
                    P.op('sp', lambda h: h.dma_start(out=cst[0:16, 0, :], in_=cs_d[MOFF:MOFF + 16, :]), writes=['cst'], dma=True)
                else:
                    P.op('sp', lambda h, t0=t0: h.dma_start(out=cst[:, :, :], in_=cs_d[t0:t0 + 512, :].rearrange("(s p) c -> p s c", p=128)),
                         writes=['cst'], dma=True)
                for cbp in range(2):
                    wt, wn = wload(WqkB[:, :, cbp * 1024:(cbp + 1) * 1024], 8192)
                    wv = wt.rearrange("p (k n) -> p k n", k=8)
                    for s_ in range(nsub):
                        qt = qk_tm[s_ % 2]; qn = 'qk_tm%d' % (s_ % 2)
                        for cb in range(2):
                            pp, ppn = nbank(0, 4)
                            for kc in range(8):
                                P.op('pe', lambda h, pp=pp, kc=kc, s_=s_, cb=cb, wv=wv: h.matmul(
                                    pp[0:npt, :], lhsT=xn[:, kc, s_ * 128: s_ * 128 + npt], rhs=wv[:, kc, cb * 512:(cb + 1) * 512],
                                    start=(kc == 0), stop=(kc == 7)), reads=[*XN, wn], writes=[ppn])
                            P.op('act', lambda h, pp=pp, cb=cb, qt=qt: h.activation(out=qt[0:npt, cb * 512:(cb + 1) * 512], in_=pp[0:npt, :], func=AF.Copy),
                                 reads=[ppn], writes=[qn])
                        qv = qt[0:npt, :].rearrange("p (m e) -> p m e", e=64)
                        x1 = qv[:, :, 0:8]; x2 = qv[:, :, 8:16]
                        cosb = cst[0:npt, s_, 0:8].unsqueeze(1).to_broadcast([npt, 16, 8])
                        sinb = cst[0:npt, s_, 8:16].unsqueeze(1).to_broadcast([npt, 16, 8])
                        for k_, (a_, b2) in enumerate(((x1, cosb), (x2, sinb), (x2, cosb), (x1, sinb))):
                            P.op('dve', lambda h, k_=k_, a_=a_, b2=b2: h.tensor_tensor(out=rtmp[k_][0:npt, 0:16, :], in0=a_, in1=b2, op=ALU.mult),
                                 reads=[qn, 'cst'], writes=['rtmp%d' % k_])
                        P.op('dve', lambda h, x1=x1: h.tensor_tensor(out=x1, in0=rtmp[0][0:npt, 0:16, :], in1=rtmp[1][0:npt, 0:16, :], op=ALU.subtract),
                             reads=['rtmp0', 'rtmp1'], writes=[qn])
                        P.op('dve', lambda h, x2=x2: h.tensor_tensor(out=x2, in0=rtmp[2][0:npt, 0:16, :], in1=rtmp[3][0:npt, 0:16, :], op=ALU.add),
                             reads=['rtmp2', 'rtmp3'], writes=[qn])
                        for i0 in range(0, 16, 4):
                            pp, ppn = nbank(0, 4)
                            for ii in range(4):
                                i_ = i0 + ii
                                P.op('pe', lambda h, pp=pp, ii=ii, i_=i_, qt=qt: h.transpose(
                                    pp[0:64, ii * 128: ii * 128 + npt], qt[0:npt, i_ * 64:(i_ + 1) * 64], identf[0:npt, 0:npt]),
                                     reads=[qn, 'identf'], writes=[ppn])
                            P.op('act', lambda h, pp=pp, i0=i0, s_=s_, cbp=cbp: h.activation(
                                out=qkT[:, cbp * 16 + i0:cbp * 16 + i0 + 4, s_ * 128: s_ * 128 + npt],
                                in_=pp[0:64, :].rearrange("p (i t) -> p i t", i=4)[:, :, 0:npt],
                                func=AF.Copy, scale=(0.125 if cbp == 0 else 1.0)), reads=[ppn], writes=['qkT'])
                wt, wn = wload(WvB, 8192)
                wv = wt.rearrange("p (k n) -> p k n", k=8)
                for s_ in range(nsub):
                    vt = v_sb[s_ % 2]; vn = 'v3_sb%d' % (s_ % 2)
                    for half in range(2):
                        pp, ppn = nbank(0, 4)
                        for kc in range(8):
                            P.op('pe', lambda h, pp=pp, kc=kc, s_=s_, half=half, wv=wv: h.matmul(
                                pp[0:npt, :], lhsT=xn[:, kc, s_ * 128: s_ * 128 + npt], rhs=wv[:, kc, half * 512:(half + 1) * 512],
                                start=(kc == 0), stop=(kc == 7)), reads=[*XN, wn], writes=[ppn])
                        P.op('dve', lambda h, pp=pp, vt=vt, half=half: h.tensor_copy(out=vt[0:npt, half * 512:(half + 1) * 512], in_=pp[0:npt, :]),
                             reads=[ppn], writes=[vn])
                    if ismeta:
                        P.op('pool', lambda h, vt=vt: h.dma_start(out=Vmb.rearrange("h m e -> m h e"),
                                                                  in_=vt[0:16, :].rearrange("m (h e) -> m h e", e=128)),
                             reads=[vn], writes=['Vmb'], dma=True)
                    else:
                        ch = t0 // 128 + s_
                        P.op('pool', lambda h, vt=vt, ch=ch: h.dma_start(out=Vb[:, :, ch, :].rearrange("h p e -> p h e"),
                                                                         in_=vt[:, :].rearrange("p (h e) -> p h e", e=128)),
                             reads=[vn], writes=[('Vb', ch)], dma=True)
                P.op('pool', lambda h, t0=t0, W=W: h.dma_start(out=QTb[:, :, :, t0:t0 + W].rearrange("h m e t -> e (h m) t"), in_=qkT[:, 0:16, 0:W]),
                     reads=['qkT'], writes=[('QTb', t0)], dma=True)
                P.op('pool', lambda h, t0=t0, W=W: h.dma_start(out=KTb[:, :, :, t0:t0 + W].rearrange("h m e t -> e (h m) t"), in_=qkT[:, 16:32, 0:W]),
                     reads=['qkT'], writes=[('KTb', t0)], dma=True)
            for (t0_, W_) in tiles:
                p3_tile(t0_, W_)
            P.flush()
        if stop <= 3:
            return nc

        with ExitStack() as es4:
            hT, yT, xn, aT, rstd, tmpf, sg, wbuf = lin_alloc('p4', es4)
            sb4 = lambda name, shape, dt=F32: es4.enter_context(nc.sbuf_tensor("s_" + name, list(shape), dt))
            NKC = KB // 128
            Kb = [sb4("Kb%d" % i, [128, KB], BF16) for i in range(2)]
            Vp = [sb4("Vp%d" % i, [128, NKC, 130], BF16) for i in range(2)]
            Km = sb4("Km", [128, 8, 16], BF16); Vpm = sb4("Vpm", [16, 8, 130], BF16)
            Qg = [sb4("Qg%d" % i, [128, 512], BF16) for i in range(2)]
            PT = [sb4("PT%d" % i, [128, 2, 512], BF16) for i in range(3)]
            Otm = sb4("Otm", [128, 4, 1024])
            t0s = sb4("t0s", [128, 128]); dsb = sb4("dsb", [128, 128]); dsq = sb4("dsq", [128, 128])
            rec = sb4("rec", [128, 4]); ss = sb4("ss", [128, 2])
            ot = [sb4("ot0", [128, 1024]), sb4("ot1", [128, 1024])]
            for i in range(2):
                P.op('pool', lambda h, i=i: h.memset(Vp[i][:], 1.0), writes=['Vp%d' % i])
            P.op('pool', lambda h: h.memset(Vpm[:], 1.0), writes=['Vpm'])
            P.op('sp', lambda h: h.dma_start(out=Km[:], in_=KTb[:, :, :, MOFF:MOFF + 16].rearrange("h m e t -> (m e) h t")), writes=['Km'], dma=True)
            P.op('sp', lambda h: h.dma_start(out=Vpm[:, :, 0:128], in_=Vmb.rearrange("h m e -> m h e")), reads=['Vpm'], writes=['Vpm'], dma=True)
            psS = [(psA, ['b0', 'b1']), (psB, ['b2', 'b3'])]
            PSO = ['b4', 'b5', 'b6', 'b7']
            kvc = 0; sc = 0
            NG = NT // 512
            GPS = SEG // 512
            for g in range(NG):
                seg = g // GPS
                q0 = g * 512
                ksegs = [0, 1] if seg < 2 else [2]
                st = {'kvc': kvc, 'sc': sc}

                def gen_units():
                    for hd in range(8):
                        qb = Qg[hd % 2]; qbn = 'Qg%d' % (hd % 2)
                        blocks = [(ks, kb0) for ks in ksegs for kb0 in range(0, SEG, KB)] + [None]
                        first = True
                        for u in blocks:
                            nch = NKC if u is not None else 1
                            for j in range(nch):
                                yield dict(hd=hd, qb=qb, qbn=qbn, u=u, j=j, first=first, last=(u is None), newhead=(first), newblk=(j == 0))
                                first = False

                def prep(d):
                    hd = d['hd']
                    if d['newhead']:
                        P.op('sp', lambda h, qb=d['qb'], hd=hd, q0=q0: h.dma_start(out=qb[:], in_=QTb[hd, :, :, q0:q0 + 512].rearrange("m e t -> (m e) t")),
                             writes=[d['qbn']], dma=True)
                    if d['u'] is not None:
                        ks, kb0 = d['u']
                        if d['newblk']:
                            i = st['kvc'] % 2; st['kvc'] += 1
                            st['i'] = i
                            k0 = ks * SEG + kb0
                            P.op('sp', lambda h, i=i, hd=hd, k0=k0: h.dma_start(out=Kb[i][:], in_=KTb[hd, :, :, k0:k0 + KB].rearrange("m e t -> (m e) t")),
                                 writes=['Kb%d' % i], dma=True)
                            P.op('sp', lambda h, i=i, hd=hd, k0=k0: h.dma_start(out=Vp[i][:, :, 0:128], in_=Vb[hd, :, k0 // 128:k0 // 128 + NKC, :]),
                                 reads=['Vp%d' % i], writes=['Vp%d' % i], dma=True)
                        i = st['i']; j = d['j']
                        d['bias'] = zeroc if ks == seg else segm
                        d['kap'] = Kb[i][:, j * 128:(j + 1) * 128]; d['vap'] = Vp[i][:, j, 0:129]; d['nk'] = 128
                        d['rd'] = ['Kb%d' % i, 'Vp%d' % i]
                    else:
                        d['bias'] = zeroc
                        d['kap'] = Km[:, hd, :]; d['vap'] = Vpm[:, hd, 0:129]; d['nk'] = 16; d['rd'] = ['Km', 'Vpm']
                    d['b'] = st['sc'] % 2; d['pb'] = st['sc'] % 3; st['sc'] += 1

                def emit_S(d):
                    b_ = d['b']; pS, pSn = psS[b_]; nk = d['nk']; kap = d['kap']; qb = d['qb']
                    for m in range(2):
                        P.op('pe', lambda h, pS=pS, m=m, kap=kap, qb=qb, nk=nk: h.matmul(
                            pS[0:nk, m * 512:(m + 1) * 512], lhsT=kap[m * 64:(m + 1) * 64, :], rhs=qb[m * 64:(m + 1) * 64, :], start=True, stop=True,
                            tile_position=(m * 64, 0)),
                             reads=[d['rd'][0], d['qbn']], writes=pSn)
                    pb = d['pb']
                    P.op('act', lambda h, pS=pS, pb=pb, nk=nk, bias=d['bias']: h.activation(
                        out=PT[pb][0:nk].rearrange("p m q -> p (m q)"), in_=pS[0:nk, :], func=AF.Exp, bias=bias[0:nk, :]),
                         reads=pSn + ['segm', 'zeroc'], writes=['PT%d' % pb])

                def emit_AV(d):
                    b_ = d['pb']; nk = d['nk']; vap = d['vap']
                    for qs in range(4):
                        for m in range(2):
                            a_ = qs * 2 + m
                            P.op('pe', lambda h, a_=a_, qs=qs, m=m, b_=b_, vap=vap, nk=nk, first=d['first'], last=d['last']: h.matmul(
                                psO[:, a_ * 256: a_ * 256 + 129], lhsT=PT[b_][0:nk, m, qs * 128:(qs + 1) * 128], rhs=vap[0:nk, :],
                                start=(first and a_ % 2 == 0), stop=last), reads=['PT%d' % b_, d['rd'][1]], writes=PSO)
                    if d['last']:
                        combine(d['hd'])

                def combine(hd):
                    for qs in range(4):
                        a0 = psO[:, (2 * qs) * 256:(2 * qs) * 256 + 129]; a1 = psO[:, (2 * qs + 1) * 256:(2 * qs + 1) * 256 + 129]
                        P.op('dve', lambda h, a0=a0: h.reciprocal(out=rec[:, 0:1], in_=a0[:, 128:129]), reads=PSO, writes=['rec'])
                        P.op('dve', lambda h, a1=a1: h.reciprocal(out=rec[:, 1:2], in_=a1[:, 128:129]), reads=PSO, writes=['rec'])
                        P.op('dve', lambda h: h.tensor_tensor(out=rec[:, 2:3], in0=rec[:, 1:2], in1=neglam[:], op=ALU.mult),
                             reads=['rec', 'neglam'], writes=['rec'])
                        P.op('dve', lambda h, a0=a0: h.tensor_scalar(out=t0s[:], in0=a0[:, 0:128], scalar1=rec[:, 0:1], scalar2=None, op0=ALU.mult),
                             reads=PSO + ['rec'], writes=['t0s'])
                        P.op('dve', lambda h, a1=a1: h.scalar_tensor_tensor(out=dsb[:], in0=a1[:, 0:128], scalar=rec[:, 2:3], in1=t0s[:],
                                                                            op0=ALU.mult, op1=ALU.add),
                             reads=PSO + ['rec', 't0s'], writes=['dsb'])
                        P.op('dve', lambda h: h.memset(ss[:, 0:1], 0.0), writes=['ss'])
                        P.op('dve', lambda h: h.scalar_tensor_tensor(out=dsq[:], in0=dsb[:], scalar=1.0, in1=dsb[:], op0=ALU.mult, op1=ALU.mult,
                                                                     accum_out=ss[:, 0:1]), reads=['dsb'], writes=['dsq', 'ss'])
                        P.op('act', lambda h: h.activation(out=ss[:, 1:2], in_=ss[:, 0:1], func=AF.Ln, bias=epsc[:], scale=1.0 / 128),
                             reads=['ss', 'epsc'], writes=['ss'])
                        P.op('act', lambda h: h.activation(out=ss[:, 1:2], in_=ss[:, 1:2], func=AF.Exp, scale=-0.5), reads=['ss'], writes=['ss'])
                        P.op('dve', lambda h, qs=qs, hd=hd: h.scalar_tensor_tensor(out=Otm[:, qs, hd * 128:(hd + 1) * 128], in0=dsb[:], scalar=ss[:, 1:2],
                                                                                 in1=subg[:], op0=ALU.mult, op1=ALU.mult),
                             reads=['dsb', 'ss', 'subg'], writes=['Otm'])

                pending = []
                for d in gen_units():
                    prep(d)
                    emit_S(d)
                    pending.append(d)
                    if len(pending) > 2:
                        emit_AV(pending.pop(0))
                while pending:
                    emit_AV(pending.pop(0))
                kvc = st['kvc']; sc = st['sc']
                if dbg:
                    for qs in range(4):
                        P.op('pool', lambda h, qs=qs, q0=q0: h.dma_start(out=dbgO[q0 + qs * 128:q0 + (qs + 1) * 128, :], in_=Otm[:, qs, :]),
                             reads=['Otm'], writes=[('dbgO', q0, qs)], dma=True)
                for qs in range(4):
                    for c0 in (0, 4):
                        pp, ppn = nbank(0, 4)
                        for cc in range(4):
                            c = c0 + cc
                            P.op('pe', lambda h, pp=pp, cc=cc, c=c, qs=qs: h.transpose(
                                pp[:, cc * 128:(cc + 1) * 128], Otm[:, qs, c * 128:(c + 1) * 128], identf[:]),
                                 reads=['Otm', 'identf'], writes=[ppn])
                        P.op('dve', lambda h, pp=pp, c0=c0, qs=qs: h.tensor_copy(
                            out=xn[:, c0:c0 + 4, qs * 128:(qs + 1) * 128], in_=pp.rearrange("p (c t) -> p c t", c=4)), reads=[ppn], writes=[*XN])
                P.op('sp', lambda h, q0=q0: h.dma_start(out=hT[:], in_=hS[:, :, q0:q0 + 512]), writes=[*HT], dma=True)
                proj_fm(WoB, 512)
                postnorm_add(6 + 3, 512, False, True)
                ffn(3, 512, True, False)
                for qs in range(4):
                    o_ = ot[qs % 2]; on_ = 'ot%d' % (qs % 2)
                    for c0 in (0, 4):
                        pp, ppn = nbank(0, 4)
                        for cc in range(4):
                            c = c0 + cc
                            P.op('pe', lambda h, pp=pp, cc=cc, c=c, qs=qs: h.transpose(
                                pp[:, cc * 128:(cc + 1) * 128], hT[:, c, qs * 128:(qs + 1) * 128], identf[:]),
                                 reads=[*HT, 'identf'], writes=[ppn])
                        P.op('act', lambda h, pp=pp, c0=c0, o_=o_: h.activation(out=o_[:, c0 * 128:(c0 + 4) * 128], in_=pp[:, :], func=AF.Copy),
                             reads=[ppn], writes=[on_])
                    P.op('pool', lambda h, o_=o_, q0=q0, qs=qs: h.dma_start(out=y[q0 + qs * 128: q0 + (qs + 1) * 128, :], in_=o_[:]),
                         reads=[on_], writes=[('y', q0, qs)], dma=True)
            P.flush()
    return nc


def host_inputs(SEG, segsA, typ, common):
    m = dict(common)
    m["xs"] = np.ascontiguousarray(np.concatenate(segsA, axis=0), dtype=np.float32)
    m["vmask"] = host_vmask(SEG, typ)
    m["cs"] = host_cs(SEG, typ)
    m["segm"] = np.full((128, 1), 0.0 if typ == 2 else NEG, np.float32)
    return m


def host_common(meta_tokens, norm_g, w_ffn_in, w_ffn_out, w_qkv_a, w_o_a, rpb_a, meta_bias_a, w_qkv_b, w_o_b, lambda_b, subln_b):
    f = lambda a: np.ascontiguousarray(np.asarray(a, dtype=np.float32))
    g = f(norm_g).reshape(12, 8, 128)
    return {
        "meta": f(meta_tokens),
        "gcol": np.ascontiguousarray(g.transpose(2, 0, 1).reshape(128, 96)),
        "w_in": f(w_ffn_in).reshape(4, D, 2 * DFF), "w_out": f(w_ffn_out).reshape(4, DFF, D),
        "wqkv_a": f(w_qkv_a)[0], "wo_a": f(w_o_a)[0], "wqkv_b": f(w_qkv_b)[0], "wo_b": f(w_o_b)[0],
        "rt": host_rt(f(rpb_a)[0]),
        "mbT": np.ascontiguousarray(f(meta_bias_a)[0].T),
        "lam": f(lambda_b)[0].reshape(1, 256), "subg": f(subln_b)[0].reshape(1, 128),
        "ident": np.eye(128, dtype=np.float32),
    }


def run(SEG, x_prompt, x_sample, params):
    lambda_init = 0.8 - 0.6 * math.exp(-0.3 * 1)
    common = host_common(**params)
    nc = build(SEG, lambda_init)
    xp = np.asarray(x_prompt, dtype=np.float32); xsamp = np.asarray(x_sample, dtype=np.float32)
    in_maps = []
    for c in range(4):
        in_maps.append(host_inputs(SEG, [xsamp[c, :SEG], xsamp[c, SEG:], xp[c]], 2, common))
    for c in range(4):
        in_maps.append(host_inputs(SEG, [xp[4 + 3 * c], xp[5 + 3 * c], xp[6 + 3 * c]], 1, common))
    res = run_bass_kernel_spmd(nc, in_maps, core_ids=list(range(8)))
    yp = np.zeros_like(xp); ysamp = np.zeros_like(xsamp)
    for c in range(4):
        yc = res.results[c]["y"]
        ysamp[c, :SEG] = yc[0:SEG]; ysamp[c, SEG:] = yc[SEG:2 * SEG]; yp[c] = yc[2 * SEG:]
    for c in range(4):
        yc = res.results[4 + c]["y"]
        for k in range(3):
            yp[4 + 3 * c + k] = yc[k * SEG:(k + 1) * SEG]
    return yp, ysamp


def kernel(x_prompt, x_sample, meta_tokens, norm_g, w_ffn_in, w_ffn_out, w_qkv_a, w_o_a,
           rpb_a, meta_bias_a, w_qkv_b, w_o_b, lambda_b, subln_b):
    params = dict(meta_tokens=meta_tokens, norm_g=norm_g, w_ffn_in=w_ffn_in, w_ffn_out=w_ffn_out,
                  w_qkv_a=w_qkv_a, w_o_a=w_o_a, rpb_a=rpb_a, meta_bias_a=meta_bias_a,
                  w_qkv_b=w_qkv_b, w_o_b=w_o_b, lambda_b=lambda_b, subln_b=subln_b)
    yp, ysamp = run(4096, x_prompt, x_sample, params)
    return (yp, ysamp)
```

```python
import math
from contextlib import ExitStack
import numpy as np
import concourse.bass as bass
import concourse.mybir as mybir
from concourse.bass_utils import run_bass_kernel_spmd

F32 = mybir.dt.float32
BF16 = mybir.dt.bfloat16
AF = mybir.ActivationFunctionType
ALU = mybir.AluOpType
ENGS = ['pe', 'act', 'dve', 'pool', 'sp']
DMA_SLOTS = 8
D = 1024
DFF = 2816
NJ = 22
NEG = -30000.0
EPS = 1e-6


class Op:
    __slots__ = ('eng', 'fn', 'deps', 'needed', 'dma', 'val', 'slot', 'prev')

    def __init__(s, eng, fn, dma):
        s.eng = eng; s.fn = fn; s.dma = dma; s.deps = set(); s.needed = False
        s.val = 0; s.slot = None; s.prev = 0


class Prog:
    def __init__(s, nc, es):
        s.nc = nc
        s.streams = {e: [] for e in ENGS}
        s.cells = {}
        s.ndma = {e: 0 for e in ENGS}
        s.count = {e: 0 for e in ENGS}
        s.seen = {e: {} for e in ENGS}
        s.csem = {e: es.enter_context(nc.semaphore('c_' + e)) for e in ENGS}
        s.dsem = {e: [es.enter_context(nc.semaphore('d_%s%d' % (e, i))) for i in range(DMA_SLOTS)]
                  for e in ('sp', 'pool', 'act')}
        s.nblk = 0

    def op(s, eng, fn, reads=(), writes=(), dma=False):
        o = Op(eng, fn, dma)
        deps = set()
        for c in reads:
            st = s.cells.get(c)
            if st is not None and st[0] is not None:
                deps.add(st[0])
        for c in writes:
            st = s.cells.get(c)
            if st is not None:
                if st[0] is not None:
                    deps.add(st[0])
                deps.update(st[1])
        for d in deps:
            if d.eng == 'pe' and eng == 'pe' and not d.dma and not dma:
                continue
            o.deps.add(d); d.needed = True
        for c in reads:
            st = s.cells.setdefault(c, [None, []])
            st[1].append(o)
        for c in writes:
            s.cells[c] = [o, []]
        if dma:
            n = s.ndma[eng]; s.ndma[eng] = n + 1
            o.slot = n % DMA_SLOTS; o.val = 16 * (n // DMA_SLOTS + 1); o.prev = 16 * (n // DMA_SLOTS)
        s.streams[eng].append(o)
        return o

    def flush(s):
        nc = s.nc
        handles = {'pe': 'tensor', 'act': 'scalar', 'dve': 'vector', 'pool': 'gpsimd', 'sp': 'sync'}
        for e in ENGS:
            c = s.count[e]
            for o in s.streams[e]:
                if not o.dma and o.needed:
                    c += 1; o.val = c
            s.count[e] = c
        s.nblk += 1
        with nc.Block() as block:
            def mk(e):
                def body(h):
                    seen = s.seen[e]

                    def wait(key, sem, val):
                        if seen.get(key, 0) < val:
                            h.wait_ge(sem, val); seen[key] = val
                    for o in s.streams[e]:
                        for d in o.deps:
                            if d.dma:
                                wait(('d', d.eng, d.slot), s.dsem[d.eng][d.slot], d.val)
                            else:
                                wait(('c', d.eng), s.csem[d.eng], d.val)
                        if o.dma and o.prev > 0:
                            wait(('d', e, o.slot), s.dsem[e][o.slot], o.prev)
                        ins = o.fn(h)
                        if o.dma:
                            ins.then_inc(s.dsem[e][o.slot], 16)
                        elif o.needed:
                            ins.then_inc(s.csem[e], 1)
                    n = s.ndma[e]
                    if n > 0:
                        for sl in range(min(n, DMA_SLOTS)):
                            last = ((n - 1 - sl) // DMA_SLOTS) + 1
                            wait(('d', e, sl), s.dsem[e][sl], 16 * last)
                return body
            for e in ENGS:
                getattr(block, handles[e])(mk(e))
        s.streams = {e: [] for e in ENGS}
        s.cells = {}


def na_classes(T):
    cls = [('INT', [-2, -1, 0, 1, 2]), ('TOP0', [0, 1, 2, 3]), ('TOP1', [-1, 0, 1, 2]),
           ('BOT0', [-2, -1, 0, 1]), ('BOT1', [-3, -2, -1, 0]),
           ('SPA0', [-2, -1, 0, 1, 2]), ('SPA1', [-3, -2, -1, 0, 1, 2]),
           ('SPB0', [-2, -1, 0, 1, 2, 3]), ('SPB1', [-2, -1, 0, 1, 2])]
    start = {}
    n = 0
    for name, dl in cls:
        start[name] = (n, dl); n += len(dl)

    def lookup(seg, t):
        if seg == 0 and t == T - 2: return start['SPA0']
        if seg == 0 and t == T - 1: return start['SPA1']
        if seg == 1 and t == 0: return start['SPB0']
        if seg == 1 and t == 1: return start['SPB1']
        if t == 0: return start['TOP0']
        if t == 1: return start['TOP1']
        if t == T - 2: return start['BOT0']
        if t == T - 1: return start['BOT1']
        return start['INT']
    rep = {'INT': (2, 2), 'TOP0': (2, 0), 'TOP1': (2, 1), 'BOT0': (2, T - 2), 'BOT1': (2, T - 1),
           'SPA0': (0, T - 2), 'SPA1': (0, T - 1), 'SPB0': (1, 0), 'SPB1': (1, 1)}
    return cls, start, lookup, rep, n


def host_vmask(SEG, typ):
    R = SEG // 64; T = R // 2
    cls, start, lookup, rep, n = na_classes(T)
    out = np.zeros((n, 128, 128), np.float32)
    qc = np.arange(64)
    cs = np.clip(qc - 8, 0, 48)
    kc = np.arange(64)
    colok = (kc[:, None] >= cs[None, :]) & (kc[:, None] < cs[None, :] + 16)
    for name, dl in cls:
        seg, t = rep[name]
        s0, _ = start[name]
        for i, dl_ in enumerate(dl):
            blk = np.full((2, 64, 2, 64), NEG, np.float32)
            for b in range(2):
                qg = seg * R + 2 * t + b
                if typ == 2 and qg < 2 * R:
                    sq0, nr = 0, 2 * R
                else:
                    sq0, nr = (qg // R) * R, R
                r = qg - sq0
                r0 = min(max(r - 4, 0), nr - 8)
                for a in range(2):
                    kg = seg * R + 2 * (t + dl_) + a
                    if sq0 + r0 <= kg <= sq0 + r0 + 7:
                        blk[a, :, b, :] = np.where(colok, 0.0, NEG)
            out[s0 + i] = blk.reshape(128, 128)
    return out


def host_rt(rpb):
    rt = np.zeros((16, 7, 2, 64, 2, 64), np.float32)
    kc = np.arange(64)[:, None]; qc = np.arange(64)[None, :]
    dcol = kc - qc + 15
    ok = (dcol >= 0) & (dcol <= 30)
    dcc = np.clip(dcol, 0, 30)
    for di, Dl in enumerate(range(-3, 4)):
        for a in range(2):
            for b in range(2):
                dr = 2 * Dl + a - b + 7
                if 0 <= dr <= 14:
                    rt[:, di, a, :, b, :] = np.where(ok[None], rpb[:, dr][:, dcc], 0.0)
    return rt.reshape(16, 7, 128, 128)


def host_cs(SEG, typ):
    NTOK = 3 * SEG + 16
    pos = np.zeros(NTOK, np.float32)
    ar = np.arange(SEG, dtype=np.float32)
    pos[0:SEG] = 16 + ar
    pos[SEG:2 * SEG] = (16 + SEG + ar) if typ == 2 else (16 + ar)
    pos[2 * SEG:3 * SEG] = 16 + ar
    pos[3 * SEG:] = np.arange(16, dtype=np.float32)
    inv = (np.float32(500000.0) ** (-np.arange(0, 16, 2, dtype=np.float32) / np.float32(16))).astype(np.float32)
    ang = (pos[:, None] * inv[None, :]).astype(np.float32)
    return np.concatenate([np.cos(ang), np.sin(ang)], axis=1).astype(np.float32)


def build(SEG, lambda_init, stop=9, dbg=False):
    NT = 3 * SEG
    NTOK = NT + 16
    MOFF = NT
    NCH = NT // 128
    R = SEG // 64; T = R // 2
    KB = min(2048, SEG)
    cls, cstart, na_lookup, _, NSLOT = na_classes(T)

    nc = bass.Bass("TRN2", target_bir_lowering=False)

    def din(name, shape, dt=F32):
        return nc.dram_tensor(name, list(shape), dt, kind="ExternalInput").ap()

    def dscr(name, shape, dt=BF16):
        return nc.dram_tensor(name, list(shape), dt, kind=("ExternalOutput" if dbg else "Internal")).ap()
    xs = din("xs", [NT, D]); meta = din("meta", [16, D]); gcol_d = din("gcol", [128, 96])
    w_in = din("w_in", [4, D, 2 * DFF]); w_out = din("w_out", [4, DFF, D])
    wqkv_a = din("wqkv_a", [D, 3 * D]); wo_a = din("wo_a", [D, D])
    wqkv_b = din("wqkv_b", [D, 3 * D]); wo_b = din("wo_b", [D, D])
    rt_d = din("rt", [16, 7, 128, 128]); vmask_d = din("vmask", [NSLOT, 128, 128])
    mbT_d = din("mbT", [16, 16]); lam_d = din("lam", [1, 256]); subg_d = din("subg", [1, 128])
    cs_d = din("cs", [NTOK, 16]); segm_d = din("segm", [128, 1]); ident_d = din("ident", [128, 128])
    y = nc.dram_tensor("y", [NT, D], F32, kind="ExternalOutput").ap()
    dbgO = nc.dram_tensor("dbgO", [NT, D], F32, kind="ExternalOutput").ap() if dbg else None

    WinS = dscr("WinS", [4, NJ, 128, 2, 8, 128])
    WoutS = dscr("WoutS", [4, 8, 128, NJ, 128])
    WqkA = dscr("WqkA", [16, 128, 8, 128])
    WvA = dscr("WvA", [128, 8, 1024])
    WoA = dscr("WoA", [8, 128, 8, 128])
    WqkB = dscr("WqkB", [128, 8, 2048])
    WvB = dscr("WvB", [128, 8, 1024])
    WoB = dscr("WoB", [8, 128, 8, 128])
    hS = dscr("hS", [128, 8, NTOK], F32)
    QTa = dscr("QTa", [16, 64, NTOK]); KTa = dscr("KTa", [16, 64, NTOK])
    Va = dscr("Va", [16, 128, NCH, 64]); Vma = dscr("Vma", [16, 16, 64])
    OTa = dscr("OTa", [8, 128, NTOK])
    QTb = dscr("QTb", [8, 2, 64, NTOK]); KTb = dscr("KTb", [8, 2, 64, NTOK])
    Vb = dscr("Vb", [8, 128, NCH, 128]); Vmb = dscr("Vmb", [8, 16, 128])

    with ExitStack() as es:
        P = Prog(nc, es)
        sb = lambda name, shape, dt=F32: es.enter_context(nc.sbuf_tensor("s_" + name, list(shape), dt))
        psA = es.enter_context(nc.psum_tensor("psA", [128, 1024], F32))
        psB = es.enter_context(nc.psum_tensor("psB", [128, 1024], F32))
        psO = es.enter_context(nc.psum_tensor("psO", [128, 2048], F32))
        banks = [(psA[:, 0:512], 'b0'), (psA[:, 512:1024], 'b1'), (psB[:, 0:512], 'b2'), (psB[:, 512:1024], 'b3'),
                 (psO[:, 0:512], 'b4'), (psO[:, 512:1024], 'b5'), (psO[:, 1024:1536], 'b6'), (psO[:, 1536:2048], 'b7')]
        identf = sb("identf", [128, 128]); onesb = sb("onesb", [128, 128], BF16)
        gcol = sb("gcol", [128, 96]); g05 = sb("g05", [128, 96])
        epsc = sb("epsc", [128, 1]); zeroc = sb("zeroc", [128, 1]); segm = sb("segm", [128, 1])
        mbT = sb("mbT", [16, 16])
        lamb = sb("lamb", [128, 256]); lamt = sb("lamt", [128, 4]); neglam = sb("neglam", [128, 1])
        subg = sb("subg", [128, 128])
        NWBS = {'p1': 3, 'p3': 3, 'p4': 3}

        def lin_alloc(tag, stack):
            wctr[0] = 0
            a = lambda name, shape, dt=F32: stack.enter_context(nc.sbuf_tensor("s_%s_%s" % (tag, name), list(shape), dt))
            return (a("hT", [128, 8, 512]), a("yT", [128, 8, 512]), a("xn", [128, 8, 512], BF16), a("aT", [128, NJ, 512], BF16),
                    a("rstd", [128, 512]), a("tmpf", [128, 512]), [a("sg0", [128, 512]), a("sg1", [128, 512])],
                    [a("wbuf%d" % i, [128, 8192], BF16) for i in range(NWBS[tag])])
        hT = yT = xn = aT = rstd = tmpf = sg = wbuf = None
        wctr = [0]

        def wload(src_ap, n_per_part):
            i = wctr[0] % len(wbuf); wctr[0] += 1
            dst = wbuf[i][:, 0:n_per_part]
            if len(src_ap.shape) == 3:
                dstv = dst.rearrange("p (a b) -> p a b", a=src_ap.shape[1])
            else:
                dstv = dst
            P.op('sp', lambda h: h.dma_start(out=dstv, in_=src_ap), writes=['wbuf%d' % i], dma=True)
            return dst, 'wbuf%d' % i
        bctr = [0]

        def nbank(lo=0, hi=4):
            i = lo + bctr[0] % (hi - lo); bctr[0] += 1
            return banks[i]

        P.op('sp', lambda h: h.dma_start(out=identf[:], in_=ident_d), writes=['identf'], dma=True)
        P.op('sp', lambda h: h.dma_start(out=gcol[:], in_=gcol_d), writes=['gcol'], dma=True)
        P.op('sp', lambda h: h.dma_start(out=segm[:], in_=segm_d), writes=['segm'], dma=True)
        P.op('sp', lambda h: h.dma_start(out=mbT[:], in_=mbT_d), writes=['mbT'], dma=True)
        P.op('sp', lambda h: h.dma_start(out=lamb[:], in_=lam_d.partition_broadcast(128)), writes=['lamb'], dma=True)
        P.op('sp', lambda h: h.dma_start(out=subg[:], in_=subg_d.partition_broadcast(128)), writes=['subg'], dma=True)
        P.op('dve', lambda h: h.memset(onesb[:], 1.0), writes=['onesb'])
        P.op('dve', lambda h: h.memset(epsc[:], EPS), writes=['epsc'])
        P.op('dve', lambda h: h.memset(zeroc[:], 0.0), writes=['zeroc'])
        P.op('dve', lambda h: h.tensor_scalar(out=g05[:], in0=gcol[:], scalar1=0.5, scalar2=None, op0=ALU.mult),
             reads=['gcol'], writes=['g05'])
        P.op('dve', lambda h: h.tensor_tensor(out=lamb[:, 0:64], in0=lamb[:, 0:64], in1=lamb[:, 64:128], op=ALU.mult),
             reads=['lamb'], writes=['lamb'])
        P.op('dve', lambda h: h.tensor_tensor(out=lamb[:, 128:192], in0=lamb[:, 128:192], in1=lamb[:, 192:256], op=ALU.mult),
             reads=['lamb'], writes=['lamb'])
        P.op('dve', lambda h: h.tensor_reduce(out=lamt[:, 0:1], in_=lamb[:, 0:64], axis=mybir.AxisListType.X, op=ALU.add),
             reads=['lamb'], writes=['lamt'])
        P.op('dve', lambda h: h.tensor_reduce(out=lamt[:, 1:2], in_=lamb[:, 128:192], axis=mybir.AxisListType.X, op=ALU.add),
             reads=['lamb'], writes=['lamt'])
        P.op('act', lambda h: h.activation(out=lamt[:, 2:4], in_=lamt[:, 0:2], func=AF.Exp), reads=['lamt'], writes=['lamt'])
        P.op('dve', lambda h: h.tensor_tensor(out=neglam[:], in0=lamt[:, 3:4], in1=lamt[:, 2:3], op=ALU.subtract),
             reads=['lamt'], writes=['neglam'])
        P.op('dve', lambda h: h.tensor_scalar(out=neglam[:], in0=neglam[:], scalar1=-float(lambda_init), scalar2=None, op0=ALU.add),
             reads=['neglam'], writes=['neglam'])
        P.op('dve', lambda h: h.tensor_scalar(out=subg[:], in0=subg[:], scalar1=float(1.0 - lambda_init), scalar2=None, op0=ALU.mult),
             reads=['subg'], writes=['subg'])

        def cast(dst, src, name):
            P.op('pool', lambda h: h.dma_start(out=dst, in_=src), writes=[name], dma=True)
        for f4 in range(4):
            for gu in range(2):
                for kc in range(8):
                    cast(WinS[f4, :, :, gu, kc, :].rearrange("j p f -> p j f"),
                         w_in[f4, kc * 128:(kc + 1) * 128, gu * DFF:(gu + 1) * DFF].rearrange("p (j f) -> p j f", f=128), 'WinS')
            for dc in range(8):
                cast(WoutS[f4, dc].rearrange("p fc d -> p fc d"),
                     w_out[f4, :, dc * 128:(dc + 1) * 128].rearrange("(fc p) d -> p fc d", p=128), 'WoutS')
        for kc in range(8):
            rows = slice(kc * 128, (kc + 1) * 128)
            cast(WqkA[:, :, kc, :].rearrange("h p e -> p h e"), wqkv_a[rows, 0:2048].rearrange("p (h e) -> p h e", e=128), 'WqkA')
            cast(WvA[:, kc, :], wqkv_a[rows, 2048:3072], 'WvA')
            cast(WqkB[:, kc, :], wqkv_b[rows, 0:2048], 'WqkB')
            cast(WvB[:, kc, :], wqkv_b[rows, 2048:3072], 'WvB')
            cast(WoA[:, :, kc, :].rearrange("dc p d -> p dc d"), wo_a[rows, :].rearrange("p (dc d) -> p dc d", d=128), 'WoA')
            cast(WoB[:, :, kc, :].rearrange("dc p d -> p dc d"), wo_b[rows, :].rearrange("p (dc d) -> p dc d", d=128), 'WoB')
        P.flush()

        XN = [('xn', c) for c in range(8)]; HT = [('hT', c) for c in range(8)]; YT = [('yT', c) for c in range(8)]

        def rstd_from_sumsq(W):
            psn = banks[7][0][:, 0:W]
            P.op('act', lambda h: h.activation(out=rstd[:, 0:W], in_=psn, func=AF.Ln, bias=epsc[:], scale=1.0 / 1024.0),
                 reads=['b7', 'epsc'], writes=['rstd'])
            P.op('act', lambda h: h.activation(out=rstd[:, 0:W], in_=rstd[:, 0:W], func=AF.Exp, scale=-0.5),
                 reads=['rstd'], writes=['rstd'])

        def sq_chunk(src_ap, src_cells, dst_ap, dst_cells, W):
            P.op('act', lambda h: h.activation(out=dst_ap, in_=src_ap, func=AF.Square), reads=src_cells, writes=dst_cells)

        def ones_mm(sq_ap, sq_cells, c, W):
            psn = banks[7][0][:, 0:W]
            P.op('pe', lambda h: h.matmul(psn, lhsT=onesb[:], rhs=sq_ap, start=(c == 0), stop=(c == 7)),
                 reads=sq_cells + ['onesb'], writes=['b7'])

        def norm_to_xn(gi, W, presq=False):
            if not presq:
                for c in range(8):
                    sq_chunk(hT[:, c, 0:W], [('hT', c)], xn[:, c, 0:W], [('xn', c)], W)
                    ones_mm(xn[:, c, 0:W], [('xn', c)], c, W)
            rstd_from_sumsq(W)
            for c in range(8):
                P.op('dve', lambda h, c=c: h.scalar_tensor_tensor(out=xn[:, c, 0:W], in0=hT[:, c, 0:W],
                                                                  scalar=gcol[:, gi * 8 + c:gi * 8 + c + 1], in1=rstd[:, 0:W],
                                                                  op0=ALU.mult, op1=ALU.mult),
                     reads=[('hT', c), 'rstd', 'gcol'], writes=[('xn', c)])

        def postnorm_add(gi, W, half, next_norm):
            gt = g05 if half else gcol
            rstd_from_sumsq(W)
            for c in range(8):
                P.op('dve', lambda h, c=c: h.scalar_tensor_tensor(out=tmpf[:, 0:W], in0=yT[:, c, 0:W],
                                                                  scalar=gt[:, gi * 8 + c:gi * 8 + c + 1], in1=rstd[:, 0:W],
                                                                  op0=ALU.mult, op1=ALU.mult),
                     reads=[('yT', c), 'rstd', 'g05', 'gcol'], writes=['tmpf'])
                P.op('dve', lambda h, c=c: h.tensor_tensor(out=hT[:, c, 0:W], in0=hT[:, c, 0:W], in1=tmpf[:, 0:W], op=ALU.add),
                     reads=[('hT', c), 'tmpf'], writes=[('hT', c)])
                if next_norm:
                    sq_chunk(hT[:, c, 0:W], [('hT', c)], xn[:, c, 0:W], [('xn', c)], W)
                    ones_mm(xn[:, c, 0:W], [('xn', c)], c, W)

        def ffn(f4, W, presq=False, next_norm=True):
            l, i = f4 // 2, f4 % 2
            norm_to_xn(l * 6 + (0 if i == 0 else 4), W, presq)
            JG = 4
            for j0 in range(0, NJ, JG):
                nj = min(JG, NJ - j0)
                wt, wn = wload(WinS[f4, j0:j0 + nj].rearrange("j p g k f -> p j (g k f)"), nj * 2048)
                wv = wt.rearrange("p (j g k f) -> p j g k f", j=nj, g=2, k=8)
                for jj in range(nj):
                    j = j0 + jj
                    (pg, pgn), (pu, pun) = banks[(2 * j) % 4], banks[(2 * j + 1) % 4]
                    for g_, (pp, ppn) in enumerate(((pg, pgn), (pu, pun))):
                        for kc in range(8):
                            P.op('pe', lambda h, pp=pp, g_=g_, kc=kc, jj=jj, wv=wv: h.matmul(
                                pp[:, 0:W], lhsT=wv[:, jj, g_, kc, :], rhs=xn[:, kc, 0:W], start=(kc == 0), stop=(kc == 7)),
                                 reads=[('xn', kc), wn], writes=[ppn])
                    sgt = sg[j % 2]; sgn = 'sg%d' % (j % 2)
                    P.op('act', lambda h, pg=pg, sgt=sgt: h.activation(out=sgt[:, 0:W], in_=pg[:, 0:W], func=AF.Silu),
                         reads=[pgn], writes=[sgn])
                    P.op('dve', lambda h, pu=pu, sgt=sgt, j=j: h.tensor_tensor(out=aT[:, j, 0:W], in0=sgt[:, 0:W], in1=pu[:, 0:W], op=ALU.mult),
                         reads=[pun, sgn], writes=['aT'])
            pend = None
            for d0 in range(0, 8, 2):
                wt, wn = wload(WoutS[f4, d0:d0 + 2].rearrange("dc p fc d -> p dc (fc d)"), 2 * NJ * 128)
                wv = wt.rearrange("p (dc fc d) -> p dc fc d", dc=2, fc=NJ)
                for dd in range(2):
                    dc = d0 + dd
                    pp, ppn = banks[4 + dc % 2]
                    for fc in range(NJ):
                        P.op('pe', lambda h, pp=pp, dd=dd, fc=fc, wv=wv: h.matmul(
                            pp[:, 0:W], lhsT=wv[:, dd, fc, :], rhs=aT[:, fc, 0:W], start=(fc == 0), stop=(fc == NJ - 1)),
                             reads=['aT', wn], writes=[ppn])
                    if pend is not None:
                        ones_mm(xn[:, pend, 0:W], [('xn', pend)], pend, W)
                    P.op('act', lambda h, pp=pp, dc=dc: h.activation(out=yT[:, dc, 0:W], in_=pp[:, 0:W], func=AF.Copy),
                         reads=[ppn], writes=[('yT', dc)])
                    sq_chunk(pp[:, 0:W], [ppn], xn[:, dc, 0:W], [('xn', dc)], W)
                    pend = dc
            ones_mm(xn[:, pend, 0:W], [('xn', pend)], pend, W)
            postnorm_add(l * 6 + (1 if i == 0 else 5), W, True, next_norm)

        def proj_fm(Wscr, W):
            pend = None
            for d0 in range(0, 8, 4):
                wt, wn = wload(Wscr[d0:d0 + 4].rearrange("dc p k d -> p dc (k d)"), 4 * 1024)
                wv = wt.rearrange("p (dc k d) -> p dc k d", dc=4, k=8)
                for dd in range(4):
                    dc = d0 + dd
                    pp, ppn = banks[4 + dc % 2]
                    for kc in range(8):
                        P.op('pe', lambda h, pp=pp, dd=dd, kc=kc, wv=wv: h.matmul(
                            pp[:, 0:W], lhsT=wv[:, dd, kc, :], rhs=xn[:, kc, 0:W], start=(kc == 0), stop=(kc == 7)),
                             reads=[('xn', kc), wn], writes=[ppn])
                    if pend is not None:
                        ones_mm(aT[:, pend, 0:W], ['aT'], pend, W)
                    P.op('act', lambda h, pp=pp, dc=dc: h.activation(out=yT[:, dc, 0:W], in_=pp[:, 0:W], func=AF.Copy),
                         reads=[ppn], writes=[('yT', dc)])
                    sq_chunk(pp[:, 0:W], [ppn], aT[:, dc, 0:W], ['aT'], W)
                    pend = dc
            ones_mm(aT[:, pend, 0:W], ['aT'], pend, W)

        tiles = [(t0, 512) for t0 in range(0, NT, 512)] + [(MOFF, 16)]

        with ExitStack() as es1:
            hT, yT, xn, aT, rstd, tmpf, sg, wbuf = lin_alloc('p1', es1)
            sb1 = lambda name, shape, dt=F32: es1.enter_context(nc.sbuf_tensor("s_" + name, list(shape), dt))
            xt = [sb1("xt0", [128, 1024]), sb1("xt1", [128, 1024])]
            qk_sb = sb1("qk_sb", [128, 16, 512], BF16)
            v_sb = [sb1("v_sb0", [128, 1024], BF16), sb1("v_sb1", [128, 1024], BF16)]
            xcl = [0]

            def p1_tile(t0, W):
                ismeta = (W == 16)
                nsub = 1 if ismeta else 4
                for s_ in range(nsub):
                    npt = 16 if ismeta else 128
                    xc = xcl[0]; xcl[0] += 1
                    xtt = xt[xc % 2]; xtn = 'xt%d' % (xc % 2)
                    src = meta if ismeta else xs[t0 + s_ * 128: t0 + (s_ + 1) * 128, :]
                    P.op('sp', lambda h, xtt=xtt, src=src, npt=npt: h.dma_start(out=xtt[0:npt, :], in_=src), writes=[xtn], dma=True)
                    for c0 in (0, 4):
                        pp, ppn = nbank(0, 4)
                        for cc in range(4):
                            c = c0 + cc
                            P.op('pe', lambda h, pp=pp, cc=cc, c=c, xtt=xtt, npt=npt: h.transpose(
                                pp[:, cc * 128: cc * 128 + npt], xtt[0:npt, c * 128:(c + 1) * 128], identf[0:npt, 0:npt]),
                                 reads=[xtn, 'identf'], writes=[ppn])
                        P.op('dve', lambda h, pp=pp, c0=c0, s_=s_, npt=npt: h.tensor_copy(
                            out=hT[:, c0:c0 + 4, s_ * 128: s_ * 128 + npt],
                            in_=pp.rearrange("p (c t) -> p c t", c=4)[:, :, 0:npt]), reads=[ppn], writes=[*HT])
                ffn(0, W, False, True)
                P.op('pool', lambda h, t0=t0, W=W: h.dma_start(out=hS[:, :, t0:t0 + W], in_=hT[:, :, 0:W]),
                     reads=[*HT], writes=[('hS', t0)], dma=True)
                norm_to_xn(2, W, True)
                for h0 in range(0, 16, 8):
                    wt, wn = wload(WqkA[h0:h0 + 8].rearrange("h p k e -> p h (k e)"), 8 * 1024)
                    wv = wt.rearrange("p (h k e) -> p h k e", h=8, k=8)
                    for hh in range(8):
                        hd = h0 + hh
                        pp, ppn = nbank(0, 4)
                        for kc in range(8):
                            P.op('pe', lambda h, pp=pp, hh=hh, kc=kc, wv=wv: h.matmul(
                                pp[:, 0:W], lhsT=wv[:, hh, kc, :], rhs=xn[:, kc, 0:W], start=(kc == 0), stop=(kc == 7)),
                                 reads=[*XN, wn], writes=[ppn])
                        P.op('act', lambda h, pp=pp, hd=hd: h.activation(out=qk_sb[:, hd, 0:W], in_=pp[:, 0:W], func=AF.Copy,
                                                                          scale=(0.125 if hd < 8 else 1.0)),
                             reads=[ppn], writes=['qk_sb'])
                P.op('pool', lambda h, t0=t0, W=W: h.dma_start(out=QTa[:, :, t0:t0 + W].rearrange("(i two) e t -> (two e) i t", two=2), in_=qk_sb[:, 0:8, 0:W]),
                     reads=['qk_sb'], writes=[('QTa', t0)], dma=True)
                P.op('pool', lambda h, t0=t0, W=W: h.dma_start(out=KTa[:, :, t0:t0 + W].rearrange("(i two) e t -> (two e) i t", two=2), in_=qk_sb[:, 8:16, 0:W]),
                     reads=['qk_sb'], writes=[('KTa', t0)], dma=True)
                wt, wn = wload(WvA, 8192)
                wv = wt.rearrange("p (k n) -> p k n", k=8)
                for s_ in range(nsub):
                    npt = 16 if ismeta else 128
                    vt = v_sb[s_ % 2]; vn = 'v_sb%d' % (s_ % 2)
                    for half in range(2):
                        pp, ppn = nbank(0, 4)
                        for kc in range(8):
                            P.op('pe', lambda h, pp=pp, kc=kc, s_=s_, half=half, npt=npt, wv=wv: h.matmul(
                                pp[0:npt, :], lhsT=xn[:, kc, s_ * 128: s_ * 128 + npt], rhs=wv[:, kc, half * 512:(half + 1) * 512],
                                start=(kc == 0), stop=(kc == 7)), reads=[*XN, wn], writes=[ppn])
                        P.op('dve', lambda h, pp=pp, vt=vt, half=half, npt=npt: h.tensor_copy(out=vt[0:npt, half * 512:(half + 1) * 512], in_=pp[0:npt, :]),
                             reads=[ppn], writes=[vn])
                    if ismeta:
                        P.op('pool', lambda h, vt=vt: h.dma_start(out=Vma.rearrange("h m e -> m h e"),
                                                                  in_=vt[0:16, :].rearrange("m (h e) -> m h e", e=64)),
                             reads=[vn], writes=['Vma'], dma=True)
                    else:
                        ch = t0 // 128 + s_
                        P.op('pool', lambda h, vt=vt, ch=ch: h.dma_start(out=Va[:, :, ch, :].rearrange("h p e -> p h e"),
                                                                         in_=vt[:, :].rearrange("p (h e) -> p h e", e=64)),
                             reads=[vn], writes=[('Va', ch)], dma=True)
            for (t0_, W_) in tiles:
                p1_tile(t0_, W_)
            P.flush()
        if stop <= 1:
            return nc

        with ExitStack() as es2:
            sb2 = lambda name, shape, dt=F32: es2.enter_context(nc.sbuf_tensor("s_" + name, list(shape), dt))
            KT = sb2("KT", [64, NTOK], BF16); QT = sb2("QT", [64, NTOK], BF16); OT = sb2("OT", [64, NTOK], BF16)
            Vh = sb2("Vh", [128, NCH, 64], BF16); Vmh = sb2("Vmh", [16, 64], BF16)
            RTh = sb2("RTh", [128, 7, 128]); RTV = sb2("RTV", [128, NSLOT, 128]); vmask = sb2("vmask", [128, NSLOT, 128])
            ssb = [sb2("ssb%d" % i, [128, 6, 128]) for i in range(3)]
            pt = [sb2("pt%d" % i, [128, 6, 128], BF16) for i in range(3)]
            pm = [sb2("pm%d" % i, [16, 128], BF16) for i in range(3)]
            rl = [sb2("rl0", [64, 128]), sb2("rl1", [64, 128])]
            P.op('sp', lambda h: h.dma_start(out=vmask[:], in_=vmask_d.rearrange("s k q -> k s q")), writes=['vmask'], dma=True)
            psS = [(psA, 'psA'), (psB, 'psB')]
            it = 0
            for hd in range(16):
                P.op('sp', lambda h, hd=hd: h.dma_start(out=KT[:], in_=KTa[hd]), writes=['KT'], dma=True)
                P.op('sp', lambda h, hd=hd: h.dma_start(out=QT[:], in_=QTa[hd]), writes=['QT'], dma=True)
                P.op('sp', lambda h, hd=hd: h.dma_start(out=Vh[:], in_=Va[hd]), writes=['Vh'], dma=True)
                P.op('sp', lambda h, hd=hd: h.dma_start(out=Vmh[:], in_=Vma[hd]), writes=['Vmh'], dma=True)
                P.op('sp', lambda h, hd=hd: h.dma_start(out=RTh[:], in_=rt_d[hd].rearrange("d k q -> k d q")), writes=['RTh'], dma=True)
                for name, dl in cls:
                    s0, _ = cstart[name]
                    for i_, dl_ in enumerate(dl):
                        P.op('pool', lambda h, s0=s0, i_=i_, dl_=dl_: h.tensor_tensor(out=RTV[:, s0 + i_, :], in0=vmask[:, s0 + i_, :],
                                                                                      in1=RTh[:, dl_ + 3, :], op=ALU.add),
                             reads=['vmask', 'RTh'], writes=['RTV'])
                def na_s1(gt, b_, b3, hd=hd):
                    ismeta = (gt == 3 * T)
                    pS, pSn = psS[b_]
                    pX, pXn = banks[4 + b_]
                    nq = 16 if ismeta else 128
                    q0 = MOFF if ismeta else gt * 128
                    if not ismeta:
                        seg, t = gt // T, gt % T
                        s0, dl = na_lookup(seg, t)
                        nb = len(dl)
                        for i_, dl_ in enumerate(dl):
                            gc = gt + dl_
                            P.op('pe', lambda h, pS=pS, i_=i_, gc=gc, q0=q0: h.matmul(
                                pS[:, i_ * 128:(i_ + 1) * 128], lhsT=KT[:, gc * 128:(gc + 1) * 128], rhs=QT[:, q0:q0 + 128], start=True, stop=True),
                                 reads=['KT', 'QT'], writes=[pSn])
                    P.op('pe', lambda h, pS=pS, q0=q0, nq=nq: h.matmul(pS[0:16, 768:768 + nq], lhsT=KT[:, MOFF:MOFF + 16], rhs=QT[:, q0:q0 + nq], start=True, stop=True),
                         reads=['KT', 'QT'], writes=[pSn])
                    P.op('act', lambda h, pS=pS, b3=b3, nq=nq, hd=hd: h.activation(out=pm[b3][:, 0:nq], in_=pS[0:16, 768:768 + nq], func=AF.Exp, bias=mbT[:, hd:hd + 1]),
                         reads=[pSn, 'mbT'], writes=['pm%d' % b3])
                    if not ismeta:
                        P.op('dve', lambda h, pS=pS, b3=b3, nb=nb, s0=s0: h.tensor_tensor(
                            out=ssb[b3][:, 0:nb, :], in0=pS[:, 0:nb * 128].rearrange("p (n q) -> p n q", q=128),
                            in1=RTV[:, s0:s0 + nb, :], op=ALU.add), reads=[pSn, 'RTV', 'pm%d' % b3], writes=['ssb%d' % b3])
                        P.op('act', lambda h, b3=b3, nb=nb: h.activation(out=pt[b3][:, 0:nb, :], in_=ssb[b3][:, 0:nb, :], func=AF.Exp),
                             reads=['ssb%d' % b3], writes=['pt%d' % b3])

                def na_s2(gt, b_, b3, hd=hd):
                    ismeta = (gt == 3 * T)
                    pX, pXn = banks[4 + b_]
                    nq = 16 if ismeta else 128
                    q0 = MOFF if ismeta else gt * 128
                    if not ismeta:
                        seg, t = gt // T, gt % T
                        s0, dl = na_lookup(seg, t)
                        for i_, dl_ in enumerate(dl):
                            gc = gt + dl_
                            P.op('pe', lambda h, pX=pX, i_=i_, gc=gc, b3=b3: h.matmul(
                                pX[0:64, 128:256], lhsT=Vh[:, gc, :], rhs=pt[b3][:, i_, :], start=(i_ == 0), stop=False),
                                 reads=['Vh', 'pt%d' % b3], writes=[pXn])
                    P.op('pe', lambda h, pX=pX, b3=b3, nq=nq, ismeta=ismeta: h.matmul(pX[0:64, 128:128 + nq], lhsT=Vmh[:, :], rhs=pm[b3][:, 0:nq], start=ismeta, stop=True),
                         reads=['Vmh', 'pm%d' % b3], writes=[pXn])
                    if not ismeta:
                        for i_, dl_ in enumerate(dl):
                            P.op('pe', lambda h, pX=pX, i_=i_, b3=b3: h.matmul(
                                pX[0:64, 256:384], lhsT=onesb[:, 0:64], rhs=pt[b3][:, i_, :], start=(i_ == 0), stop=False),
                                 reads=['onesb', 'pt%d' % b3], writes=[pXn])
                    P.op('pe', lambda h, pX=pX, b3=b3, nq=nq, ismeta=ismeta: h.matmul(pX[0:64, 256:256 + nq], lhsT=onesb[0:16, 0:64], rhs=pm[b3][:, 0:nq], start=ismeta, stop=True),
                         reads=['onesb', 'pm%d' % b3], writes=[pXn])
                    P.op('dve', lambda h, pX=pX, b_=b_, nq=nq: h.reciprocal(out=rl[b_][:, 0:nq], in_=pX[0:64, 256:256 + nq]),
                         reads=[pXn], writes=['rl%d' % b_])
                    P.op('dve', lambda h, pX=pX, b_=b_, nq=nq, q0=q0: h.tensor_tensor(out=OT[:, q0:q0 + nq], in0=pX[0:64, 128:128 + nq], in1=rl[b_][:, 0:nq], op=ALU.mult),
                         reads=[pXn, 'rl%d' % b_], writes=['OT'])

                pend = []
                for gt in range(3 * T + 1):
                    b_ = it % 2; b3 = it % 3; it += 1
                    na_s1(gt, b_, b3)
                    pend.append((gt, b_, b3))
                    if len(pend) > 2:
                        na_s2(*pend.pop(0))
                while pend:
                    na_s2(*pend.pop(0))
                P.op('pool', lambda h, hd=hd: h.dma_start(out=OTa[hd // 2, (hd % 2) * 64:(hd % 2) * 64 + 64, :], in_=OT[:]),
                     reads=['OT'], writes=[('OTa', hd)], dma=True)
            P.flush()
        if stop <= 2:
            return nc

        with ExitStack() as es3:
            hT, yT, xn, aT, rstd, tmpf, sg, wbuf = lin_alloc('p3', es3)
            sb3 = lambda name, shape, dt=F32: es3.enter_context(nc.sbuf_tensor("s_" + name, list(shape), dt))
            qk_tm = [sb3("qk_tm0", [128, 1024]), sb3("qk_tm1", [128, 1024])]
            rtmp = [sb3("rtmp%d" % i, [128, 32, 8]) for i in range(4)]
            cst = sb3("cst", [128, 4, 16])
            qkT = sb3("qkT", [64, 32, 512], BF16)
            v_sb = [sb3("v3_sb0", [128, 1024], BF16), sb3("v3_sb1", [128, 1024], BF16)]

            def p3_tile(t0, W):
                ismeta = (W == 16)
                nsub = 1 if ismeta else 4
                npt = 16 if ismeta else 128
                P.op('sp', lambda h, t0=t0, W=W: h.dma_start(out=xn[:, :, 0:W], in_=OTa[:, :, t0:t0 + W].rearrange("c p t -> p c t")),
                     writes=[*XN], dma=True)
                P.op('sp', lambda h, t0=t0, W=W: h.dma_start(out=hT[:, :, 0:W], in_=hS[:, :, t0:t0 + W]), writes=[*HT], dma=True)
                proj_fm(WoA, W)
                postnorm_add(3, W, False, True)
                ffn(1, W, True, True)
                ffn(2, W, True, True)
                P.op('pool', lambda h, t0=t0, W=W: h.dma_start(out=hS[:, :, t0:t0 + W], in_=hT[:, :, 0:W]),
                     reads=[*HT], writes=[('hS', t0)], dma=True)
                norm_to_xn(6 + 2, W, True)
                if ismeta:
                    P.op('sp', lambda h: h.dma_start(out=cst[0:16, 0, :], in_=cs_d[MOFF:MOFF + 16, :]), writes=['cst'], dma=True)
                else:
                    P.op('sp', lambda h, t0=t0: h.dma_start(out=cst[:, :, :], in_=cs_d[t0:t0 + 512, :].rearrange("(s p) c -> p s c", p=128)),
                         writes=['cst'], dma=True)
                for cbp in range(2):
                    wt, wn = wload(WqkB[:, :, cbp * 1024:(cbp + 1) * 1024], 8192)
                    wv = wt.rearrange("p (k n) -> p k n", k=8)
                    for s_ in range(nsub):
                        qt = qk_tm[s_ % 2]; qn = 'qk_tm%d' % (s_ % 2)
                        for cb in range(2):
                            pp, ppn = nbank(0, 4)
                            for kc in range(8):
                                P.op('pe', lambda h, pp=pp, kc=kc, s_=s_, cb=cb, wv=wv: h.matmul(
                                    pp[0:npt, :], lhsT=xn[:, kc, s_ * 128: s_ * 128 + npt], rhs=wv[:, kc, cb * 512:(cb + 1) * 512],
                                    start=(kc == 0), stop=(kc == 7)), reads=[*XN, wn], writes=[ppn])
                            P.op('act', lambda h, pp=pp, cb=cb, qt=qt: h.activation(out=qt[0:npt, cb * 512:(cb + 1) * 512], in_=pp[0:npt, :], func=AF.Copy),
                                 reads=[ppn], writes=[qn])
                        qv = qt[0:npt, :].rearrange("p (m e) -> p m e", e=64)
                        x1 = qv[:, :, 0:8]; x2 = qv[:, :, 8:16]
                        cosb = cst[0:npt, s_, 0:8].unsqueeze(1).to_broadcast([npt, 16, 8])
                        sinb = cst[0:npt, s_, 8:16].unsqueeze(1).to_broadcast([npt, 16, 8])
                        for k_, (a_, b2) in enumerate(((x1, cosb), (x2, sinb), (x2, cosb), (x1, sinb))):
                            P.op('dve', lambda h, k_=k_, a_=a_, b2=b2: h.tensor_tensor(out=rtmp[k_][0:npt, 0:16, :], in0=a_, in1=b2, op=ALU.mult),
                                 reads=[qn, 'cst'], writes=['rtmp%d' % k_])
                        P.op('dve', lambda h, x1=x1: h.tensor_tensor(out=x1, in0=rtmp[0][0:npt, 0:16, :], in1=rtmp[1][0:npt, 0:16, :], op=ALU.subtract),
                             reads=['rtmp0', 'rtmp1'], writes=[qn])
                        P.op('dve', lambda h, x2=x2: h.tensor_tensor(out=x2, in0=rtmp[2][0:npt, 0:16, :], in1=rtmp[3][0:npt, 0:16, :], op=ALU.add),
                             reads=['rtmp2', 'rtmp3'], writes=[qn])
                        for i0 in range(0, 16, 4):
                            pp, ppn = nbank(0, 4)
                            for ii in range(4):
                                i_ = i0 + ii
                                P.op('pe', lambda h, pp=pp, ii=ii, i_=i_, qt=qt: h.transpose(
                                    pp[0:64, ii * 128: ii * 128 + npt], qt[0:npt, i_ * 64:(i_ + 1) * 64], identf[0:npt, 0:npt]),
                                     reads=[qn, 'identf'], writes=[ppn])
                            P.op('act', lambda h, pp=pp, i0=i0, s_=s_, cbp=cbp: h.activation(
                                out=qkT[:, cbp * 16 + i0:cbp * 16 + i0 + 4, s_ * 128: s_ * 128 + npt],
                                in_=pp[0:64, :].rearrange("p (i t) -> p i t", i=4)[:, :, 0:npt],
                                func=AF.Copy, scale=(0.125 if cbp == 0 else 1.0)), reads=[ppn], writes=['qkT'])
                wt, wn = wload(WvB, 8192)
                wv = wt.rearrange("p (k n) -> p k n", k=8)
                for s_ in range(nsub):
                    vt = v_sb[s_ % 2]; vn = 'v3_sb%d' % (s_ % 2)
                    for half in range(2):
                        pp, ppn = nbank(0, 4)
                        for kc in range(8):
                            P.op('pe', lambda h, pp=pp, kc=kc, s_=s_, half=half, wv=wv: h.matmul(
                                pp[0:npt, :], lhsT=xn[:, kc, s_ * 128: s_ * 128 + npt], rhs=wv[:, kc, half * 512:(half + 1) * 512],
                                start=(kc == 0), stop=(kc == 7)), reads=[*XN, wn], writes=[ppn])
                        P.op('dve', lambda h, pp=pp, vt=vt, half=half: h.tensor_copy(out=vt[0:npt, half * 512:(half + 1) * 512], in_=pp[0:npt, :]),
                             reads=[ppn], writes=[vn])
                    if ismeta:
                        P.op('pool', lambda h, vt=vt: h.dma_start(out=Vmb.rearrange("h m e -> m h e"),
                                                                  in_=vt[0:16, :].rearrange("m (h e) -> m h e", e=128)),
                             reads=[vn], writes=['Vmb'], dma=True)
                    else:
                        ch = t0 // 128 + s_
                        P.op('pool', lambda h, vt=vt, ch=ch: h.dma_start(out=Vb[:, :, ch, :].rearrange("h p e -> p h e"),
                                                                         in_=vt[:, :].rearrange("p (h e) -> p h e", e=128)),
                             reads=[vn], writes=[('Vb', ch)], dma=True)
                P.op('pool', lambda h, t0=t0, W=W: h.dma_start(out=QTb[:, :, :, t0:t0 + W].rearrange("h m e t -> e (h m) t"), in_=qkT[:, 0:16, 0:W]),
                     reads=['qkT'], writes=[('QTb', t0)], dma=True)
                P.op('pool', lambda h, t0=t0, W=W: h.dma_start(out=KTb[:, :, :, t0:t0 + W].rearrange("h m e t -> e (h m) t"), in_=qkT[:, 16:32, 0:W]),
                     reads=['qkT'], writes=[('KTb', t0)], dma=True)
            for (t0_, W_) in tiles:
                p3_tile(t0_, W_)
            P.flush()
        if stop <= 3:
            return nc

        with ExitStack() as es4:
            hT, yT, xn, aT, rstd, tmpf, sg, wbuf = lin_alloc('p4', es4)
            sb4 = lambda name, shape, dt=F32: es4.enter_context(nc.sbuf_tensor("s_" + name, list(shape), dt))
            NKC = KB // 128
            Kb = [sb4("Kb%d" % i, [128, KB], BF16) for i in range(2)]
            Vp = [sb4("Vp%d" % i, [128, NKC, 130], BF16) for i in range(2)]
            Km = sb4("Km", [128, 8, 16], BF16); Vpm = sb4("Vpm", [16, 8, 130], BF16)
            Qg = [sb4("Qg%d" % i, [128, 512], BF16) for i in range(2)]
            PT = [sb4("PT%d" % i, [128, 2, 512], BF16) for i in range(3)]
            Otm = sb4("Otm", [128, 4, 1024])
            t0s = sb4("t0s", [128, 128]); dsb = sb4("dsb", [128, 128]); dsq = sb4("dsq", [128, 128])
            rec = sb4("rec", [128, 4]); ss = sb4("ss", [128, 2])
            ot = [sb4("ot0", [128, 1024]), sb4("ot1", [128, 1024])]
            for i in range(2):
                P.op('pool', lambda h, i=i: h.memset(Vp[i][:], 1.0), writes=['Vp%d' % i])
            P.op('pool', lambda h: h.memset(Vpm[:], 1.0), writes=['Vpm'])
            P.op('sp', lambda h: h.dma_start(out=Km[:], in_=KTb[:, :, :, MOFF:MOFF + 16].rearrange("h m e t -> (m e) h t")), writes=['Km'], dma=True)
            P.op('sp', lambda h: h.dma_start(out=Vpm[:, :, 0:128], in_=Vmb.rearrange("h m e -> m h e")), reads=['Vpm'], writes=['Vpm'], dma=True)
            psS = [(psA, ['b0', 'b1']), (psB, ['b2', 'b3'])]
            PSO = ['b4', 'b5', 'b6', 'b7']
            kvc = 0; sc = 0
            NG = NT // 512
            GPS = SEG // 512
            for g in range(NG):
                seg = g // GPS
                q0 = g * 512
                ksegs = [0, 1] if seg < 2 else [2]
                st = {'kvc': kvc, 'sc': sc}

                def gen_units():
                    for hd in range(8):
                        qb = Qg[hd % 2]; qbn = 'Qg%d' % (hd % 2)
                        blocks = [(ks, kb0) for ks in ksegs for kb0 in range(0, SEG, KB)] + [None]
                        first = True
                        for u in blocks:
                            nch = NKC if u is not None else 1
                            for j in range(nch):
                                yield dict(hd=hd, qb=qb, qbn=qbn, u=u, j=j, first=first, last=(u is None), newhead=(first), newblk=(j == 0))
                                first = False

                def prep(d):
                    hd = d['hd']
                    if d['newhead']:
                        P.op('sp', lambda h, qb=d['qb'], hd=hd, q0=q0: h.dma_start(out=qb[:], in_=QTb[hd, :, :, q0:q0 + 512].rearrange("m e t -> (m e) t")),
                             writes=[d['qbn']], dma=True)
                    if d['u'] is not None:
                        ks, kb0 = d['u']
                        if d['newblk']:
                            i = st['kvc'] % 2; st['kvc'] += 1
                            st['i'] = i
                            k0 = ks * SEG + kb0
                            P.op('sp', lambda h, i=i, hd=hd, k0=k0: h.dma_start(out=Kb[i][:], in_=KTb[hd, :, :, k0:k0 + KB].rearrange("m e t -> (m e) t")),
                                 writes=['Kb%d' % i], dma=True)
                            P.op('sp', lambda h, i=i, hd=hd, k0=k0: h.dma_start(out=Vp[i][:, :, 0:128], in_=Vb[hd, :, k0 // 128:k0 // 128 + NKC, :]),
                                 reads=['Vp%d' % i], writes=['Vp%d' % i], dma=True)
                        i = st['i']; j = d['j']
                        d['bias'] = zeroc if ks == seg else segm
                        d['kap'] = Kb[i][:, j * 128:(j + 1) * 128]; d['vap'] = Vp[i][:, j, 0:129]; d['nk'] = 128
                        d['rd'] = ['Kb%d' % i, 'Vp%d' % i]
                    else:
                        d['bias'] = zeroc
                        d['kap'] = Km[:, hd, :]; d['vap'] = Vpm[:, hd, 0:129]; d['nk'] = 16; d['rd'] = ['Km', 'Vpm']
                    d['b'] = st['sc'] % 2; d['pb'] = st['sc'] % 3; st['sc'] += 1

                def emit_S(d):
                    b_ = d['b']; pS, pSn = psS[b_]; nk = d['nk']; kap = d['kap']; qb = d['qb']
                    for m in range(2):
                        P.op('pe', lambda h, pS=pS, m=m, kap=kap, qb=qb, nk=nk: h.matmul(
                            pS[0:nk, m * 512:(m + 1) * 512], lhsT=kap[m * 64:(m + 1) * 64, :], rhs=qb[m * 64:(m + 1) * 64, :], start=True, stop=True,
                            tile_position=(m * 64, 0)),
                             reads=[d['rd'][0], d['qbn']], writes=pSn)
                    pb = d['pb']
                    P.op('act', lambda h, pS=pS, pb=pb, nk=nk, bias=d['bias']: h.activation(
                        out=PT[pb][0:nk].rearrange("p m q -> p (m q)"), in_=pS[0:nk, :], func=AF.Exp, bias=bias[0:nk, :]),
                         reads=pSn + ['segm', 'zeroc'], writes=['PT%d' % pb])

                def emit_AV(d):
                    b_ = d['pb']; nk = d['nk']; vap = d['vap']
                    for qs in range(4):
                        for m in range(2):
                            a_ = qs * 2 + m
                            P.op('pe', lambda h, a_=a_, qs=qs, m=m, b_=b_, vap=vap, nk=nk, first=d['first'], last=d['last']: h.matmul(
                                psO[:, a_ * 256: a_ * 256 + 129], lhsT=PT[b_][0:nk, m, qs * 128:(qs + 1) * 128], rhs=vap[0:nk, :],
                                start=(first and a_ % 2 == 0), stop=last), reads=['PT%d' % b_, d['rd'][1]], writes=PSO)
                    if d['last']:
                        combine(d['hd'])

                def combine(hd):
                    for qs in range(4):
                        a0 = psO[:, (2 * qs) * 256:(2 * qs) * 256 + 129]; a1 = psO[:, (2 * qs + 1) * 256:(2 * qs + 1) * 256 + 129]
                        P.op('dve', lambda h, a0=a0: h.reciprocal(out=rec[:, 0:1], in_=a0[:, 128:129]), reads=PSO, writes=['rec'])
                        P.op('dve', lambda h, a1=a1: h.reciprocal(out=rec[:, 1:2], in_=a1[:, 128:129]), reads=PSO, writes=['rec'])
                        P.op('dve', lambda h: h.tensor_tensor(out=rec[:, 2:3], in0=rec[:, 1:2], in1=neglam[:], op=ALU.mult),
                             reads=['rec', 'neglam'], writes=['rec'])
                        P.op('dve', lambda h, a0=a0: h.tensor_scalar(out=t0s[:], in0=a0[:, 0:128], scalar1=rec[:, 0:1], scalar2=None, op0=ALU.mult),
                             reads=PSO + ['rec'], writes=['t0s'])
                        P.op('dve', lambda h, a1=a1: h.scalar_tensor_tensor(out=dsb[:], in0=a1[:, 0:128], scalar=rec[:, 2:3], in1=t0s[:],
                                                                            op0=ALU.mult, op1=ALU.add),
                             reads=PSO + ['rec', 't0s'], writes=['dsb'])
                        P.op('dve', lambda h: h.memset(ss[:, 0:1], 0.0), writes=['ss'])
                        P.op('dve', lambda h: h.scalar_tensor_tensor(out=dsq[:], in0=dsb[:], scalar=1.0, in1=dsb[:], op0=ALU.mult, op1=ALU.mult,
                                                                     accum_out=ss[:, 0:1]), reads=['dsb'], writes=['dsq', 'ss'])
                        P.op('act', lambda h: h.activation(out=ss[:, 1:2], in_=ss[:, 0:1], func=AF.Ln, bias=epsc[:], scale=1.0 / 128),
                             reads=['ss', 'epsc'], writes=['ss'])
                        P.op('act', lambda h: h.activation(out=ss[:, 1:2], in_=ss[:, 1:2], func=AF.Exp, scale=-0.5), reads=['ss'], writes=['ss'])
                        P.op('dve', lambda h, qs=qs, hd=hd: h.scalar_tensor_tensor(out=Otm[:, qs, hd * 128:(hd + 1) * 128], in0=dsb[:], scalar=ss[:, 1:2],
                                                                                 in1=subg[:], op0=ALU.mult, op1=ALU.mult),
                             reads=['dsb', 'ss', 'subg'], writes=['Otm'])

                pending = []
                for d in gen_units():
                    prep(d)
                    emit_S(d)
                    pending.append(d)
                    if len(pending) > 2:
                        emit_AV(pending.pop(0))
                while pending:
                    emit_AV(pending.pop(0))
                kvc = st['kvc']; sc = st['sc']
                if dbg:
                    for qs in range(4):
                        P.op('pool', lambda h, qs=qs, q0=q0: h.dma_start(out=dbgO[q0 + qs * 128:q0 + (qs + 1) * 128, :], in_=Otm[:, qs, :]),
                             reads=['Otm'], writes=[('dbgO', q0, qs)], dma=True)
                for qs in range(4):
                    for c0 in (0, 4):
                        pp, ppn = nbank(0, 4)
                        for cc in range(4):
                            c = c0 + cc
                            P.op('pe', lambda h, pp=pp, cc=cc, c=c, qs=qs: h.transpose(
                                pp[:, cc * 128:(cc + 1) * 128], Otm[:, qs, c * 128:(c + 1) * 128], identf[:]),
                                 reads=['Otm', 'identf'], writes=[ppn])
                        P.op('dve', lambda h, pp=pp, c0=c0, qs=qs: h.tensor_copy(
                            out=xn[:, c0:c0 + 4, qs * 128:(qs + 1) * 128], in_=pp.rearrange("p (c t) -> p c t", c=4)), reads=[ppn], writes=[*XN])
                P.op('sp', lambda h, q0=q0: h.dma_start(out=hT[:], in_=hS[:, :, q0:q0 + 512]), writes=[*HT], dma=True)
                proj_fm(WoB, 512)
                postnorm_add(6 + 3, 512, False, True)
                ffn(3, 512, True, False)
                for qs in range(4):
                    o_ = ot[qs % 2]; on_ = 'ot%d' % (qs % 2)
                    for c0 in (0, 4):
                        pp, ppn = nbank(0, 4)
                        for cc in range(4):
                            c = c0 + cc
                            P.op('pe', lambda h, pp=pp, cc=cc, c=c, qs=qs: h.transpose(
                                pp[:, cc * 128:(cc + 1) * 128], hT[:, c, qs * 128:(qs + 1) * 128], identf[:]),
                                 reads=[*HT, 'identf'], writes=[ppn])
                        P.op('act', lambda h, pp=pp, c0=c0, o_=o_: h.activation(out=o_[:, c0 * 128:(c0 + 4) * 128], in_=pp[:, :], func=AF.Copy),
                             reads=[ppn], writes=[on_])
                    P.op('pool', lambda h, o_=o_, q0=q0, qs=qs: h.dma_start(out=y[q0 + qs * 128: q0 + (qs + 1) * 128, :], in_=o_[:]),
                         reads=[on_], writes=[('y', q0, qs)], dma=True)
            P.flush()
    return nc


def host_inputs(SEG, segsA, typ, common):
    m = dict(common)
    m["xs"] = np.ascontiguousarray(np.concatenate(segsA, axis=0), dtype=np.float32)
    m["vmask"] = host_vmask(SEG, typ)
    m["cs"] = host_cs(SEG, typ)
    m["segm"] = np.full((128, 1), 0.0 if typ == 2 else NEG, np.float32)
    return m


def host_common(meta_tokens, norm_g, w_ffn_in, w_ffn_out, w_qkv_a, w_o_a, rpb_a, meta_bias_a, w_qkv_b, w_o_b, lambda_b, subln_b):
    f = lambda a: np.ascontiguousarray(np.asarray(a, dtype=np.float32))
    g = f(norm_g).reshape(12, 8, 128)
    return {
        "meta": f(meta_tokens),
        "gcol": np.ascontiguousarray(g.transpose(2, 0, 1).reshape(128, 96)),
        "w_in": f(w_ffn_in).reshape(4, D, 2 * DFF), "w_out": f(w_ffn_out).reshape(4, DFF, D),
        "wqkv_a": f(w_qkv_a)[0], "wo_a": f(w_o_a)[0], "wqkv_b": f(w_qkv_b)[0], "wo_b": f(w_o_b)[0],
        "rt": host_rt(f(rpb_a)[0]),
        "mbT": np.ascontiguousarray(f(meta_bias_a)[0].T),
        "lam": f(lambda_b)[0].reshape(1, 256), "subg": f(subln_b)[0].reshape(1, 128),
        "ident": np.eye(128, dtype=np.float32),
    }


def run(SEG, x_prompt, x_sample, params):
    lambda_init = 0.8 - 0.6 * math.exp(-0.3 * 1)
    common = host_common(**params)
    nc = build(SEG, lambda_init)
    xp = np.asarray(x_prompt, dtype=np.float32); xsamp = np.asarray(x_sample, dtype=np.float32)
    in_maps = []
    for c in range(4):
        in_maps.append(host_inputs(SEG, [xsamp[c, :SEG], xsamp[c, SEG:], xp[c]], 2, common))
    for c in range(4):
        in_maps.append(host_inputs(SEG, [xp[4 + 3 * c], xp[5 + 3 * c], xp[6 + 3 * c]], 1, common))
    res = run_bass_kernel_spmd(nc, in_maps, core_ids=list(range(8)))
    yp = np.zeros_like(xp); ysamp = np.zeros_like(xsamp)
    for c in range(4):
        yc = res.results[c]["y"]
        ysamp[c, :SEG] = yc[0:SEG]; ysamp[c, SEG:] = yc[SEG:2 * SEG]; yp[c] = yc[2 * SEG:]
    for c in range(4):
        yc = res.results[4 + c]["y"]
        for k in range(3):
            yp[4 + 3 * c + k] = yc[k * SEG:(k + 1) * SEG]
    return yp, ysamp


def kernel(x_prompt, x_sample, meta_tokens, norm_g, w_ffn_in, w_ffn_out, w_qkv_a, w_o_a,
           rpb_a, meta_bias_a, w_qkv_b, w_o_b, lambda_b, subln_b):
    params = dict(meta_tokens=meta_tokens, norm_g=norm_g, w_ffn_in=w_ffn_in, w_ffn_out=w_ffn_out,
                  w_qkv_a=w_qkv_a, w_o_a=w_o_a, rpb_a=rpb_a, meta_bias_a=meta_bias_a,
                  w_qkv_b=w_qkv_b, w_o_b=w_o_b, lambda_b=lambda_b, subln_b=subln_b)
    yp, ysamp = run(4096, x_prompt, x_sample, params)
    return (yp, ysamp)
```

```python
import math
from contextlib import ExitStack
import numpy as np
import concourse.bass as bass
import concourse.mybir as mybir
from concourse.bass_utils import run_bass_kernel_spmd

F32 = mybir.dt.float32
BF16 = mybir.dt.bfloat16
AF = mybir.ActivationFunctionType
ALU = mybir.AluOpType
ENGS = ['pe', 'act', 'dve', 'pool', 'sp']
DMA_SLOTS = 8
D = 1024
DFF = 2816
NJ = 22
NEG = -30000.0
EPS = 1e-6


class Op:
    __slots__ = ('eng', 'fn', 'deps', 'needed', 'dma', 'val', 'slot', 'prev')

    def __init__(s, eng, fn, dma):
        s.eng = eng; s.fn = fn; s.dma = dma; s.deps = set(); s.needed = False
        s.val = 0; s.slot = None; s.prev = 0


class Prog:
    def __init__(s, nc, es):
        s.nc = nc
        s.streams = {e: [] for e in ENGS}
        s.cells = {}
        s.ndma = {e: 0 for e in ENGS}
        s.count = {e: 0 for e in ENGS}
        s.seen = {e: {} for e in ENGS}
        s.csem = {e: es.enter_context(nc.semaphore('c_' + e)) for e in ENGS}
        s.dsem = {e: [es.enter_context(nc.semaphore('d_%s%d' % (e, i))) for i in range(DMA_SLOTS)]
                  for e in ('sp', 'pool', 'act')}
        s.nblk = 0

    def op(s, eng, fn, reads=(), writes=(), dma=False):
        o = Op(eng, fn, dma)
        deps = set()
        for c in reads:
            st = s.cells.get(c)
            if st is not None and st[0] is not None:
                deps.add(st[0])
        for c in writes:
            st = s.cells.get(c)
            if st is not None:
                if st[0] is not None:
                    deps.add(st[0])
                deps.update(st[1])
        for d in deps:
            if d.eng == 'pe' and eng == 'pe' and not d.dma and not dma:
                continue
            o.deps.add(d); d.needed = True
        for c in reads:
            st = s.cells.setdefault(c, [None, []])
            st[1].append(o)
        for c in writes:
            s.cells[c] = [o, []]
        if dma:
            n = s.ndma[eng]; s.ndma[eng] = n + 1
            o.slot = n % DMA_SLOTS; o.val = 16 * (n // DMA_SLOTS + 1); o.prev = 16 * (n // DMA_SLOTS)
        s.streams[eng].append(o)
        return o

    def flush(s):
        nc = s.nc
        handles = {'pe': 'tensor', 'act': 'scalar', 'dve': 'vector', 'pool': 'gpsimd', 'sp': 'sync'}
        for e in ENGS:
            c = s.count[e]
            for o in s.streams[e]:
                if not o.dma and o.needed:
                    c += 1; o.val = c
            s.count[e] = c
        s.nblk += 1
        with nc.Block() as block:
            def mk(e):
                def body(h):
                    seen = s.seen[e]

                    def wait(key, sem, val):
                        if seen.get(key, 0) < val:
                            h.wait_ge(sem, val); seen[key] = val
                    for o in s.streams[e]:
                        for d in o.deps:
                            if d.dma:
                                wait(('d', d.eng, d.slot), s.dsem[d.eng][d.slot], d.val)
                            else:
                                wait(('c', d.eng), s.csem[d.eng], d.val)
                        if o.dma and o.prev > 0:
                            wait(('d', e, o.slot), s.dsem[e][o.slot], o.prev)
                        ins = o.fn(h)
                        if o.dma:
                            ins.then_inc(s.dsem[e][o.slot], 16)
                        elif o.needed:
                            ins.then_inc(s.csem[e], 1)
                    n = s.ndma[e]
                    if n > 0:
                        for sl in range(min(n, DMA_SLOTS)):
                            last = ((n - 1 - sl) // DMA_SLOTS) + 1
                            wait(('d', e, sl), s.dsem[e][sl], 16 * last)
                return body
            for e in ENGS:
                getattr(block, handles[e])(mk(e))
        s.streams = {e: [] for e in ENGS}
        s.cells = {}


def na_classes(T):
    cls = [('INT', [-2, -1, 0, 1, 2]), ('TOP0', [0, 1, 2, 3]), ('TOP1', [-1, 0, 1, 2]),
           ('BOT0', [-2, -1, 0, 1]), ('BOT1', [-3, -2, -1, 0]),
           ('SPA0', [-2, -1, 0, 1, 2]), ('SPA1', [-3, -2, -1, 0, 1, 2]),
           ('SPB0', [-2, -1, 0, 1, 2, 3]), ('SPB1', [-2, -1, 0, 1, 2])]
    start = {}
    n = 0
    for name, dl in cls:
        start[name] = (n, dl); n += len(dl)

    def lookup(seg, t):
        if seg == 0 and t == T - 2: return start['SPA0']
        if seg == 0 and t == T - 1: return start['SPA1']
        if seg == 1 and t == 0: return start['SPB0']
        if seg == 1 and t == 1: return start['SPB1']
        if t == 0: return start['TOP0']
        if t == 1: return start['TOP1']
        if t == T - 2: return start['BOT0']
        if t == T - 1: return start['BOT1']
        return start['INT']
    rep = {'INT': (2, 2), 'TOP0': (2, 0), 'TOP1': (2, 1), 'BOT0': (2, T - 2), 'BOT1': (2, T - 1),
           'SPA0': (0, T - 2), 'SPA1': (0, T - 1), 'SPB0': (1, 0), 'SPB1': (1, 1)}
    return cls, start, lookup, rep, n


def host_vmask(SEG, typ):
    R = SEG // 64; T = R // 2
    cls, start, lookup, rep, n = na_classes(T)
    out = np.zeros((n, 128, 128), np.float32)
    qc = np.arange(64)
    cs = np.clip(qc - 8, 0, 48)
    kc = np.arange(64)
    colok = (kc[:, None] >= cs[None, :]) & (kc[:, None] < cs[None, :] + 16)
    for name, dl in cls:
        seg, t = rep[name]
        s0, _ = start[name]
        for i, dl_ in enumerate(dl):
            blk = np.full((2, 64, 2, 64), NEG, np.float32)
            for b in range(2):
                qg = seg * R + 2 * t + b
                if typ == 2 and qg < 2 * R:
                    sq0, nr = 0, 2 * R
                else:
                    sq0, nr = (qg // R) * R, R
                r = qg - sq0
                r0 = min(max(r - 4, 0), nr - 8)
                for a in range(2):
                    kg = seg * R + 2 * (t + dl_) + a
                    if sq0 + r0 <= kg <= sq0 + r0 + 7:
                        blk[a, :, b, :] = np.where(colok, 0.0, NEG)
            out[s0 + i] = blk.reshape(128, 128)
    return out


def host_rt(rpb):
    rt = np.zeros((16, 7, 2, 64, 2, 64), np.float32)
    kc = np.arange(64)[:, None]; qc = np.arange(64)[None, :]
    dcol = kc - qc + 15
    ok = (dcol >= 0) & (dcol <= 30)
    dcc = np.clip(dcol, 0, 30)
    for di, Dl in enumerate(range(-3, 4)):
        for a in range(2):
            for b in range(2):
                dr = 2 * Dl + a - b + 7
                if 0 <= dr <= 14:
                    rt[:, di, a, :, b, :] = np.where(ok[None], rpb[:, dr][:, dcc], 0.0)
    return rt.reshape(16, 7, 128, 128)


def host_cs(SEG, typ):
    NTOK = 3 * SEG + 16
    pos = np.zeros(NTOK, np.float32)
    ar = np.arange(SEG, dtype=np.float32)
    pos[0:SEG] = 16 + ar
    pos[SEG:2 * SEG] = (16 + SEG + ar) if typ == 2 else (16 + ar)
    pos[2 * SEG:3 * SEG] = 16 + ar
    pos[3 * SEG:] = np.arange(16, dtype=np.float32)
    inv = (np.float32(500000.0) ** (-np.arange(0, 16, 2, dtype=np.float32) / np.float32(16))).astype(np.float32)
    ang = (pos[:, None] * inv[None, :]).astype(np.float32)
    return np.concatenate([np.cos(ang), np.sin(ang)], axis=1).astype(np.float32)


def build(SEG, lambda_init, stop=9, dbg=False):
    NT = 3 * SEG
    NTOK = NT + 16
    MOFF = NT
    NCH = NT // 128
    R = SEG // 64; T = R // 2
    KB = min(2048, SEG)
    cls, cstart, na_lookup, _, NSLOT = na_classes(T)

    nc = bass.Bass("TRN2", target_bir_lowering=False)

    def din(name, shape, dt=F32):
        return nc.dram_tensor(name, list(shape), dt, kind="ExternalInput").ap()

    def dscr(name, shape, dt=BF16):
        return nc.dram_tensor(name, list(shape), dt, kind=("ExternalOutput" if dbg else "Internal")).ap()
    xs = din("xs", [NT, D]); meta = din("meta", [16, D]); gcol_d = din("gcol", [128, 96])
    w_in = din("w_in", [4, D, 2 * DFF]); w_out = din("w_out", [4, DFF, D])
    wqkv_a = din("wqkv_a", [D, 3 * D]); wo_a = din("wo_a", [D, D])
    wqkv_b = din("wqkv_b", [D, 3 * D]); wo_b = din("wo_b", [D, D])
    rt_d = din("rt", [16, 7, 128, 128]); vmask_d = din("vmask", [NSLOT, 128, 128])
    mbT_d = din("mbT", [16, 16]); lam_d = din("lam", [1, 256]); subg_d = din("subg", [1, 128])
    cs_d = din("cs", [NTOK, 16]); segm_d = din("segm", [128, 1]); ident_d = din("ident", [128, 128])
    y = nc.dram_tensor("y", [NT, D], F32, kind="ExternalOutput").ap()
    dbgO = nc.dram_tensor("dbgO", [NT, D], F32, kind="ExternalOutput").ap() if dbg else None

    WinS = dscr("WinS", [4, NJ, 128, 2, 8, 128])
    WoutS = dscr("WoutS", [4, 8, 128, NJ, 128])
    WqkA = dscr("WqkA", [16, 128, 8, 128])
    WvA = dscr("WvA", [128, 8, 1024])
    WoA = dscr("WoA", [8, 128, 8, 128])
    WqkB = dscr("WqkB", [128, 8, 2048])
    WvB = dscr("WvB", [128, 8, 1024])
    WoB = dscr("WoB", [8, 128, 8, 128])
    hS = dscr("hS", [128, 8, NTOK], F32)
    QTa = dscr("QTa", [16, 64, NTOK]); KTa = dscr("KTa", [16, 64, NTOK])
    Va = dscr("Va", [16, 128, NCH, 64]); Vma = dscr("Vma", [16, 16, 64])
    OTa = dscr("OTa", [8, 128, NTOK])
    QTb = dscr("QTb", [8, 2, 64, NTOK]); KTb = dscr("KTb", [8, 2, 64, NTOK])
    Vb = dscr("Vb", [8, 128, NCH, 128]); Vmb = dscr("Vmb", [8, 16, 128])

    with ExitStack() as es:
        P = Prog(nc, es)
        sb = lambda name, shape, dt=F32: es.enter_context(nc.sbuf_tensor("s_" + name, list(shape), dt))
        psA = es.enter_context(nc.psum_tensor("psA", [128, 1024], F32))
        psB = es.enter_context(nc.psum_tensor("psB", [128, 1024], F32))
        psO = es.enter_context(nc.psum_tensor("psO", [128, 2048], F32))
        banks = [(psA[:, 0:512], 'b0'), (psA[:, 512:1024], 'b1'), (psB[:, 0:512], 'b2'), (psB[:, 512:1024], 'b3'),
                 (psO[:, 0:512], 'b4'), (psO[:, 512:1024], 'b5'), (psO[:, 1024:1536], 'b6'), (psO[:, 1536:2048], 'b7')]
        identf = sb("identf", [128, 128]); onesb = sb("onesb", [128, 128], BF16)
        gcol = sb("gcol", [128, 96]); g05 = sb("g05", [128, 96])
        epsc = sb("epsc", [128, 1]); zeroc = sb("zeroc", [128, 1]); segm = sb("segm", [128, 1])
        mbT = sb("mbT", [16, 16])
        lamb = sb("lamb", [128, 256]); lamt = sb("lamt", [128, 4]); neglam = sb("neglam", [128, 1])
        subg = sb("subg", [128, 128])
        NWB = 2

        def lin_alloc(tag, stack):
            a = lambda name, shape, dt=F32: stack.enter_context(nc.sbuf_tensor("s_%s_%s" % (tag, name), list(shape), dt))
            return (a("hT", [128, 8, 512]), a("yT", [128, 8, 512]), a("xn", [128, 8, 512], BF16), a("aT", [128, NJ, 512], BF16),
                    a("rstd", [128, 512]), a("tmpf", [128, 512]), [a("sg0", [128, 512]), a("sg1", [128, 512])],
                    [a("wbuf%d" % i, [128, 8192], BF16) for i in range(NWB)])
        hT = yT = xn = aT = rstd = tmpf = sg = wbuf = None
        wctr = [0]

        def wload(src_ap, n_per_part):
            i = wctr[0] % NWB; wctr[0] += 1
            dst = wbuf[i][:, 0:n_per_part]
            if len(src_ap.shape) == 3:
                dstv = dst.rearrange("p (a b) -> p a b", a=src_ap.shape[1])
            else:
                dstv = dst
            P.op('sp', lambda h: h.dma_start(out=dstv, in_=src_ap), writes=['wbuf%d' % i], dma=True)
            return dst, 'wbuf%d' % i
        bctr = [0]

        def nbank(lo=0, hi=4):
            i = lo + bctr[0] % (hi - lo); bctr[0] += 1
            return banks[i]

        P.op('sp', lambda h: h.dma_start(out=identf[:], in_=ident_d), writes=['identf'], dma=True)
        P.op('sp', lambda h: h.dma_start(out=gcol[:], in_=gcol_d), writes=['gcol'], dma=True)
        P.op('sp', lambda h: h.dma_start(out=segm[:], in_=segm_d), writes=['segm'], dma=True)
        P.op('sp', lambda h: h.dma_start(out=mbT[:], in_=mbT_d), writes=['mbT'], dma=True)
        P.op('sp', lambda h: h.dma_start(out=lamb[:], in_=lam_d.partition_broadcast(128)), writes=['lamb'], dma=True)
        P.op('sp', lambda h: h.dma_start(out=subg[:], in_=subg_d.partition_broadcast(128)), writes=['subg'], dma=True)
        P.op('dve', lambda h: h.memset(onesb[:], 1.0), writes=['onesb'])
        P.op('dve', lambda h: h.memset(epsc[:], EPS), writes=['epsc'])
        P.op('dve', lambda h: h.memset(zeroc[:], 0.0), writes=['zeroc'])
        P.op('dve', lambda h: h.tensor_scalar(out=g05[:], in0=gcol[:], scalar1=0.5, scalar2=None, op0=ALU.mult),
             reads=['gcol'], writes=['g05'])
        P.op('dve', lambda h: h.tensor_tensor(out=lamb[:, 0:64], in0=lamb[:, 0:64], in1=lamb[:, 64:128], op=ALU.mult),
             reads=['lamb'], writes=['lamb'])
        P.op('dve', lambda h: h.tensor_tensor(out=lamb[:, 128:192], in0=lamb[:, 128:192], in1=lamb[:, 192:256], op=ALU.mult),
             reads=['lamb'], writes=['lamb'])
        P.op('dve', lambda h: h.tensor_reduce(out=lamt[:, 0:1], in_=lamb[:, 0:64], axis=mybir.AxisListType.X, op=ALU.add),
             reads=['lamb'], writes=['lamt'])
        P.op('dve', lambda h: h.tensor_reduce(out=lamt[:, 1:2], in_=lamb[:, 128:192], axis=mybir.AxisListType.X, op=ALU.add),
             reads=['lamb'], writes=['lamt'])
        P.op('act', lambda h: h.activation(out=lamt[:, 2:4], in_=lamt[:, 0:2], func=AF.Exp), reads=['lamt'], writes=['lamt'])
        P.op('dve', lambda h: h.tensor_tensor(out=neglam[:], in0=lamt[:, 3:4], in1=lamt[:, 2:3], op=ALU.subtract),
             reads=['lamt'], writes=['neglam'])
        P.op('dve', lambda h: h.tensor_scalar(out=neglam[:], in0=neglam[:], scalar1=-float(lambda_init), scalar2=None, op0=ALU.add),
             reads=['neglam'], writes=['neglam'])
        P.op('dve', lambda h: h.tensor_scalar(out=subg[:], in0=subg[:], scalar1=float(1.0 - lambda_init), scalar2=None, op0=ALU.mult),
             reads=['subg'], writes=['subg'])

        later = []

        def cast(dst, src, name, defer=False):
            if defer:
                later.append((dst, src, name))
            else:
                P.op('pool', lambda h: h.dma_start(out=dst, in_=src), writes=[name], dma=True)
        for f4 in range(4):
            for gu in range(2):
                for kc in range(8):
                    cast(WinS[f4, :, :, gu, kc, :].rearrange("j p f -> p j f"),
                         w_in[f4, kc * 128:(kc + 1) * 128, gu * DFF:(gu + 1) * DFF].rearrange("p (j f) -> p j f", f=128), 'WinS', f4 > 0)
            for dc in range(8):
                cast(WoutS[f4, dc].rearrange("p fc d -> p fc d"),
                     w_out[f4, :, dc * 128:(dc + 1) * 128].rearrange("(fc p) d -> p fc d", p=128), 'WoutS', f4 > 0)
        for kc in range(8):
            rows = slice(kc * 128, (kc + 1) * 128)
            cast(WqkA[:, :, kc, :].rearrange("h p e -> p h e"), wqkv_a[rows, 0:2048].rearrange("p (h e) -> p h e", e=128), 'WqkA')
            cast(WvA[:, kc, :], wqkv_a[rows, 2048:3072], 'WvA')
            cast(WqkB[:, kc, :], wqkv_b[rows, 0:2048], 'WqkB', True)
            cast(WvB[:, kc, :], wqkv_b[rows, 2048:3072], 'WvB', True)
            cast(WoA[:, :, kc, :].rearrange("dc p d -> p dc d"), wo_a[rows, :].rearrange("p (dc d) -> p dc d", d=128), 'WoA', True)
            cast(WoB[:, :, kc, :].rearrange("dc p d -> p dc d"), wo_b[rows, :].rearrange("p (dc d) -> p dc d", d=128), 'WoB', True)
        P.flush()

        XN = [('xn', c) for c in range(8)]; HT = [('hT', c) for c in range(8)]; YT = [('yT', c) for c in range(8)]

        def rstd_from_sumsq(W):
            psn = banks[7][0][:, 0:W]
            P.op('act', lambda h: h.activation(out=rstd[:, 0:W], in_=psn, func=AF.Ln, bias=epsc[:], scale=1.0 / 1024.0),
                 reads=['b7', 'epsc'], writes=['rstd'])
            P.op('act', lambda h: h.activation(out=rstd[:, 0:W], in_=rstd[:, 0:W], func=AF.Exp, scale=-0.5),
                 reads=['rstd'], writes=['rstd'])

        def sq_chunk(src_ap, src_cells, dst_ap, dst_cells, W):
            P.op('act', lambda h: h.activation(out=dst_ap, in_=src_ap, func=AF.Square), reads=src_cells, writes=dst_cells)

        def ones_mm(sq_ap, sq_cells, c, W):
            psn = banks[7][0][:, 0:W]
            P.op('pe', lambda h: h.matmul(psn, lhsT=onesb[:], rhs=sq_ap, start=(c == 0), stop=(c == 7)),
                 reads=sq_cells + ['onesb'], writes=['b7'])

        def norm_to_xn(gi, W, presq=False):
            if not presq:
                for c in range(8):
                    sq_chunk(hT[:, c, 0:W], [('hT', c)], xn[:, c, 0:W], [('xn', c)], W)
                    ones_mm(xn[:, c, 0:W], [('xn', c)], c, W)
            rstd_from_sumsq(W)
            for c in range(8):
                P.op('dve', lambda h, c=c: h.scalar_tensor_tensor(out=xn[:, c, 0:W], in0=hT[:, c, 0:W],
                                                                  scalar=gcol[:, gi * 8 + c:gi * 8 + c + 1], in1=rstd[:, 0:W],
                                                                  op0=ALU.mult, op1=ALU.mult),
                     reads=[('hT', c), 'rstd', 'gcol'], writes=[('xn', c)])

        def postnorm_add(gi, W, half, next_norm):
            gt = g05 if half else gcol
            rstd_from_sumsq(W)
            for c in range(8):
                P.op('dve', lambda h, c=c: h.scalar_tensor_tensor(out=tmpf[:, 0:W], in0=yT[:, c, 0:W],
                                                                  scalar=gt[:, gi * 8 + c:gi * 8 + c + 1], in1=rstd[:, 0:W],
                                                                  op0=ALU.mult, op1=ALU.mult),
                     reads=[('yT', c), 'rstd', 'g05', 'gcol'], writes=['tmpf'])
                P.op('dve', lambda h, c=c: h.tensor_tensor(out=hT[:, c, 0:W], in0=hT[:, c, 0:W], in1=tmpf[:, 0:W], op=ALU.add),
                     reads=[('hT', c), 'tmpf'], writes=[('hT', c)])
                if next_norm:
                    sq_chunk(hT[:, c, 0:W], [('hT', c)], xn[:, c, 0:W], [('xn', c)], W)
                    ones_mm(xn[:, c, 0:W], [('xn', c)], c, W)

        def ffn(f4, W, presq=False, next_norm=True):
            l, i = f4 // 2, f4 % 2
            norm_to_xn(l * 6 + (0 if i == 0 else 4), W, presq)
            JG = 4
            for j0 in range(0, NJ, JG):
                nj = min(JG, NJ - j0)
                wt, wn = wload(WinS[f4, j0:j0 + nj].rearrange("j p g k f -> p j (g k f)"), nj * 2048)
                wv = wt.rearrange("p (j g k f) -> p j g k f", j=nj, g=2, k=8)
                for jj in range(nj):
                    j = j0 + jj
                    (pg, pgn), (pu, pun) = banks[(2 * j) % 4], banks[(2 * j + 1) % 4]
                    for g_, (pp, ppn) in enumerate(((pg, pgn), (pu, pun))):
                        for kc in range(8):
                            P.op('pe', lambda h, pp=pp, g_=g_, kc=kc, jj=jj, wv=wv: h.matmul(
                                pp[:, 0:W], lhsT=wv[:, jj, g_, kc, :], rhs=xn[:, kc, 0:W], start=(kc == 0), stop=(kc == 7)),
                                 reads=[('xn', kc), wn], writes=[ppn])
                    sgt = sg[j % 2]; sgn = 'sg%d' % (j % 2)
                    P.op('act', lambda h, pg=pg, sgt=sgt: h.activation(out=sgt[:, 0:W], in_=pg[:, 0:W], func=AF.Silu),
                         reads=[pgn], writes=[sgn])
                    P.op('dve', lambda h, pu=pu, sgt=sgt, j=j: h.tensor_tensor(out=aT[:, j, 0:W], in0=sgt[:, 0:W], in1=pu[:, 0:W], op=ALU.mult),
                         reads=[pun, sgn], writes=['aT'])
            pend = None
            for d0 in range(0, 8, 2):
                wt, wn = wload(WoutS[f4, d0:d0 + 2].rearrange("dc p fc d -> p dc (fc d)"), 2 * NJ * 128)
                wv = wt.rearrange("p (dc fc d) -> p dc fc d", dc=2, fc=NJ)
                for dd in range(2):
                    dc = d0 + dd
                    pp, ppn = banks[4 + dc % 2]
                    for fc in range(NJ):
                        P.op('pe', lambda h, pp=pp, dd=dd, fc=fc, wv=wv: h.matmul(
                            pp[:, 0:W], lhsT=wv[:, dd, fc, :], rhs=aT[:, fc, 0:W], start=(fc == 0), stop=(fc == NJ - 1)),
                             reads=['aT', wn], writes=[ppn])
                    if pend is not None:
                        ones_mm(xn[:, pend, 0:W], [('xn', pend)], pend, W)
                    P.op('act', lambda h, pp=pp, dc=dc: h.activation(out=yT[:, dc, 0:W], in_=pp[:, 0:W], func=AF.Copy),
                         reads=[ppn], writes=[('yT', dc)])
                    sq_chunk(pp[:, 0:W], [ppn], xn[:, dc, 0:W], [('xn', dc)], W)
                    pend = dc
            ones_mm(xn[:, pend, 0:W], [('xn', pend)], pend, W)
            postnorm_add(l * 6 + (1 if i == 0 else 5), W, True, next_norm)

        def proj_fm(Wscr, W):
            pend = None
            for d0 in range(0, 8, 4):
                wt, wn = wload(Wscr[d0:d0 + 4].rearrange("dc p k d -> p dc (k d)"), 4 * 1024)
                wv = wt.rearrange("p (dc k d) -> p dc k d", dc=4, k=8)
                for dd in range(4):
                    dc = d0 + dd
                    pp, ppn = banks[4 + dc % 2]
                    for kc in range(8):
                        P.op('pe', lambda h, pp=pp, dd=dd, kc=kc, wv=wv: h.matmul(
                            pp[:, 0:W], lhsT=wv[:, dd, kc, :], rhs=xn[:, kc, 0:W], start=(kc == 0), stop=(kc == 7)),
                             reads=[('xn', kc), wn], writes=[ppn])
                    if pend is not None:
                        ones_mm(aT[:, pend, 0:W], ['aT'], pend, W)
                    P.op('act', lambda h, pp=pp, dc=dc: h.activation(out=yT[:, dc, 0:W], in_=pp[:, 0:W], func=AF.Copy),
                         reads=[ppn], writes=[('yT', dc)])
                    sq_chunk(pp[:, 0:W], [ppn], aT[:, dc, 0:W], ['aT'], W)
                    pend = dc
            ones_mm(aT[:, pend, 0:W], ['aT'], pend, W)

        tiles = [(t0, 512) for t0 in range(0, NT, 512)] + [(MOFF, 16)]

        with ExitStack() as es1:
            hT, yT, xn, aT, rstd, tmpf, sg, wbuf = lin_alloc('p1', es1)
            sb1 = lambda name, shape, dt=F32: es1.enter_context(nc.sbuf_tensor("s_" + name, list(shape), dt))
            xt = [sb1("xt0", [128, 1024]), sb1("xt1", [128, 1024])]
            qk_sb = sb1("qk_sb", [128, 16, 512], BF16)
            v_sb = [sb1("v_sb0", [128, 1024], BF16), sb1("v_sb1", [128, 1024], BF16)]
            xcl = [0]

            def p1_tile(t0, W):
                ismeta = (W == 16)
                nsub = 1 if ismeta else 4
                for s_ in range(nsub):
                    npt = 16 if ismeta else 128
                    xc = xcl[0]; xcl[0] += 1
                    xtt = xt[xc % 2]; xtn = 'xt%d' % (xc % 2)
                    src = meta if ismeta else xs[t0 + s_ * 128: t0 + (s_ + 1) * 128, :]
                    P.op('sp', lambda h, xtt=xtt, src=src, npt=npt: h.dma_start(out=xtt[0:npt, :], in_=src), writes=[xtn], dma=True)
                    for c0 in (0, 4):
                        pp, ppn = nbank(0, 4)
                        for cc in range(4):
                            c = c0 + cc
                            P.op('pe', lambda h, pp=pp, cc=cc, c=c, xtt=xtt, npt=npt: h.transpose(
                                pp[:, cc * 128: cc * 128 + npt], xtt[0:npt, c * 128:(c + 1) * 128], identf[0:npt, 0:npt]),
                                 reads=[xtn, 'identf'], writes=[ppn])
                        P.op('dve', lambda h, pp=pp, c0=c0, s_=s_, npt=npt: h.tensor_copy(
                            out=hT[:, c0:c0 + 4, s_ * 128: s_ * 128 + npt],
                            in_=pp.rearrange("p (c t) -> p c t", c=4)[:, :, 0:npt]), reads=[ppn], writes=[*HT])
                ffn(0, W, False, True)
                P.op('pool', lambda h, t0=t0, W=W: h.dma_start(out=hS[:, :, t0:t0 + W], in_=hT[:, :, 0:W]),
                     reads=[*HT], writes=[('hS', t0)], dma=True)
                norm_to_xn(2, W, True)
                for h0 in range(0, 16, 8):
                    wt, wn = wload(WqkA[h0:h0 + 8].rearrange("h p k e -> p h (k e)"), 8 * 1024)
                    wv = wt.rearrange("p (h k e) -> p h k e", h=8, k=8)
                    for hh in range(8):
                        hd = h0 + hh
                        pp, ppn = nbank(0, 4)
                        for kc in range(8):
                            P.op('pe', lambda h, pp=pp, hh=hh, kc=kc, wv=wv: h.matmul(
                                pp[:, 0:W], lhsT=wv[:, hh, kc, :], rhs=xn[:, kc, 0:W], start=(kc == 0), stop=(kc == 7)),
                                 reads=[*XN, wn], writes=[ppn])
                        P.op('act', lambda h, pp=pp, hd=hd: h.activation(out=qk_sb[:, hd, 0:W], in_=pp[:, 0:W], func=AF.Copy,
                                                                          scale=(0.125 if hd < 8 else 1.0)),
                             reads=[ppn], writes=['qk_sb'])
                P.op('pool', lambda h, t0=t0, W=W: h.dma_start(out=QTa[:, :, t0:t0 + W].rearrange("(i two) e t -> (two e) i t", two=2), in_=qk_sb[:, 0:8, 0:W]),
                     reads=['qk_sb'], writes=[('QTa', t0)], dma=True)
                P.op('pool', lambda h, t0=t0, W=W: h.dma_start(out=KTa[:, :, t0:t0 + W].rearrange("(i two) e t -> (two e) i t", two=2), in_=qk_sb[:, 8:16, 0:W]),
                     reads=['qk_sb'], writes=[('KTa', t0)], dma=True)
                wt, wn = wload(WvA, 8192)
                wv = wt.rearrange("p (k n) -> p k n", k=8)
                for s_ in range(nsub):
                    npt = 16 if ismeta else 128
                    vt = v_sb[s_ % 2]; vn = 'v_sb%d' % (s_ % 2)
                    for half in range(2):
                        pp, ppn = nbank(0, 4)
                        for kc in range(8):
                            P.op('pe', lambda h, pp=pp, kc=kc, s_=s_, half=half, npt=npt, wv=wv: h.matmul(
                                pp[0:npt, :], lhsT=xn[:, kc, s_ * 128: s_ * 128 + npt], rhs=wv[:, kc, half * 512:(half + 1) * 512],
                                start=(kc == 0), stop=(kc == 7)), reads=[*XN, wn], writes=[ppn])
                        P.op('dve', lambda h, pp=pp, vt=vt, half=half, npt=npt: h.tensor_copy(out=vt[0:npt, half * 512:(half + 1) * 512], in_=pp[0:npt, :]),
                             reads=[ppn], writes=[vn])
                    if ismeta:
                        P.op('pool', lambda h, vt=vt: h.dma_start(out=Vma.rearrange("h m e -> m h e"),
                                                                  in_=vt[0:16, :].rearrange("m (h e) -> m h e", e=64)),
                             reads=[vn], writes=['Vma'], dma=True)
                    else:
                        ch = t0 // 128 + s_
                        P.op('pool', lambda h, vt=vt, ch=ch: h.dma_start(out=Va[:, :, ch, :].rearrange("h p e -> p h e"),
                                                                         in_=vt[:, :].rearrange("p (h e) -> p h e", e=64)),
                             reads=[vn], writes=[('Va', ch)], dma=True)
            per = (len(later) + len(tiles) - 1) // len(tiles)
            for (t0_, W_) in tiles:
                p1_tile(t0_, W_)
                for _ in range(per):
                    if later:
                        cast(*later.pop(0))
            while later:
                cast(*later.pop(0))
            P.flush()
        if stop <= 1:
            return nc

        with ExitStack() as es2:
            sb2 = lambda name, shape, dt=F32: es2.enter_context(nc.sbuf_tensor("s_" + name, list(shape), dt))
            KT = sb2("KT", [64, NTOK], BF16); QT = sb2("QT", [64, NTOK], BF16); OT = sb2("OT", [64, NTOK], BF16)
            Vh = sb2("Vh", [128, NCH, 64], BF16); Vmh = sb2("Vmh", [16, 64], BF16)
            RTh = sb2("RTh", [128, 7, 128]); RTV = sb2("RTV", [128, NSLOT, 128]); vmask = sb2("vmask", [128, NSLOT, 128])
            ssb = [sb2("ssb%d" % i, [128, 6, 128]) for i in range(3)]
            pt = [sb2("pt%d" % i, [128, 6, 128], BF16) for i in range(3)]
            pm = [sb2("pm%d" % i, [16, 128], BF16) for i in range(3)]
            rl = [sb2("rl0", [64, 128]), sb2("rl1", [64, 128])]
            P.op('sp', lambda h: h.dma_start(out=vmask[:], in_=vmask_d.rearrange("s k q -> k s q")), writes=['vmask'], dma=True)
            psS = [(psA, 'psA'), (psB, 'psB')]
            it = 0
            for hd in range(16):
                P.op('sp', lambda h, hd=hd: h.dma_start(out=RTh[:], in_=rt_d[hd].rearrange("d k q -> k d q")), writes=['RTh'], dma=True)
                P.op('sp', lambda h, hd=hd: h.dma_start(out=KT[:], in_=KTa[hd]), writes=['KT'], dma=True)
                P.op('sp', lambda h, hd=hd: h.dma_start(out=QT[:], in_=QTa[hd]), writes=['QT'], dma=True)
                P.op('sp', lambda h, hd=hd: h.dma_start(out=Vh[:], in_=Va[hd]), writes=['Vh'], dma=True)
                P.op('sp', lambda h, hd=hd: h.dma_start(out=Vmh[:], in_=Vma[hd]), writes=['Vmh'], dma=True)
                for name, dl in cls:
                    s0, _ = cstart[name]
                    for i_, dl_ in enumerate(dl):
                        P.op('pool', lambda h, s0=s0, i_=i_, dl_=dl_: h.tensor_tensor(out=RTV[:, s0 + i_, :], in0=vmask[:, s0 + i_, :],
                                                                                      in1=RTh[:, dl_ + 3, :], op=ALU.add),
                             reads=['vmask', 'RTh'], writes=['RTV'])
                def na_s1(gt, b_, b3, hd=hd):
                    ismeta = (gt == 3 * T)
                    pS, pSn = psS[b_]
                    pX, pXn = banks[4 + b_]
                    nq = 16 if ismeta else 128
                    q0 = MOFF if ismeta else gt * 128
                    if not ismeta:
                        seg, t = gt // T, gt % T
                        s0, dl = na_lookup(seg, t)
                        nb = len(dl)
                        for i_, dl_ in enumerate(dl):
                            gc = gt + dl_
                            P.op('pe', lambda h, pS=pS, i_=i_, gc=gc, q0=q0: h.matmul(
                                pS[:, i_ * 128:(i_ + 1) * 128], lhsT=KT[:, gc * 128:(gc + 1) * 128], rhs=QT[:, q0:q0 + 128], start=True, stop=True),
                                 reads=['KT', 'QT'], writes=[pSn])
                    P.op('pe', lambda h, pS=pS, q0=q0, nq=nq: h.matmul(pS[0:16, 768:768 + nq], lhsT=KT[:, MOFF:MOFF + 16], rhs=QT[:, q0:q0 + nq], start=True, stop=True),
                         reads=['KT', 'QT'], writes=[pSn])
                    P.op('act', lambda h, pS=pS, b3=b3, nq=nq, hd=hd: h.activation(out=pm[b3][:, 0:nq], in_=pS[0:16, 768:768 + nq], func=AF.Exp, bias=mbT[:, hd:hd + 1]),
                         reads=[pSn, 'mbT'], writes=['pm%d' % b3])
                    if not ismeta:
                        P.op('dve', lambda h, pS=pS, b3=b3, nb=nb, s0=s0: h.tensor_tensor(
                            out=ssb[b3][:, 0:nb, :], in0=pS[:, 0:nb * 128].rearrange("p (n q) -> p n q", q=128),
                            in1=RTV[:, s0:s0 + nb, :], op=ALU.add), reads=[pSn, 'RTV', 'pm%d' % b3], writes=['ssb%d' % b3])
                        P.op('act', lambda h, b3=b3, nb=nb: h.activation(out=pt[b3][:, 0:nb, :], in_=ssb[b3][:, 0:nb, :], func=AF.Exp),
                             reads=['ssb%d' % b3], writes=['pt%d' % b3])

                def na_s2(gt, b_, b3, hd=hd):
                    ismeta = (gt == 3 * T)
                    pX, pXn = banks[4 + b_]
                    nq = 16 if ismeta else 128
                    q0 = MOFF if ismeta else gt * 128
                    if not ismeta:
                        seg, t = gt // T, gt % T
                        s0, dl = na_lookup(seg, t)
                        for i_, dl_ in enumerate(dl):
                            gc = gt + dl_
                            P.op('pe', lambda h, pX=pX, i_=i_, gc=gc, b3=b3: h.matmul(
                                pX[0:64, 128:256], lhsT=Vh[:, gc, :], rhs=pt[b3][:, i_, :], start=(i_ == 0), stop=False),
                                 reads=['Vh', 'pt%d' % b3], writes=[pXn])
                    P.op('pe', lambda h, pX=pX, b3=b3, nq=nq, ismeta=ismeta: h.matmul(pX[0:64, 128:128 + nq], lhsT=Vmh[:, :], rhs=pm[b3][:, 0:nq], start=ismeta, stop=True),
                         reads=['Vmh', 'pm%d' % b3], writes=[pXn])
                    if not ismeta:
                        for i_, dl_ in enumerate(dl):
                            P.op('pe', lambda h, pX=pX, i_=i_, b3=b3: h.matmul(
                                pX[0:64, 256:384], lhsT=onesb[:, 0:64], rhs=pt[b3][:, i_, :], start=(i_ == 0), stop=False),
                                 reads=['onesb', 'pt%d' % b3], writes=[pXn])
                    P.op('pe', lambda h, pX=pX, b3=b3, nq=nq, ismeta=ismeta: h.matmul(pX[0:64, 256:256 + nq], lhsT=onesb[0:16, 0:64], rhs=pm[b3][:, 0:nq], start=ismeta, stop=True),
                         reads=['onesb', 'pm%d' % b3], writes=[pXn])
                    P.op('dve', lambda h, pX=pX, b_=b_, nq=nq: h.reciprocal(out=rl[b_][:, 0:nq], in_=pX[0:64, 256:256 + nq]),
                         reads=[pXn], writes=['rl%d' % b_])
                    P.op('dve', lambda h, pX=pX, b_=b_, nq=nq, q0=q0: h.tensor_tensor(out=OT[:, q0:q0 + nq], in0=pX[0:64, 128:128 + nq], in1=rl[b_][:, 0:nq], op=ALU.mult),
                         reads=[pXn, 'rl%d' % b_], writes=['OT'])

                pend = []
                for gt in range(3 * T + 1):
                    b_ = it % 2; b3 = it % 3; it += 1
                    na_s1(gt, b_, b3)
                    pend.append((gt, b_, b3))
                    if len(pend) > 2:
                        na_s2(*pend.pop(0))
                while pend:
                    na_s2(*pend.pop(0))
                P.op('pool', lambda h, hd=hd: h.dma_start(out=OTa[hd // 2, (hd % 2) * 64:(hd % 2) * 64 + 64, :], in_=OT[:]),
                     reads=['OT'], writes=[('OTa', hd)], dma=True)
            P.flush()
        if stop <= 2:
            return nc

        with ExitStack() as es3:
            hT, yT, xn, aT, rstd, tmpf, sg, wbuf = lin_alloc('p3', es3)
            sb3 = lambda name, shape, dt=F32: es3.enter_context(nc.sbuf_tensor("s_" + name, list(shape), dt))
            qk_tm = [sb3("qk_tm0", [128, 1024]), sb3("qk_tm1", [128, 1024])]
            rtmp = [sb3("rtmp%d" % i, [128, 32, 8]) for i in range(4)]
            cst = sb3("cst", [128, 4, 16])
            qkT = sb3("qkT", [64, 32, 512], BF16)
            v_sb = [sb3("v3_sb0", [128, 1024], BF16), sb3("v3_sb1", [128, 1024], BF16)]

            def p3_tile(t0, W):
                ismeta = (W == 16)
                nsub = 1 if ismeta else 4
                npt = 16 if ismeta else 128
                P.op('sp', lambda h, t0=t0, W=W: h.dma_start(out=xn[:, :, 0:W], in_=OTa[:, :, t0:t0 + W].rearrange("c p t -> p c t")),
                     writes=[*XN], dma=True)
                P.op('sp', lambda h, t0=t0, W=W: h.dma_start(out=hT[:, :, 0:W], in_=hS[:, :, t0:t0 + W]), writes=[*HT], dma=True)
                proj_fm(WoA, W)
                postnorm_add(3, W, False, True)
                ffn(1, W, True, True)
                ffn(2, W, True, True)
                P.op('pool', lambda h, t0=t0, W=W: h.dma_start(out=hS[:, :, t0:t0 + W], in_=hT[:, :, 0:W]),
                     reads=[*HT], writes=[('hS', t0)], dma=True)
                norm_to_xn(6 + 2, W, True)
                if ismeta:
                    P.op('sp', lambda h: h.dma_start(out=cst[0:16, 0, :], in_=cs_d[MOFF:MOFF + 16, :]), writes=['cst'], dma=True)
                else:
                    P.op('sp', lambda h, t0=t0: h.dma_start(out=cst[:, :, :], in_=cs_d[t0:t0 + 512, :].rearrange("(s p) c -> p s c", p=128)),
                         writes=['cst'], dma=True)
                for cbp in range(2):
                    wt, wn = wload(WqkB[:, :, cbp * 1024:(cbp + 1) * 1024], 8192)
                    wv = wt.rearrange("p (k n) -> p k n", k=8)
                    def stA(s_, cbp=cbp, wv=wv, wn=wn):
                        qt = qk_tm[s_ % 2]; qn = 'qk_tm%d' % (s_ % 2)
                        for cb in range(2):
                            pp, ppn = nbank(0, 4)
                            for kc in range(8):
                                P.op('pe', lambda h, pp=pp, kc=kc, s_=s_, cb=cb, wv=wv: h.matmul(
                                    pp[0:npt, :], lhsT=xn[:, kc, s_ * 128: s_ * 128 + npt], rhs=wv[:, kc, cb * 512:(cb + 1) * 512],
                                    start=(kc == 0), stop=(kc == 7)), reads=[*XN, wn], writes=[ppn])
                            P.op('act', lambda h, pp=pp, cb=cb, qt=qt: h.activation(out=qt[0:npt, cb * 512:(cb + 1) * 512], in_=pp[0:npt, :], func=AF.Copy),
                                 reads=[ppn], writes=[qn])
                        qv = qt[0:npt, :].rearrange("p (m e) -> p m e", e=64)
                        x1 = qv[:, :, 0:8]; x2 = qv[:, :, 8:16]
                        cosb = cst[0:npt, s_, 0:8].unsqueeze(1).to_broadcast([npt, 16, 8])
                        sinb = cst[0:npt, s_, 8:16].unsqueeze(1).to_broadcast([npt, 16, 8])
                        for k_, (a_, b2) in enumerate(((x1, cosb), (x2, sinb), (x2, cosb), (x1, sinb))):
                            P.op('dve', lambda h, k_=k_, a_=a_, b2=b2: h.tensor_tensor(out=rtmp[k_][0:npt, 0:16, :], in0=a_, in1=b2, op=ALU.mult),
                                 reads=[qn, 'cst'], writes=['rtmp%d' % k_])
                        P.op('dve', lambda h, x1=x1: h.tensor_tensor(out=x1, in0=rtmp[0][0:npt, 0:16, :], in1=rtmp[1][0:npt, 0:16, :], op=ALU.subtract),
                             reads=['rtmp0', 'rtmp1'], writes=[qn])
                        P.op('dve', lambda h, x2=x2: h.tensor_tensor(out=x2, in0=rtmp[2][0:npt, 0:16, :], in1=rtmp[3][0:npt, 0:16, :], op=ALU.add),
                             reads=['rtmp2', 'rtmp3'], writes=[qn])

                    def stB(s_, cbp=cbp):
                        qt = qk_tm[s_ % 2]; qn = 'qk_tm%d' % (s_ % 2)
                        for i0 in range(0, 16, 4):
                            pp, ppn = nbank(0, 4)
                            for ii in range(4):
                                i_ = i0 + ii
                                P.op('pe', lambda h, pp=pp, ii=ii, i_=i_, qt=qt: h.transpose(
                                    pp[0:64, ii * 128: ii * 128 + npt], qt[0:npt, i_ * 64:(i_ + 1) * 64], identf[0:npt, 0:npt]),
                                     reads=[qn, 'identf'], writes=[ppn])
                            P.op('act', lambda h, pp=pp, i0=i0, s_=s_, cbp=cbp: h.activation(
                                out=qkT[:, cbp * 16 + i0:cbp * 16 + i0 + 4, s_ * 128: s_ * 128 + npt],
                                in_=pp[0:64, :].rearrange("p (i t) -> p i t", i=4)[:, :, 0:npt],
                                func=AF.Copy, scale=(0.125 if cbp == 0 else 1.0)), reads=[ppn], writes=['qkT'])

                    stA(0)
                    for s_ in range(nsub):
                        if s_ + 1 < nsub:
                            stA(s_ + 1)
                        stB(s_)
                wt, wn = wload(WvB, 8192)
                wv = wt.rearrange("p (k n) -> p k n", k=8)
                for s_ in range(nsub):
                    vt = v_sb[s_ % 2]; vn = 'v3_sb%d' % (s_ % 2)
                    for half in range(2):
                        pp, ppn = nbank(0, 4)
                        for kc in range(8):
                            P.op('pe', lambda h, pp=pp, kc=kc, s_=s_, half=half, wv=wv: h.matmul(
                                pp[0:npt, :], lhsT=xn[:, kc, s_ * 128: s_ * 128 + npt], rhs=wv[:, kc, half * 512:(half + 1) * 512],
                                start=(kc == 0), stop=(kc == 7)), reads=[*XN, wn], writes=[ppn])
                        P.op('dve', lambda h, pp=pp, vt=vt, half=half: h.tensor_copy(out=vt[0:npt, half * 512:(half + 1) * 512], in_=pp[0:npt, :]),
                             reads=[ppn], writes=[vn])
                    if ismeta:
                        P.op('pool', lambda h, vt=vt: h.dma_start(out=Vmb.rearrange("h m e -> m h e"),
                                                                  in_=vt[0:16, :].rearrange("m (h e) -> m h e", e=128)),
                             reads=[vn], writes=['Vmb'], dma=True)
                    else:
                        ch = t0 // 128 + s_
                        P.op('pool', lambda h, vt=vt, ch=ch: h.dma_start(out=Vb[:, :, ch, :].rearrange("h p e -> p h e"),
                                                                         in_=vt[:, :].rearrange("p (h e) -> p h e", e=128)),
                             reads=[vn], writes=[('Vb', ch)], dma=True)
                P.op('pool', lambda h, t0=t0, W=W: h.dma_start(out=QTb[:, :, :, t0:t0 + W].rearrange("h m e t -> e (h m) t"), in_=qkT[:, 0:16, 0:W]),
                     reads=['qkT'], writes=[('QTb', t0)], dma=True)
                P.op('pool', lambda h, t0=t0, W=W: h.dma_start(out=KTb[:, :, :, t0:t0 + W].rearrange("h m e t -> e (h m) t"), in_=qkT[:, 16:32, 0:W]),
                     reads=['qkT'], writes=[('KTb', t0)], dma=True)
            for (t0_, W_) in tiles:
                p3_tile(t0_, W_)
            P.flush()
        if stop <= 3:
            return nc

        with ExitStack() as es4:
            hT, yT, xn, aT, rstd, tmpf, sg, wbuf = lin_alloc('p4', es4)
            sb4 = lambda name, shape, dt=F32: es4.enter_context(nc.sbuf_tensor("s_" + name, list(shape), dt))
            NKC = KB // 128
            Kb = [sb4("Kb%d" % i, [128, KB], BF16) for i in range(2)]
            Vp = [sb4("Vp%d" % i, [128, NKC, 130], BF16) for i in range(2)]
            Km = sb4("Km", [128, 8, 16], BF16); Vpm = sb4("Vpm", [16, 8, 130], BF16)
            Qg = [sb4("Qg%d" % i, [128, 512], BF16) for i in range(2)]
            PT = [sb4("PT%d" % i, [128, 2, 512], BF16) for i in range(3)]
            Otm = sb4("Otm", [128, 4, 1024])
            t0s = sb4("t0s", [128, 128]); dsb = sb4("dsb", [128, 128]); dsq = sb4("dsq", [128, 128])
            rec = sb4("rec", [128, 4]); ss = sb4("ss", [128, 2])
            ot = [sb4("ot0", [128, 1024]), sb4("ot1", [128, 1024])]
            for i in range(2):
                P.op('pool', lambda h, i=i: h.memset(Vp[i][:], 1.0), writes=['Vp%d' % i])
            P.op('pool', lambda h: h.memset(Vpm[:], 1.0), writes=['Vpm'])
            P.op('sp', lambda h: h.dma_start(out=Km[:], in_=KTb[:, :, :, MOFF:MOFF + 16].rearrange("h m e t -> (m e) h t")), writes=['Km'], dma=True)
            P.op('sp', lambda h: h.dma_start(out=Vpm[:, :, 0:128], in_=Vmb.rearrange("h m e -> m h e")), reads=['Vpm'], writes=['Vpm'], dma=True)
            psS = [(psA, ['b0', 'b1']), (psB, ['b2', 'b3'])]
            PSO = ['b4', 'b5', 'b6', 'b7']
            kvc = 0; sc = 0
            NG = NT // 512
            GPS = SEG // 512
            for g in range(NG):
                seg = g // GPS
                q0 = g * 512
                ksegs = [0, 1] if seg < 2 else [2]
                st = {'kvc': kvc, 'sc': sc}

                def gen_units():
                    for hd in range(8):
                        qb = Qg[hd % 2]; qbn = 'Qg%d' % (hd % 2)
                        blocks = [(ks, kb0) for ks in ksegs for kb0 in range(0, SEG, KB)] + [None]
                        first = True
                        for u in blocks:
                            nch = NKC if u is not None else 1
                            for j in range(nch):
                                yield dict(hd=hd, qb=qb, qbn=qbn, u=u, j=j, first=first, last=(u is None), newhead=(first), newblk=(j == 0))
                                first = False

                def prep(d):
                    hd = d['hd']
                    if d['newhead']:
                        P.op('sp', lambda h, qb=d['qb'], hd=hd, q0=q0: h.dma_start(out=qb[:], in_=QTb[hd, :, :, q0:q0 + 512].rearrange("m e t -> (m e) t")),
                             writes=[d['qbn']], dma=True)
                    if d['u'] is not None:
                        ks, kb0 = d['u']
                        if d['newblk']:
                            i = st['kvc'] % 2; st['kvc'] += 1
                            st['i'] = i
                            k0 = ks * SEG + kb0
                            P.op('sp', lambda h, i=i, hd=hd, k0=k0: h.dma_start(out=Kb[i][:], in_=KTb[hd, :, :, k0:k0 + KB].rearrange("m e t -> (m e) t")),
                                 writes=['Kb%d' % i], dma=True)
                            P.op('sp', lambda h, i=i, hd=hd, k0=k0: h.dma_start(out=Vp[i][:, :, 0:128], in_=Vb[hd, :, k0 // 128:k0 // 128 + NKC, :]),
                                 reads=['Vp%d' % i], writes=['Vp%d' % i], dma=True)
                        i = st['i']; j = d['j']
                        d['bias'] = zeroc if ks == seg else segm
                        d['kap'] = Kb[i][:, j * 128:(j + 1) * 128]; d['vap'] = Vp[i][:, j, 0:129]; d['nk'] = 128
                        d['rd'] = ['Kb%d' % i, 'Vp%d' % i]
                    else:
                        d['bias'] = zeroc
                        d['kap'] = Km[:, hd, :]; d['vap'] = Vpm[:, hd, 0:129]; d['nk'] = 16; d['rd'] = ['Km', 'Vpm']
                    d['b'] = st['sc'] % 2; d['pb'] = st['sc'] % 3; st['sc'] += 1

                def emit_S(d):
                    b_ = d['b']; pS, pSn = psS[b_]; nk = d['nk']; kap = d['kap']; qb = d['qb']
                    for m in range(2):
                        P.op('pe', lambda h, pS=pS, m=m, kap=kap, qb=qb, nk=nk: h.matmul(
                            pS[0:nk, m * 512:(m + 1) * 512], lhsT=kap[m * 64:(m + 1) * 64, :], rhs=qb[m * 64:(m + 1) * 64, :], start=True, stop=True,
                            tile_position=(m * 64, 0)),
                             reads=[d['rd'][0], d['qbn']], writes=pSn)
                    pb = d['pb']
                    P.op('act', lambda h, pS=pS, pb=pb, nk=nk, bias=d['bias']: h.activation(
                        out=PT[pb][0:nk].rearrange("p m q -> p (m q)"), in_=pS[0:nk, :], func=AF.Exp, bias=bias[0:nk, :]),
                         reads=pSn + ['segm', 'zeroc'], writes=['PT%d' % pb])

                def emit_AV(d):
                    b_ = d['pb']; nk = d['nk']; vap = d['vap']
                    for qs in range(4):
                        for m in range(2):
                            a_ = qs * 2 + m
                            P.op('pe', lambda h, a_=a_, qs=qs, m=m, b_=b_, vap=vap, nk=nk, first=d['first'], last=d['last']: h.matmul(
                                psO[:, a_ * 256: a_ * 256 + 129], lhsT=PT[b_][0:nk, m, qs * 128:(qs + 1) * 128], rhs=vap[0:nk, :],
                                start=(first and a_ % 2 == 0), stop=last), reads=['PT%d' % b_, d['rd'][1]], writes=PSO)
                    if d['last']:
                        combine(d['hd'])

                def combine(hd):
                    for qs in range(4):
                        a0 = psO[:, (2 * qs) * 256:(2 * qs) * 256 + 129]; a1 = psO[:, (2 * qs + 1) * 256:(2 * qs + 1) * 256 + 129]
                        P.op('dve', lambda h, a0=a0: h.reciprocal(out=rec[:, 0:1], in_=a0[:, 128:129]), reads=PSO, writes=['rec'])
                        P.op('dve', lambda h, a1=a1: h.reciprocal(out=rec[:, 1:2], in_=a1[:, 128:129]), reads=PSO, writes=['rec'])
                        P.op('dve', lambda h: h.tensor_tensor(out=rec[:, 2:3], in0=rec[:, 1:2], in1=neglam[:], op=ALU.mult),
                             reads=['rec', 'neglam'], writes=['rec'])
                        P.op('dve', lambda h, a0=a0: h.tensor_scalar(out=t0s[:], in0=a0[:, 0:128], scalar1=rec[:, 0:1], scalar2=None, op0=ALU.mult),
                             reads=PSO + ['rec'], writes=['t0s'])
                        P.op('dve', lambda h, a1=a1: h.scalar_tensor_tensor(out=dsb[:], in0=a1[:, 0:128], scalar=rec[:, 2:3], in1=t0s[:],
                                                                            op0=ALU.mult, op1=ALU.add),
                             reads=PSO + ['rec', 't0s'], writes=['dsb'])
                        P.op('dve', lambda h: h.memset(ss[:, 0:1], 0.0), writes=['ss'])
                        P.op('dve', lambda h: h.scalar_tensor_tensor(out=dsq[:], in0=dsb[:], scalar=1.0, in1=dsb[:], op0=ALU.mult, op1=ALU.mult,
                                                                     accum_out=ss[:, 0:1]), reads=['dsb'], writes=['dsq', 'ss'])
                        P.op('act', lambda h: h.activation(out=ss[:, 1:2], in_=ss[:, 0:1], func=AF.Ln, bias=epsc[:], scale=1.0 / 128),
                             reads=['ss', 'epsc'], writes=['ss'])
                        P.op('act', lambda h: h.activation(out=ss[:, 1:2], in_=ss[:, 1:2], func=AF.Exp, scale=-0.5), reads=['ss'], writes=['ss'])
                        P.op('dve', lambda h, qs=qs, hd=hd: h.scalar_tensor_tensor(out=Otm[:, qs, hd * 128:(hd + 1) * 128], in0=dsb[:], scalar=ss[:, 1:2],
                                                                                 in1=subg[:], op0=ALU.mult, op1=ALU.mult),
                             reads=['dsb', 'ss', 'subg'], writes=['Otm'])

                pending = []
                for d in gen_units():
                    prep(d)
                    emit_S(d)
                    pending.append(d)
                    if len(pending) > 2:
                        emit_AV(pending.pop(0))
                while pending:
                    emit_AV(pending.pop(0))
                kvc = st['kvc']; sc = st['sc']
                if dbg:
                    for qs in range(4):
                        P.op('pool', lambda h, qs=qs, q0=q0: h.dma_start(out=dbgO[q0 + qs * 128:q0 + (qs + 1) * 128, :], in_=Otm[:, qs, :]),
                             reads=['Otm'], writes=[('dbgO', q0, qs)], dma=True)
                for qs in range(4):
                    for c0 in (0, 4):
                        pp, ppn = nbank(0, 4)
                        for cc in range(4):
                            c = c0 + cc
                            P.op('pe', lambda h, pp=pp, cc=cc, c=c, qs=qs: h.transpose(
                                pp[:, cc * 128:(cc + 1) * 128], Otm[:, qs, c * 128:(c + 1) * 128], identf[:]),
                                 reads=['Otm', 'identf'], writes=[ppn])
                        P.op('dve', lambda h, pp=pp, c0=c0, qs=qs: h.tensor_copy(
                            out=xn[:, c0:c0 + 4, qs * 128:(qs + 1) * 128], in_=pp.rearrange("p (c t) -> p c t", c=4)), reads=[ppn], writes=[*XN])
                P.op('sp', lambda h, q0=q0: h.dma_start(out=hT[:], in_=hS[:, :, q0:q0 + 512]), writes=[*HT], dma=True)
                proj_fm(WoB, 512)
                postnorm_add(6 + 3, 512, False, True)
                ffn(3, 512, True, False)
                for qs in range(4):
                    o_ = ot[qs % 2]; on_ = 'ot%d' % (qs % 2)
                    for c0 in (0, 4):
                        pp, ppn = nbank(0, 4)
                        for cc in range(4):
                            c = c0 + cc
                            P.op('pe', lambda h, pp=pp, cc=cc, c=c, qs=qs: h.transpose(
                                pp[:, cc * 128:(cc + 1) * 128], hT[:, c, qs * 128:(qs + 1) * 128], identf[:]),
                                 reads=[*HT, 'identf'], writes=[ppn])
                        P.op('act', lambda h, pp=pp, c0=c0, o_=o_: h.activation(out=o_[:, c0 * 128:(c0 + 4) * 128], in_=pp[:, :], func=AF.Copy),
                             reads=[ppn], writes=[on_])
                    P.op('pool', lambda h, o_=o_, q0=q0, qs=qs: h.dma_start(out=y[q0 + qs * 128: q0 + (qs + 1) * 128, :], in_=o_[:]),
                         reads=[on_], writes=[('y', q0, qs)], dma=True)
            P.flush()
    return nc


def host_inputs(SEG, segsA, typ, common):
    m = dict(common)
    m["xs"] = np.ascontiguousarray(np.concatenate(segsA, axis=0), dtype=np.float32)
    m["vmask"] = host_vmask(SEG, typ)
    m["cs"] = host_cs(SEG, typ)
    m["segm"] = np.full((128, 1), 0.0 if typ == 2 else NEG, np.float32)
    return m


def host_common(meta_tokens, norm_g, w_ffn_in, w_ffn_out, w_qkv_a, w_o_a, rpb_a, meta_bias_a, w_qkv_b, w_o_b, lambda_b, subln_b):
    f = lambda a: np.ascontiguousarray(np.asarray(a, dtype=np.float32))
    g = f(norm_g).reshape(12, 8, 128)
    return {
        "meta": f(meta_tokens),
        "gcol": np.ascontiguousarray(g.transpose(2, 0, 1).reshape(128, 96)),
        "w_in": f(w_ffn_in).reshape(4, D, 2 * DFF), "w_out": f(w_ffn_out).reshape(4, DFF, D),
        "wqkv_a": f(w_qkv_a)[0], "wo_a": f(w_o_a)[0], "wqkv_b": f(w_qkv_b)[0], "wo_b": f(w_o_b)[0],
        "rt": host_rt(f(rpb_a)[0]),
        "mbT": np.ascontiguousarray(f(meta_bias_a)[0].T),
        "lam": f(lambda_b)[0].reshape(1, 256), "subg": f(subln_b)[0].reshape(1, 128),
        "ident": np.eye(128, dtype=np.float32),
    }


def run(SEG, x_prompt, x_sample, params):
    lambda_init = 0.8 - 0.6 * math.exp(-0.3 * 1)
    common = host_common(**params)
    nc = build(SEG, lambda_init)
    xp = np.asarray(x_prompt, dtype=np.float32); xsamp = np.asarray(x_sample, dtype=np.float32)
    in_maps = []
    for c in range(4):
        in_maps.append(host_inputs(SEG, [xsamp[c, :SEG], xsamp[c, SEG:], xp[c]], 2, common))
    for c in range(4):
        in_maps.append(host_inputs(SEG, [xp[4 + 3 * c], xp[5 + 3 * c], xp[6 + 3 * c]], 1, common))
    res = run_bass_kernel_spmd(nc, in_maps, core_ids=list(range(8)))
    yp = np.zeros_like(xp); ysamp = np.zeros_like(xsamp)
    for c in range(4):
        yc = res.results[c]["y"]
        ysamp[c, :SEG] = yc[0:SEG]; ysamp[c, SEG:] = yc[SEG:2 * SEG]; yp[c] = yc[2 * SEG:]
    for c in range(4):
        yc = res.results[4 + c]["y"]
        for k in range(3):
            yp[4 + 3 * c + k] = yc[k * SEG:(k + 1) * SEG]
    return yp, ysamp


def kernel(x_prompt, x_sample, meta_tokens, norm_g, w_ffn_in, w_ffn_out, w_qkv_a, w_o_a,
           rpb_a, meta_bias_a, w_qkv_b, w_o_b, lambda_b, subln_b):
    params = dict(meta_tokens=meta_tokens, norm_g=norm_g, w_ffn_in=w_ffn_in, w_ffn_out=w_ffn_out,
                  w_qkv_a=w_qkv_a, w_o_a=w_o_a, rpb_a=rpb_a, meta_bias_a=meta_bias_a,
                  w_qkv_b=w_qkv_b, w_o_b=w_o_b, lambda_b=lambda_b, subln_b=subln_b)
    yp, ysamp = run(4096, x_prompt, x_sample, params)
    return (yp, ysamp)
```
